# Optimizing a Trainium2 kernel written in Bass

```python
import jax
import jax.numpy as jnp
from jax import lax
import numpy as np

D_MODEL = 2048
BATCH = 4
SEQ = 2048
DEPTH = 2
DEC_BATCH = 128
DEC_SEQ = 4
PAST_LEN = 16384
PAGE_SIZE = 128

N_GROUPS = 4
GROUP_W = D_MODEL // N_GROUPS
RWKV_HEAD = 64
RWKV_HEADS = GROUP_W // RWKV_HEAD
W_LORA = 32
A_LORA = 32
G_LORA = 96
RWKV_GN_EPS = 64e-5
SGU_CHUNK = 128
SGU_HEADS = 4
SGU_HEAD = GROUP_W // SGU_HEADS
HGRN_HEADS = 4
HGRN_HEAD = GROUP_W // HGRN_HEADS
HGRN_CHUNK = 64
POOL_WINDOWS = (2, 4, 8, 16)
POOL_GROUPS = len(POOL_WINDOWS)
POOL_CH = GROUP_W // POOL_GROUPS
POOL_HIST = max(POOL_WINDOWS) - 1
D_FF = -(-8 * D_MODEL // (3 * 256)) * 256
PLE_DIM = 256
NORM_EPS = 1e-6
LN_EPS = 1e-5
RWKV_COLS = 3 * GROUP_W + W_LORA + A_LORA + G_LORA
SGU_COLS = 2 * GROUP_W
HGRN_COLS = 4 * GROUP_W
POOL_COLS = GROUP_W
IN_COLS = RWKV_COLS + SGU_COLS + HGRN_COLS + POOL_COLS

kernel_name = "hybrid_rwkv7_sgu_hgrn2_pool_decode_step"

F32 = jnp.float32


def _rmsnorm(x, g):
    xf = x.astype(F32)
    return xf * lax.rsqrt(jnp.mean(xf * xf, axis=-1, keepdims=True) + NORM_EPS) * g.astype(F32)


def _rwkv7(z, shift_prev, wkv0, w):
    B, L, _ = z.shape
    G = GROUP_W
    z_prev = jnp.concatenate([shift_prev[:, None, :].astype(F32), z[:, :-1]], axis=1)
    zm = z + w['rwkv_mu'] * (z_prev - z)
    r, k, v, xw, xa, xg = jnp.split(zm, [G, 2 * G, 3 * G, 3 * G + W_LORA, 3 * G + W_LORA + A_LORA], axis=-1)
    w_log = -jax.nn.softplus(-(w['rwkv_w0'] + jnp.tanh(xw) @ w['rwkv_w_lora'])) - 0.5
    decay = jnp.exp(-jnp.exp(w_log))
    a = jax.nn.sigmoid(w['rwkv_a0'] + xa @ w['rwkv_a_lora'])
    g = jax.nn.sigmoid(xg) @ w['rwkv_g_lora']
    hs = lambda t: t.reshape(B, L, RWKV_HEADS, RWKV_HEAD)
    kk = hs(k * w['rwkv_k_k'])
    kk = kk / jnp.maximum(jnp.linalg.norm(kk, axis=-1, keepdims=True), 1e-12)
    k = k * (1.0 + (a - 1.0) * w['rwkv_k_a'])
    r, k, v, decay, a = hs(r), hs(k), hs(v), hs(decay), hs(a)
    a_vec = -kk
    b_vec = kk * a

    def step(S, inp):
        r_t, w_t, k_t, v_t, av_t, bv_t = inp
        sa = jnp.einsum('bhvk,bhk->bhv', S, av_t)
        S = (S * w_t[:, :, None, :] + sa[..., None] * bv_t[:, :, None, :]
             + v_t[..., None] * k_t[:, :, None, :])
        return S, jnp.einsum('bhvk,bhk->bhv', S, r_t)

    xs = tuple(jnp.moveaxis(t, 1, 0) for t in (r, decay, k, v, a_vec, b_vec))
    S_fin, ys = lax.scan(step, wkv0.astype(F32), xs)
    y = jnp.moveaxis(ys, 0, 1)
    mean = jnp.mean(y, axis=-1, keepdims=True)
    var = jnp.mean(jnp.square(y - mean), axis=-1, keepdims=True)
    yn = ((y - mean) * lax.rsqrt(var + RWKV_GN_EPS)).reshape(B, L, G) * w['rwkv_gn_w'] + w['rwkv_gn_b']
    bonus = (jnp.sum(r * k * w['rwkv_r_k'], axis=-1, keepdims=True) * v).reshape(B, L, G)
    return (yn + bonus) * g, S_fin, z[:, -1]


def _sgu(z, w):
    B, L, _ = z.shape
    z = jax.nn.gelu(z, approximate=False)
    u, v = jnp.split(z, 2, axis=-1)
    mu = jnp.mean(v, axis=-1, keepdims=True)
    var = jnp.mean(jnp.square(v - mu), axis=-1, keepdims=True)
    v = (v - mu) * lax.rsqrt(var + LN_EPS) * w['sgu_ln_w'] + w['sgu_ln_b']
    n = -(-L // SGU_CHUNK)
    vp = jnp.pad(v, ((0, 0), (0, n * SGU_CHUNK - L), (0, 0))).reshape(B, n, SGU_CHUNK, SGU_HEADS, SGU_HEAD)
    mask = jnp.tril(jnp.ones((SGU_CHUNK, SGU_CHUNK), dtype=bool))
    wm = jnp.where(mask[None], w['sgu_w'], 0.0)
    s = jnp.einsum('hts,bnshd->bnthd', wm, vp) + w['sgu_b'].T[None, None, :, :, None]
    s = s.reshape(B, n * SGU_CHUNK, GROUP_W)[:, :L]
    return _rmsnorm(u * s, w['sgu_norm']), v


def _hgrn2_chunked(q, k, v, logf, S0):
    B, H, L, _ = q.shape
    C = min(HGRN_CHUNK, L)
    n = -(-L // C)
    pad = n * C - L

    def blocks(t):
        t = jnp.pad(t, ((0, 0), (0, 0), (0, pad), (0, 0)))
        return jnp.moveaxis(t.reshape(B, H, n, C, t.shape[-1]), 2, 0)

    mask = jnp.tril(jnp.ones((C, C), dtype=bool))[:, :, None]

    def step(S, inp):
        qc, kc, vc, gc = inp
        b = jnp.cumsum(gc, axis=2)
        diff = b[:, :, :, None, :] - b[:, :, None, :, :]
        dec = jnp.exp(jnp.where(mask, diff, -jnp.inf))
        att = jnp.einsum('bhtsd,bhsd->bhts', qc[:, :, :, None, :] * dec, kc)
        o = (jnp.einsum('bhts,bhsv->bhtv', att, vc)
             + jnp.einsum('bhtd,bhdv->bhtv', qc * jnp.exp(b), S))
        b_last = b[:, :, -1:, :]
        S = (jnp.exp(b_last[:, :, 0, :])[..., None] * S
             + jnp.einsum('bhsd,bhsv->bhdv', kc * jnp.exp(b_last - b), vc))
        return S, o

    S, o = lax.scan(step, S0, (blocks(q), blocks(k), blocks(v), blocks(logf)))
    o = jnp.moveaxis(o, 0, 2).reshape(B, H, n * C, -1)[:, :, :L]
    return o, S


def _hgrn2(z, S0, lb, w):
    B, L, _ = z.shape
    q, f, i_in, g = jnp.split(z, 4, axis=-1)
    q = jax.nn.silu(q)
    fg = lb + (1.0 - lb) * jax.nn.sigmoid(f)
    logf = jnp.log(fg)
    k = 1.0 - fg
    th = lambda t: t.reshape(B, L, HGRN_HEADS, HGRN_HEAD).transpose(0, 2, 1, 3)
    o, S = _hgrn2_chunked(th(q), th(k), th(i_in), th(logf), S0.astype(F32))
    o = o.transpose(0, 2, 1, 3)
    o = o * lax.rsqrt(jnp.mean(o * o, axis=-1, keepdims=True) + NORM_EPS)
    return o.reshape(B, L, GROUP_W) * w['hgrn_norm'] * jax.nn.silu(g), S


def _pool(z, hist, start_pos, w):
    B, L, _ = z.shape
    full = jnp.concatenate([hist.astype(F32), z], axis=1)
    cs = jnp.pad(jnp.cumsum(full, axis=1), ((0, 0), (1, 0), (0, 0)))
    pos = start_pos + jnp.arange(L)
    outs = []
    for gi, win in enumerate(POOL_WINDOWS):
        sl = slice(gi * POOL_CH, (gi + 1) * POOL_CH)
        hi = cs[:, POOL_HIST + 1:POOL_HIST + 1 + L, sl]
        lo = cs[:, POOL_HIST + 1 - win:POOL_HIST + 1 - win + L, sl]
        cnt = jnp.minimum(pos + 1, win).astype(F32)
        outs.append((hi - lo) / cnt[None, :, None])
    d = (jnp.concatenate(outs, axis=-1) - z).reshape(B, L, POOL_GROUPS, POOL_CH)
    y = jnp.einsum('blgc,gcd->blgd', d, w['pool_w']).reshape(B, L, GROUP_W) * w['pool_scale']
    return y, full[:, -POOL_HIST:]


def _trunk(x, p, wkv0, shift0, hgrn0, pool0, start_pos, W, lb_all):
    dtype = x.dtype
    wkv_l, shift_l, hgrn_l, pool_l, sgu_l = [], [], [], [], []
    c1, c2, c3 = RWKV_COLS, RWKV_COLS + SGU_COLS, RWKV_COLS + SGU_COLS + HGRN_COLS
    for i in range(DEPTH):
        w = {name: arr[i] for name, arr in W.items()}
        h = _rmsnorm(x, w['ln_mix_pre'])
        z = jnp.einsum('bld,dc->blc', h, w['w_in']).astype(F32)
        z_r, z_s, z_h, z_p = jnp.split(z, [c1, c2, c3], axis=-1)
        y_r, wkv_new, shift_new = _rwkv7(z_r, shift0[i], wkv0[i], w)
        y_s, v_rows = _sgu(z_s, w)
        y_h, hgrn_new = _hgrn2(z_h, hgrn0[i], lb_all[i], w)
        y_p, pool_new = _pool(z_p, pool0[i], start_pos, w)
        mix = jnp.einsum('blc,cd->bld', jnp.concatenate([y_r, y_s, y_h, y_p], axis=-1), w['w_out'])
        xf = x.astype(F32) + _rmsnorm(mix, w['ln_mix_post'])
        h2 = _rmsnorm(xf, w['ln_ffn_pre'])
        gate, up = jnp.split(jnp.einsum('bld,df->blf', h2, w['ffn_w_gu']), 2, axis=-1)
        ff = jnp.einsum('blf,fd->bld', jax.nn.silu(gate) * up, w['ffn_w_down'])
        xf = xf + _rmsnorm(ff, w['ln_ffn_post'])
        ple = jnp.einsum('ble,ed->bld', p[i].astype(F32), w['ple_proj'])
        xf = xf + jax.nn.sigmoid(jnp.einsum('bld,de->ble', xf, w['ple_gate'])) * ple
        x = xf.astype(dtype)
        wkv_l.append(wkv_new)
        shift_l.append(shift_new)
        hgrn_l.append(hgrn_new)
        pool_l.append(pool_new)
        sgu_l.append(v_rows)
    return (x, jnp.stack(wkv_l).astype(dtype), jnp.stack(shift_l).astype(dtype),
            jnp.stack(hgrn_l).astype(dtype), jnp.stack(pool_l).astype(dtype),
            jnp.stack(sgu_l).astype(dtype))


def setup_inputs(seed: int = 0) -> dict:
    key = jax.random.key(seed)
    ks = iter(jax.random.split(key, 64))
    nrm = lambda shape, scale: scale * jax.random.normal(next(ks), shape, F32)
    gain = lambda shape: 1.0 + 0.1 * jax.random.normal(next(ks), shape, F32)
    Dp = DEPTH
    return {
        'x_prompt': nrm((BATCH, SEQ, D_MODEL), 1.0),
        'x_sample': nrm((DEC_BATCH, DEC_SEQ, D_MODEL), 1.0),
        'state_rwkv_wkv': nrm((Dp, DEC_BATCH, RWKV_HEADS, RWKV_HEAD, RWKV_HEAD), 0.3),
        'state_rwkv_shift': nrm((Dp, DEC_BATCH, RWKV_COLS), 1.0),
        'state_hgrn': nrm((Dp, DEC_BATCH, HGRN_HEADS, HGRN_HEAD, HGRN_HEAD), 0.5),
        'state_pool': nrm((Dp, DEC_BATCH, POOL_HIST, GROUP_W), 1.0),
        'p_prompt': nrm((Dp, BATCH, SEQ, PLE_DIM), 1.0),
        'p_sample': nrm((Dp, DEC_BATCH, DEC_SEQ, PLE_DIM), 1.0),
        'ln_mix_pre': gain((Dp, D_MODEL)),
        'ln_mix_post': gain((Dp, D_MODEL)),
        'ln_ffn_pre': gain((Dp, D_MODEL)),
        'ln_ffn_post': gain((Dp, D_MODEL)),
        'w_in': nrm((Dp, D_MODEL, IN_COLS), D_MODEL ** -0.5),
        'rwkv_mu': jax.random.uniform(next(ks), (Dp, RWKV_COLS), F32),
        'rwkv_w_lora': nrm((Dp, W_LORA, GROUP_W), W_LORA ** -0.5),
        'rwkv_w0': nrm((Dp, GROUP_W), 0.5),
        'rwkv_a_lora': nrm((Dp, A_LORA, GROUP_W), A_LORA ** -0.5),
        'rwkv_a0': nrm((Dp, GROUP_W), 0.1),
        'rwkv_g_lora': nrm((Dp, G_LORA, GROUP_W), G_LORA ** -0.5),
        'rwkv_k_k': gain((Dp, GROUP_W)),
        'rwkv_k_a': gain((Dp, GROUP_W)),
        'rwkv_r_k': nrm((Dp, RWKV_HEADS, RWKV_HEAD), 0.1),
        'rwkv_gn_w': gain((Dp, GROUP_W)),
        'rwkv_gn_b': nrm((Dp, GROUP_W), 0.01),
        'sgu_ln_w': gain((Dp, GROUP_W)),
        'sgu_ln_b': nrm((Dp, GROUP_W), 0.02),
        'sgu_w': nrm((Dp, SGU_HEADS, SGU_CHUNK, SGU_CHUNK), SGU_CHUNK ** -0.5),
        'sgu_b': gain((Dp, SGU_HEADS, SGU_CHUNK)),
        'sgu_norm': gain((Dp, GROUP_W)),
        'hgrn_lb_logits': nrm((Dp, GROUP_W), 0.5),
        'hgrn_norm': gain((Dp, GROUP_W)),
        'pool_w': nrm((Dp, POOL_GROUPS, POOL_CH, POOL_CH), POOL_CH ** -0.5),
        'pool_scale': gain((Dp, GROUP_W)),
        'w_out': nrm((Dp, D_MODEL, D_MODEL), D_MODEL ** -0.5),
        'ffn_w_gu': nrm((Dp, D_MODEL, 2 * D_FF), D_MODEL ** -0.5),
        'ffn_w_down': nrm((Dp, D_FF, D_MODEL), D_FF ** -0.5),
        'ple_gate': nrm((Dp, D_MODEL, D_MODEL), D_MODEL ** -0.5),
        'ple_proj': nrm((Dp, PLE_DIM, D_MODEL), PLE_DIM ** -0.5),
    }


def reference(x_prompt, x_sample, state_rwkv_wkv, state_rwkv_shift, state_hgrn, state_pool,
              p_prompt, p_sample, ln_mix_pre, ln_mix_post, ln_ffn_pre, ln_ffn_post, w_in,
              rwkv_mu, rwkv_w_lora, rwkv_w0, rwkv_a_lora, rwkv_a0, rwkv_g_lora, rwkv_k_k,
              rwkv_k_a, rwkv_r_k, rwkv_gn_w, rwkv_gn_b, sgu_ln_w, sgu_ln_b, sgu_w, sgu_b,
              sgu_norm, hgrn_lb_logits, hgrn_norm, pool_w, pool_scale, w_out, ffn_w_gu,
              ffn_w_down, ple_gate, ple_proj):
    W = dict(ln_mix_pre=ln_mix_pre, ln_mix_post=ln_mix_post, ln_ffn_pre=ln_ffn_pre,
             ln_ffn_post=ln_ffn_post, w_in=w_in, rwkv_mu=rwkv_mu, rwkv_w_lora=rwkv_w_lora,
             rwkv_w0=rwkv_w0, rwkv_a_lora=rwkv_a_lora, rwkv_a0=rwkv_a0, rwkv_g_lora=rwkv_g_lora,
             rwkv_k_k=rwkv_k_k, rwkv_k_a=rwkv_k_a, rwkv_r_k=rwkv_r_k, rwkv_gn_w=rwkv_gn_w,
             rwkv_gn_b=rwkv_gn_b, sgu_ln_w=sgu_ln_w, sgu_ln_b=sgu_ln_b, sgu_w=sgu_w, sgu_b=sgu_b,
             sgu_norm=sgu_norm, hgrn_norm=hgrn_norm, pool_w=pool_w, pool_scale=pool_scale,
             w_out=w_out, ffn_w_gu=ffn_w_gu, ffn_w_down=ffn_w_down, ple_gate=ple_gate,
             ple_proj=ple_proj)
    lb_soft = jax.nn.softmax(hgrn_lb_logits.astype(F32), axis=0)
    lb_all = jnp.cumsum(lb_soft, axis=0) - lb_soft[0:1]
    zeros_wkv = jnp.zeros((DEPTH, BATCH, RWKV_HEADS, RWKV_HEAD, RWKV_HEAD), F32)
    zeros_shift = jnp.zeros((DEPTH, BATCH, RWKV_COLS), F32)
    zeros_hgrn = jnp.zeros((DEPTH, BATCH, HGRN_HEADS, HGRN_HEAD, HGRN_HEAD), F32)
    zeros_pool = jnp.zeros((DEPTH, BATCH, POOL_HIST, GROUP_W), F32)
    y_prompt, wkv_p, shift_p, hgrn_p, pool_p, _ = _trunk(
        x_prompt, p_prompt, zeros_wkv, zeros_shift, zeros_hgrn, zeros_pool, 0, W, lb_all)
    y_sample, wkv_s, shift_s, hgrn_s, pool_s, sgu_v_s = _trunk(
        x_sample, p_sample, state_rwkv_wkv, state_rwkv_shift, state_hgrn, state_pool,
        PAST_LEN, W, lb_all)
    return (y_prompt, y_sample, wkv_p, shift_p, hgrn_p, pool_p, wkv_s, shift_s, hgrn_s, pool_s, sgu_v_s)
```

```python
import numpy as np
import concourse.bass as bass
import concourse.mybir as mybir
from concourse.bass_utils import run_bass_kernel_spmd

F32 = mybir.dt.float32
BF16 = mybir.dt.bfloat16
AF = mybir.ActivationFunctionType
ALU = mybir.AluOpType
AX = mybir.AxisListType
PE, DVE, ACT, POOL, SP = "tensor", "vector", "scalar", "gpsimd", "sync"

D = 2048
DEPTH = 2
SEQ = 2048
NSEQ_S = 16
LS = 4
G = 512
RC = 1696
IN_COLS = 5280
DFF = 5632
PLE = 256
C_R, C_S, C_H, C_P = 0, 1696, 2720, 4768
NEG_E05 = -0.6065306597126334


class Res:
    __slots__ = ("name", "t", "last_w", "readers", "const", "dma_w")

    def __init__(self, name, t, const=False):
        self.name = name
        self.t = t
        self.last_w = None
        self.readers = []
        self.const = const
        self.dma_w = []

    def __getitem__(self, idx):
        return self.t[idx]


class Op:
    __slots__ = ("eng", "fn", "deps", "signal", "sigidx", "is_dma", "sem", "target")

    def __init__(self, eng, fn, is_dma):
        self.eng = eng
        self.fn = fn
        self.deps = []
        self.signal = False
        self.sigidx = 0
        self.is_dma = is_dma
        self.sem = None
        self.target = 0


class Prog:
    NDMASEM = 48

    def __init__(self, nc):
        self.nc = nc
        self.ops = []
        self.dma_last = [None] * self.NDMASEM
        self.dma_uses = [0] * self.NDMASEM
        self.dma_rr = 0
        self.out_dmas = []

    def sb(self, name, shape, dt=F32):
        return Res(name, self.nc.alloc_sbuf_tensor(name, list(shape), dt))

    def ps(self, name, shape, dt=F32):
        return Res(name, self.nc.alloc_psum_tensor(name, list(shape), dt))

    def dram(self, name, shape, dt=F32, kind="Internal"):
        return Res(name, self.nc.dram_tensor(name, list(shape), dt, kind=kind).ap())

    def view(self, name, ap):
        return Res(name, ap)

    def op(self, eng, fn, r=(), w=(), dma=False, out=False):
        o = Op(eng, fn, dma)
        deps = {}
        for res in r:
            if res.last_w is not None:
                deps[id(res.last_w)] = (res.last_w, "RAW")
            for dw in res.dma_w:
                deps[id(dw)] = (dw, "RAW")
        for res in w:
            if res.last_w is not None and id(res.last_w) not in deps:
                deps[id(res.last_w)] = (res.last_w, "WAW")
            for dw in res.dma_w:
                if id(dw) not in deps:
                    deps[id(dw)] = (dw, "WAW")
            for rd in res.readers:
                if id(rd) not in deps:
                    deps[id(rd)] = (rd, "WAR")
        for p, kind in deps.values():
            if (not p.is_dma) and (not dma) and p.eng == eng:
                if eng == PE or kind != "RAW":
                    continue
            o.deps.append(p)
        if dma:
            j = self.dma_rr
            self.dma_rr = (j + 1) % self.NDMASEM
            prev = self.dma_last[j]
            if prev is not None and all(prev is not d for d in o.deps):
                o.deps.append(prev)
            self.dma_uses[j] += 1
            o.sem = j
            o.target = 16 * self.dma_uses[j]
            self.dma_last[j] = o
            if out:
                self.out_dmas.append(o)
        for res in r:
            if not res.const:
                if not dma:
                    res.readers = [x for x in res.readers if x.is_dma or x.eng != eng]
                res.readers.append(o)
        for res in w:
            res.last_w = o
            res.readers = []
            if dma:
                res.dma_w.append(o)
                if len(res.dma_w) > 40:
                    res.dma_w = res.dma_w[-40:]
            else:
                res.dma_w = []
        self.ops.append(o)
        return o

    def fence(self, eng, fn, ress):
        return self.op(eng, fn, r=(), w=list(ress))

    def mm(self, out_ap, lhsT, rhs, start, stop, r, w):
        return self.op(PE, lambda e: e.matmul(out_ap, lhsT, rhs, start=start, stop=stop), r=r, w=w)

    def tr(self, out_ap, in_ap, ident_ap, r, w):
        return self.op(PE, lambda e: e.transpose(out_ap, in_ap, ident_ap), r=r, w=w)

    def dma(self, out_ap, in_ap, r, w, eng=SP, out=False, **kw):
        return self.op(eng, lambda e: e.dma_start(out=out_ap, in_=in_ap, **kw), r=r, w=w, dma=True, out=out)

    def emit(self):
        nc = self.nc
        ops = self.ops
        fin = Op(SP, None, False)
        fin.deps = list(self.out_dmas)
        ops.append(fin)
        for o in ops:
            for d in o.deps:
                d.signal = True
        engs = [PE, DVE, ACT, POOL, SP]
        cnt = {e: 0 for e in engs}
        for o in ops:
            if o.signal and not o.is_dma:
                cnt[o.eng] += 1
                o.sigidx = cnt[o.eng]
        esem = {e: nc.alloc_semaphore("es_" + e) for e in engs}
        dsem = [nc.alloc_semaphore("ds_%d" % j) for j in range(self.NDMASEM)]
        per = {e: [o for o in ops if o.eng == e] for e in engs}
        NS = self.NDMASEM

        nw = {x: 0 for x in engs}

        def run(e, engobj):
            seen = {x: 0 for x in engs}
            seend = [0] * NS
            for o in per[e]:
                nw[e] += sum(1 for d in o.deps if ((seend[d.sem] < d.target) if d.is_dma else (seen[d.eng] < d.sigidx)))
                for d in o.deps:
                    if d.is_dma:
                        if seend[d.sem] >= d.target:
                            continue
                        engobj.wait_ge(dsem[d.sem], d.target)
                        seend[d.sem] = d.target
                    else:
                        if seen[d.eng] >= d.sigidx:
                            continue
                        engobj.wait_ge(esem[d.eng], d.sigidx)
                        seen[d.eng] = d.sigidx
                if o.fn is None:
                    continue
                ins = o.fn(engobj)
                if o.is_dma:
                    ins.then_inc(dsem[o.sem], 16)
                elif o.signal:
                    ins.then_inc(esem[e], 1)

        with nc.Block() as block:
            @block.tensor
            def _(e):
                run(PE, e)

            @block.vector
            def _(e):
                run(DVE, e)

            @block.scalar
            def _(e):
                run(ACT, e)

            @block.gpsimd
            def _(e):
                run(POOL, e)

            @block.sync
            def _(e):
                run(SP, e)
        return dict(n_ops=len(ops), per={e: len(per[e]) for e in engs}, sig=cnt, waits=nw)


CONST_COLS = {}


def _make_consts():
    cols = []
    off = [0]

    def add(name, arr):
        a = np.zeros((128, arr.shape[1]), np.float32)
        a[:arr.shape[0]] = arr
        CONST_COLS[name] = (off[0], arr.shape[1])
        off[0] += arr.shape[1]
        cols.append(a)

    add("ident", np.eye(128, dtype=np.float32))
    bo = np.zeros((128, 128), np.float32)
    bo[:64, :64] = 1
    bo[64:, 64:] = 1
    add("blockones", bo)
    add("ones", np.ones((128, 128), np.float32))
    i = np.arange(64)
    add("SU", (i[:, None] < i[None, :]).astype(np.float32))
    add("IU", (i[:, None] <= i[None, :]).astype(np.float32))
    add("SL", (i[:, None] > i[None, :]).astype(np.float32))
    r64 = np.ones((128, 512), np.float32)
    r64[:, ::64] = 0
    add("rst64", r64)
    r4 = np.ones((128, 64), np.float32)
    r4[:, ::4] = 0
    add("rst4", r4)
    t128 = np.arange(128)
    add("TRIL", (t128[:, None] >= t128[None, :]).astype(np.float32))
    pc = np.ones((128, 64), np.float32)
    for gi, win in enumerate((2, 4, 8, 16)):
        pos = np.arange(16)
        pc[:, gi * 16:(gi + 1) * 16] = (win / np.minimum(pos + 1, win))[None, :]
    add("poolc", pc)
    bd = np.zeros((64, 64), np.float32)
    for s in range(16):
        bd[4 * s:4 * s + 4, 4 * s:4 * s + 4] = np.tril(np.ones((4, 4))).T
    add("BD4T", bd.T.copy())
    return np.concatenate(cols, axis=1)


CONSTS = _make_consts()
NCONST = CONSTS.shape[1]

_BUILT = {}
_DEBUG = False
_RW_LEVEL = 3


def build(stage=99, mini=None):
    nc = bass.Bass("TRN2", target_bir_lowering=False)
    P = Prog(nc)
    EI, EO = "ExternalInput", "ExternalOutput"
    d_in = {}

    def din(name, shape):
        d_in[name] = P.dram(name, shape, F32, kind=EI)
        d_in[name].const = True
        return d_in[name]

    xp = din("xp", [SEQ, D])
    xs = din("xs", [64, D])
    pp = din("pp", [DEPTH, SEQ, PLE])
    psm = din("psm", [DEPTH, 64, PLE])
    st_wkv = din("st_wkv", [DEPTH, NSEQ_S, 8, 64, 64])
    st_shift = din("st_shift", [DEPTH, NSEQ_S, RC])
    st_hgrn = din("st_hgrn", [DEPTH, NSEQ_S, 4, 128, 128])
    st_pool = din("st_pool", [DEPTH, NSEQ_S, 15, G])
    consts_d = din("consts", [128, NCONST])
    wnames = dict(ln_mix_pre=[DEPTH, D], ln_mix_post=[DEPTH, D], ln_ffn_pre=[DEPTH, D], ln_ffn_post=[DEPTH, D],
                  w_in=[DEPTH, D, IN_COLS], rwkv_mu=[DEPTH, RC], rwkv_w_lora=[DEPTH, 32, G], rwkv_w0=[DEPTH, G],
                  rwkv_a_lora=[DEPTH, 32, G], rwkv_a0=[DEPTH, G], rwkv_g_lora=[DEPTH, 96, G], rwkv_k_k=[DEPTH, G],
                  rwkv_k_a=[DEPTH, G], rwkv_r_k=[DEPTH, G], rwkv_gn_w=[DEPTH, G], rwkv_gn_b=[DEPTH, G],
                  sgu_ln_w=[DEPTH, G], sgu_ln_b=[DEPTH, G], sgu_w=[DEPTH, 4, 128, 128], sgu_b=[DEPTH, 4, 128],
                  sgu_norm=[DEPTH, G], hgrn_lb_logits=[DEPTH, G], hgrn_norm=[DEPTH, G], pool_w=[DEPTH, 4, 128, 128],
                  pool_scale=[DEPTH, G], w_out=[DEPTH, D, D], ffn_w_gu=[DEPTH, D, 2 * DFF], ffn_w_down=[DEPTH, DFF, D],
                  ple_gate=[DEPTH, D, D], ple_proj=[DEPTH, PLE, D])
    W = {k: din(k, v) for k, v in wnames.items()}
    d_out = {}

    def dout(name, shape):
        d_out[name] = P.dram(name, shape, F32, kind=EO)
        return d_out[name]

    y_p = dout("y_p", [SEQ, D])
    y_s = dout("y_s", [64, D])
    wkv_p = dout("wkv_p", [DEPTH, 8, 64, 64])
    shift_p = dout("shift_p", [DEPTH, RC])
    hgrn_p = dout("hgrn_p", [DEPTH, 4, 128, 128])
    pool_p = dout("pool_p", [DEPTH, 15, G])
    wkv_s = dout("wkv_s", [DEPTH, NSEQ_S, 8, 64, 64])
    shift_s = dout("shift_s", [DEPTH, NSEQ_S, RC])
    hgrn_s = dout("hgrn_s", [DEPTH, NSEQ_S, 4, 128, 128])
    pool_s = dout("pool_s", [DEPTH, NSEQ_S, 15, G])
    sguv_s = dout("sguv_s", [DEPTH, NSEQ_S, LS, G])
    DBG = EO if _DEBUG else "Internal"
    xmid_p = P.dram("xmid_p", [SEQ, D], kind=DBG)
    xmid_s = P.dram("xmid_s", [64, D], kind=DBG)
    xf1_d = P.dram("xf1_d", [512, D], kind=DBG)
    dbg_d = P.dram("dbg_d", [6, 128, D], kind=DBG)

    cst = P.sb("cst", [128, NCONST])
    cstb = P.sb("cstb", [128, 384], BF16)
    P.dma(cst[:, :], consts_d[:, :], r=[consts_d], w=[cst])
    P.dma(cstb[:, :], consts_d[:, 0:384], r=[consts_d], w=[cstb], eng=POOL)
    cst.const = True
    cstb.const = True

    def CF(name, rows=128, c0=0, c1=None):
        o, n = CONST_COLS[name]
        c1 = n if c1 is None else c1
        return cst[0:rows, o + c0:o + c1]

    def CB(name, rows=128, c0=0, c1=None):
        o, n = CONST_COLS[name]
        c1 = n if c1 is None else c1
        return cstb[0:rows, o + c0:o + c1]

    NSLOT = 3
    wslot = [P.sb("wslot%d" % i, [128, 16, 512], BF16) for i in range(NSLOT)]
    slot_rr = [0]
    hT = P.sb("hT", [128, 16, 512], BF16)
    yT = P.sb("yT", [128, 16, 512], BF16)
    xa = P.sb("xa", [128, D])
    xb = P.sb("xb", [128, D])
    xnb = P.sb("xnb", [128, D], BF16)
    gbc = P.sb("gbc", [128, D])
    stat = P.sb("stat", [128, 8])
    epsn = P.sb("epsn", [128, 1])
    P.op(DVE, lambda e: e.memset(epsn[:, :], 1e-6), w=[epsn])
    epsn.const = True
    ARENA_F = 4 * D + 44 * 256
    arena = nc.alloc_sbuf_tensor("arena", [128, ARENA_F], F32)
    big = P.view("big", arena[:, 0:4 * D].rearrange("p (a b) -> p a b", a=4))
    actT = P.view("actT", arena[:, 4 * D:ARENA_F].bitcast(BF16).rearrange("p (a b) -> p a b", a=44))
    pT = P.view("pT", arena[:, 4 * D:4 * D + 512].bitcast(BF16).rearrange("p (a b) -> p a b", a=2))
    aoff = [0]
    mix_views = []
    last_fence = [None]

    def arena_reset():
        aoff[0] = 0
        del mix_views[:]

    def av(name, rows, cols, dt=F32):
        n32 = cols if dt == F32 else (cols + 1) // 2
        a0 = aoff[0]
        aoff[0] += n32
        assert aoff[0] <= ARENA_F, (name, aoff[0])
        ap = arena[0:rows, a0:a0 + n32]
        if dt != F32:
            ap = ap.bitcast(dt)
        r_ = P.view(name, ap)
        r_.last_w = last_fence[0]
        mix_views.append(r_)
        return r_

    fence_t = P.sb("fence_t", [128, 1])

    def arena_fence():
        last_fence[0] = P.fence(DVE, lambda e: e.memset(fence_t[:, :], 0.0), [fence_t, big, actT, pT] + mix_views)
    silu_t = P.sb("silu_t", [128, 512])
    pbig = [P.ps("pbig%d" % i, [128, 512]) for i in range(3)]
    ptr = [P.ps("ptr%d" % i, [128, 512], BF16) for i in range(2)]
    pg = P.ps("pg", [128, 512])
    pu = P.ps("pu", [128, 512])
    rr = {"pbig": 0, "ptr": 0}

    def next_pbig():
        rr["pbig"] = (rr["pbig"] + 1) % 3
        return pbig[rr["pbig"]]

    def next_ptr():
        rr["ptr"] = (rr["ptr"] + 1) % 2
        return ptr[rr["ptr"]]

    def load_w(wres, l, k0, nk, c0, ncols):
        s = wslot[slot_rr[0]]
        slot_rr[0] = (slot_rr[0] + 1) % NSLOT
        src = wres.t[l].rearrange("(kc p) c -> p kc c", p=128)[:, k0:k0 + nk, c0:c0 + ncols]
        P.dma(s[:, 0:nk, 0:ncols], src, r=[wres], w=[s], eng=POOL)
        return s

    def bcast_load(dst, vec_res, l, n=D, c0=0):
        v = vec_res.t[l]
        src = bass.AP(tensor=v.tensor, offset=v.offset + c0, ap=[[0, 128], [1, n]])
        P.dma(dst[:, 0:n], src, r=[vec_res], w=[dst])

    def rms_rstd(src_res, src_ap, m, ncol, col, scratch_res, scratch_ap, inv_n, eps_res):
        P.op(DVE, lambda e: e.memset(stat[0:m, col:col + 1], 0.0), w=[stat])
        P.op(ACT, lambda e: e.activation(scratch_ap, src_ap, AF.Square, accum_out=stat[0:m, col:col + 1]),
             r=[src_res, stat], w=[scratch_res, stat])
        P.op(ACT, lambda e: e.activation(stat[0:m, col + 1:col + 2], stat[0:m, col:col + 1], AF.Sqrt,
                                         bias=eps_res[0:m, 0:1], scale=inv_n), r=[stat, eps_res], w=[stat])
        P.op(DVE, lambda e: e.reciprocal(stat[0:m, col + 2:col + 3], stat[0:m, col + 1:col + 2]), r=[stat], w=[stat])
        return stat[0:m, col + 2:col + 3]

    def to_featmajor(src_res, src_bf_ap_fn, m, t0, dstT, nkc=16, kc0=0):
        for g4 in range(0, nkc, 4):
            n4 = min(4, nkc - g4)
            pt = next_ptr()
            for q in range(n4):
                kc = g4 + q
                P.tr(pt[:, q * 128:q * 128 + m], src_bf_ap_fn(kc), CB("ident", m, 0, m), r=[src_res, cstb], w=[pt])
            o_ap = dstT[:, kc0 + g4:kc0 + g4 + n4, t0:t0 + m]
            i_ap = pt[:, 0:n4 * 128].rearrange("p (a b) -> p a b", a=n4)[:, :, 0:m]
            if (g4 // 4) % 2 == 0:
                P.op(DVE, lambda e, o_ap=o_ap, i_ap=i_ap: e.tensor_copy(o_ap, i_ap), r=[pt], w=[dstT])
            else:
                P.op(ACT, lambda e, o_ap=o_ap, i_ap=i_ap: e.copy(o_ap, i_ap), r=[pt], w=[dstT])


    NCOLP = 80
    pcol = P.sb("pcol", [128, NCOLP])
    PC = {}

    def load_cols(name, vec_res, l, base, n=G):
        nchk = n // 128
        src = vec_res.t[l, 0:n].rearrange("(j p) -> p j", p=128)
        P.dma(pcol[:, base:base + nchk], src, r=[vec_res], w=[pcol], allow_slow_non_contiguous=True)
        PC[name] = base

    pw_sb = P.sb("pw_sb", [128, 4, 128], BF16)
    pool_carry = P.sb("pool_carry", [128, 4, 15])
    sgu_bc = P.sb("sgu_bc", [128, 3, G])
    wmT = P.sb("wmT", [128, 4, 128], BF16)
    bd4 = P.sb("bd4", [64, 4, 64], BF16)
    sgub_p = P.sb("sgub_p", [128, 4])
    sgub_s = P.sb("sgub_s", [64, 4])
    px = P.ps("px", [128, 512])

    def layer_params(l):
        load_cols("pool_scale", W["pool_scale"], l, 0)
        load_cols("hgrn_norm", W["hgrn_norm"], l, 4)
        load_cols("lg0", W["hgrn_lb_logits"], 0, 8)
        load_cols("lg1", W["hgrn_lb_logits"], 1, 12)
        PC["lb"], PC["oml"] = 16, 20
        load_cols("mu", W["rwkv_mu"], l, 24, n=1536)
        for q, (a_, b_) in enumerate(((1536, 1568), (1568, 1600), (1600, 1696))):
            P.dma(pcol[0:b_ - a_, 36 + q:37 + q], W["rwkv_mu"].t[l, a_:b_].rearrange("(c o) -> c o", o=1), r=[W["rwkv_mu"]], w=[pcol],
                  allow_slow_non_contiguous=True)
        for q, nm in enumerate(("w0", "a0", "k_k", "k_a", "r_k", "gn_w", "gn_b")):
            load_cols(nm, W["rwkv_" + nm], l, 40 + 4 * q)
        PC["omk"] = 68
        P.op(DVE, lambda e: e.tensor_scalar(pcol[:, 68:72], pcol[:, PC["k_a"]:PC["k_a"] + 4], -1.0, 1.0, ALU.mult, ALU.add), r=[pcol], w=[pcol])
        if l == 0:
            P.op(DVE, lambda e: e.memset(pcol[:, 16:20], 0.0), w=[pcol])
            P.op(DVE, lambda e: e.memset(pcol[:, 20:24], 1.0), w=[pcol])
        else:
            P.op(DVE, lambda e: e.tensor_tensor(pcol[:, 16:20], pcol[:, 12:16], pcol[:, 8:12], ALU.subtract), r=[pcol], w=[pcol])
            P.op(ACT, lambda e: e.activation(pcol[:, 16:20], pcol[:, 16:20], AF.Sigmoid), r=[pcol], w=[pcol])
            P.op(DVE, lambda e: e.tensor_scalar(pcol[:, 20:24], pcol[:, 16:20], -1.0, 1.0, ALU.mult, ALU.add), r=[pcol], w=[pcol])
        P.dma(pw_sb[:, :, :], W["pool_w"].t[l].rearrange("g c d -> c g d"), r=[W["pool_w"]], w=[pw_sb], eng=POOL)
        for i, nm in enumerate(("sgu_ln_w", "sgu_ln_b", "sgu_norm")):
            v = W[nm].t[l]
            P.dma(sgu_bc[:, i, :], bass.AP(tensor=v.tensor, offset=v.offset, ap=[[0, 128], [1, G]]), r=[W[nm]], w=[sgu_bc])
        wtmp = P.view("wtmp", arena[:, 0:512].rearrange("p (a b) -> p a b", a=4))
        wtmp.last_w = last_fence[0]
        wtb = P.view("wtb", arena[:, 512:768].bitcast(BF16).rearrange("p (a b) -> p a b", a=4))
        wtb.last_w = last_fence[0]
        mix_views.extend([wtmp, wtb])
        P.dma(wtmp[:, :, :], W["sgu_w"].t[l].rearrange("h t s -> t h s"), r=[W["sgu_w"]], w=[wtmp])
        tril = CF("TRIL")
        P.op(DVE, lambda e: e.tensor_tensor(wtb[:, :, :], wtmp[:, :, :], tril.unsqueeze(1).to_broadcast([128, 4, 128]), ALU.mult),
             r=[wtmp, cst], w=[wtb])
        pt = next_ptr()
        for h in range(4):
            P.tr(pt[:, h * 128:(h + 1) * 128], wtb[:, h, :], CB("ident"), r=[wtb, cstb], w=[pt])
        P.op(DVE, lambda e: e.tensor_copy(wmT[:, :, :], pt[:, :].rearrange("p (a b) -> p a b", a=4)), r=[pt], w=[wmT])
        P.dma(sgub_p[:, :], W["sgu_b"].t[l].rearrange("h t -> t h"), r=[W["sgu_b"]], w=[sgub_p], allow_slow_non_contiguous=True)
        w4 = P.view("w4", arena[0:64, 768:1024].rearrange("p (a b) -> p a b", a=4))
        w4.last_w = last_fence[0]
        w4b = P.view("w4b", arena[0:64, 1024:1152].bitcast(BF16).rearrange("p (a b) -> p a b", a=4))
        w4b.last_w = last_fence[0]
        mix_views.extend([w4, w4b])
        P.op(DVE, lambda e: e.memset(w4[:, :, :], 0.0), w=[w4])
        for sq in range(NSEQ_S):
            P.dma(w4[4 * sq:4 * sq + 4, :, 4 * sq:4 * sq + 4], W["sgu_w"].t[l, :, 0:4, 0:4].rearrange("h t s -> t h s"),
                  r=[W["sgu_w"]], w=[w4], allow_slow_non_contiguous=True)
            P.dma(sgub_s[4 * sq:4 * sq + 4, :], W["sgu_b"].t[l, :, 0:4].rearrange("h t -> t h"), r=[W["sgu_b"]], w=[sgub_s],
                  allow_slow_non_contiguous=True)
        bdt = CF("BD4T", 64)
        P.op(DVE, lambda e: e.tensor_tensor(w4b[:, :, :], w4[:, :, :], bdt.unsqueeze(1).to_broadcast([64, 4, 64]), ALU.mult),
             r=[w4, cst], w=[w4b])
        pt2 = next_ptr()
        for h in range(4):
            P.tr(pt2[0:64, h * 64:(h + 1) * 64], w4b[:, h, :], CB("ident", 64, 0, 64), r=[w4b, cstb], w=[pt2])
        P.op(DVE, lambda e: e.tensor_copy(bd4[:, :, :], pt2[0:64, 0:256].rearrange("p (a b) -> p a b", a=4)), r=[pt2], w=[bd4])

    def inproj_fm(l, col0, ncols, NT, consume):
        c = 0
        ci = 0
        while c < ncols:
            n = min(512, ncols - c)
            s = load_w(W["w_in"], l, 0, 16, col0 + c, n)
            for q in range(0, n, 128):
                rows = min(128, n - q)
                pb = next_pbig()
                for kc in range(16):
                    P.mm(pb[0:rows, 0:NT], s[:, kc, q:q + rows], hT[:, kc, 0:NT], kc == 0, kc == 15, r=[s, hT], w=[pb])
                consume(ci, pb, rows)
                ci += 1
            c += n

    def mixer_pool(l, kind, tok0, NT):
        nseq, L = (1, NT) if kind == "p" else (NSEQ_S, LS)
        E = 15 + L
        Zx = [av("poolZ%d" % g, 128, nseq * E) for g in range(4)]
        Ea = av("poolEa", 128, nseq * E)
        Eb = av("poolEb", 128, nseq * E)
        dbf = av("pooldbf", 128, NT, BF16)
        v3 = lambda r_: r_[:, :].rearrange("p (s e) -> p s e", s=nseq)
        for g in range(4):
            z3 = v3(Zx[g])
            if kind == "p":
                if tok0 == 0:
                    P.op(DVE, lambda e, z3=z3: e.memset(z3[:, :, 0:15], 0.0), w=[Zx[g]])
                else:
                    P.op(DVE, lambda e, z3=z3, g=g: e.tensor_copy(z3[:, 0, 0:15], pool_carry[:, g, :]), r=[pool_carry], w=[Zx[g]])
            else:
                for sq in range(NSEQ_S):
                    P.dma(z3[:, sq, 0:15], st_pool.t[l, sq, :, g * 128:(g + 1) * 128].rearrange("p c -> c p"),
                          r=[st_pool], w=[Zx[g]], allow_slow_non_contiguous=True)

        def consume(ci, pb, rows):
            z3 = v3(Zx[ci])
            P.op(ACT, lambda e: e.copy(z3[:, :, 15:E], pb[:, 0:NT].rearrange("p (s e) -> p s e", s=nseq)), r=[pb], w=[Zx[ci]])
        inproj_fm(l, C_P, G, NT, consume)
        for g, win in enumerate((2, 4, 8, 16)):
            src = Zx[g]
            bufs = [Ea, Eb]
            for k in range(1, g + 2):
                sft = 1 << (k - 1)
                lo = (1 << k) - 1
                dst = bufs[k % 2]
                s3, d3 = v3(src), v3(dst)
                P.op(DVE, lambda e, s3=s3, d3=d3, lo=lo, sft=sft: e.tensor_tensor(d3[:, :, lo:E], s3[:, :, lo:E], s3[:, :, lo - sft:E - sft], ALU.add),
                     r=[src], w=[dst])
                src = dst
            s3, z3 = v3(src), v3(Zx[g])
            d3 = dbf[:, :].rearrange("p (s e) -> p s e", s=nseq)
            P.op(DVE, lambda e, s3=s3, z3=z3, d3=d3, win=win: e.scalar_tensor_tensor(d3, s3[:, :, 15:E], 1.0 / win, z3[:, :, 15:E], ALU.mult, ALU.subtract),
                 r=[src, Zx[g]], w=[dbf])
            if kind == "p" and tok0 == 0:
                tmpc = bufs[(g + 2) % 2]
                pcv = CF("poolc", 128, g * 16, (g + 1) * 16)
                P.op(DVE, lambda e, s3=s3, tmpc=tmpc, pcv=pcv: e.tensor_tensor(tmpc[:, 0:16], s3[:, 0, 15:31], pcv, ALU.mult), r=[src, cst], w=[tmpc])
                P.op(DVE, lambda e, tmpc=tmpc, z3=z3, win=win: e.scalar_tensor_tensor(dbf[:, 0:16], tmpc[:, 0:16], 1.0 / win, z3[:, 0, 15:31], ALU.mult, ALU.subtract),
                     r=[tmpc, Zx[g]], w=[dbf])
            P.mm(px[:, 0:NT], pw_sb[:, g, :], dbf[:, 0:NT], True, True, r=[pw_sb, dbf], w=[px])
            sc = pcol[:, PC["pool_scale"] + g:PC["pool_scale"] + g + 1]
            P.op(ACT, lambda e, g=g, sc=sc: e.activation(yT[:, 12 + g, 0:NT], px[:, 0:NT], AF.Copy, scale=sc), r=[px, pcol], w=[yT])
            if kind == "p":
                if tok0 + NT < SEQ:
                    P.op(ACT, lambda e, z3=z3, g=g: e.copy(pool_carry[:, g, :], z3[:, 0, L:E]), r=[Zx[g]], w=[pool_carry])
                else:
                    P.dma(pool_p.t[l, :, g * 128:(g + 1) * 128].rearrange("p c -> c p"), z3[:, 0, L:E], r=[Zx[g]], w=[pool_p],
                          out=True, allow_slow_non_contiguous=True)
            else:
                for sq in range(NSEQ_S):
                    P.dma(pool_s.t[l, sq, :, g * 128:(g + 1) * 128].rearrange("p c -> c p"), z3[:, sq, L:E], r=[Zx[g]], w=[pool_s],
                          out=True, allow_slow_non_contiguous=True)

    def mixer_sgu(l, kind, tok0, NT, tiles):
        su_ = load_w(W["w_in"], l, 0, 16, C_S, 512)
        sv_ = load_w(W["w_in"], l, 0, 16, C_S + 512, 512)
        u_sb = av("sgu_u", 128, G)
        v_sb = av("sgu_v", 128, G)
        t_sb = av("sgu_t", 128, G)
        vnb = av("sgu_vnb", 128, G, BF16)
        ynb = av("sgu_ynb", 128, G, BF16)
        for (o, m) in tiles:
            pbu = next_pbig()
            for kc in range(16):
                P.mm(pbu[0:m, :], hT[:, kc, o:o + m], su_[:, kc, :], kc == 0, kc == 15, r=[hT, su_], w=[pbu])
            pbv = next_pbig()
            for kc in range(16):
                P.mm(pbv[0:m, :], hT[:, kc, o:o + m], sv_[:, kc, :], kc == 0, kc == 15, r=[hT, sv_], w=[pbv])
            P.op(DVE, lambda e, m=m: e.memset(stat[0:m, 0:8], 0.0), w=[stat])
            P.op(ACT, lambda e, m=m, pbu=pbu: e.activation(u_sb[0:m, :], pbu[0:m, :], AF.Gelu), r=[pbu], w=[u_sb])
            P.op(ACT, lambda e, m=m, pbv=pbv: e.activation(v_sb[0:m, :], pbv[0:m, :], AF.Gelu, accum_out=stat[0:m, 0:1]),
                 r=[pbv, stat], w=[v_sb, stat])
            P.op(ACT, lambda e, m=m: e.activation(t_sb[0:m, :], v_sb[0:m, :], AF.Square, accum_out=stat[0:m, 1:2]),
                 r=[v_sb, stat], w=[t_sb, stat])
            P.op(DVE, lambda e, m=m: e.tensor_scalar(stat[0:m, 2:4], stat[0:m, 0:2], 1.0 / G, None, ALU.mult), r=[stat], w=[stat])
            P.op(DVE, lambda e, m=m: e.tensor_tensor(stat[0:m, 4:5], stat[0:m, 2:3], stat[0:m, 2:3], ALU.mult), r=[stat], w=[stat])
            P.op(DVE, lambda e, m=m: e.tensor_tensor(stat[0:m, 5:6], stat[0:m, 3:4], stat[0:m, 4:5], ALU.subtract), r=[stat], w=[stat])
            P.op(DVE, lambda e, m=m: e.tensor_scalar(stat[0:m, 5:6], stat[0:m, 5:6], 1e-5, None, ALU.add), r=[stat], w=[stat])
            P.op(ACT, lambda e, m=m: e.activation(stat[0:m, 6:7], stat[0:m, 5:6], AF.Sqrt), r=[stat], w=[stat])
            P.op(DVE, lambda e, m=m: e.reciprocal(stat[0:m, 7:8], stat[0:m, 6:7]), r=[stat], w=[stat])
            P.op(DVE, lambda e, m=m: e.tensor_scalar(v_sb[0:m, :], v_sb[0:m, :], stat[0:m, 2:3], stat[0:m, 7:8], ALU.subtract, ALU.mult),
                 r=[v_sb, stat], w=[v_sb])
            P.op(DVE, lambda e, m=m: e.tensor_tensor(v_sb[0:m, :], v_sb[0:m, :], sgu_bc[0:m, 0, :], ALU.mult), r=[v_sb, sgu_bc], w=[v_sb])
            P.op(DVE, lambda e, m=m: e.tensor_tensor(v_sb[0:m, :], v_sb[0:m, :], sgu_bc[0:m, 1, :], ALU.add), r=[v_sb, sgu_bc], w=[v_sb])
            if kind == "s":
                P.dma(sguv_s.t[l].rearrange("s t c -> (s t) c"), v_sb[0:m, :], r=[v_sb], w=[sguv_s], out=True)
            P.op(ACT, lambda e, m=m: e.copy(vnb[0:m, :], v_sb[0:m, :]), r=[v_sb], w=[vnb])
            for h in range(4):
                lh = wmT[:, h, :] if kind == "p" else bd4[:, h, :]
                P.mm(px[0:m, h * 128:(h + 1) * 128], lh, vnb[0:m, h * 128:(h + 1) * 128], True, True,
                     r=[wmT if kind == "p" else bd4, vnb], w=[px])
            sbc = (sgub_p if kind == "p" else sgub_s)
            sb_ap = sbc[0:m, 0:4].unsqueeze(2).to_broadcast([m, 4, 128])
            P.op(DVE, lambda e, m=m, sb_ap=sb_ap: e.tensor_tensor(t_sb[0:m, :].rearrange("p (a b) -> p a b", a=4),
                                                                 px[0:m, :].rearrange("p (a b) -> p a b", a=4), sb_ap, ALU.add),
                 r=[px, sbc], w=[t_sb])
            P.op(DVE, lambda e, m=m: e.tensor_tensor(u_sb[0:m, :], u_sb[0:m, :], t_sb[0:m, :], ALU.mult), r=[u_sb, t_sb], w=[u_sb])
            rs = rms_rstd(u_sb, u_sb[0:m, :], m, G, 0, t_sb, t_sb[0:m, :], 1.0 / G, epsn)
            P.op(DVE, lambda e, m=m, rs=rs: e.scalar_tensor_tensor(ynb[0:m, :], u_sb[0:m, :], rs, sgu_bc[0:m, 2, :], ALU.mult, ALU.mult),
                 r=[u_sb, stat, sgu_bc], w=[ynb])
            to_featmajor(ynb, lambda kc, m=m: ynb[0:m, kc * 128:(kc + 1) * 128], m, o, yT, nkc=4, kc0=4)

    hS = P.sb("hS", [128, 4, 128])
    hSb = P.sb("hSb", [128, 4, 128], BF16)

    def mixer_hgrn(l, kind, tok0, NT):
        C = 64 if kind == "p" else LS
        nch = NT // C
        rst = CF("rst64") if kind == "p" else CF("rst4")
        qf = [av("h_q%d" % h, 128, NT) for h in range(4)]
        gate = [av("h_g%d" % h, 128, NT) for h in range(4)]
        qt = [av("h_qt%d" % h, 128, NT, BF16) for h in range(4)]
        kt = [av("h_kt%d" % h, 128, NT, BF16) for h in range(4)]
        kh = [av("h_kh%d" % h, 128, NT, BF16) for h in range(4)]
        vb = [av("h_vb%d" % h, 128, NT, BF16) for h in range(4)]
        egC = [av("h_egC%d" % h, 128, 16) for h in range(4)]
        T = [av("h_T%d" % i, 128, NT) for i in range(4)]
        Osb = av("h_O", 128, 4 * NT)
        TMK = [av("h_TMK%d" % i, 64, 512, BF16) for i in range(2)]
        TMV = [av("h_TMV%d" % i, 64, 512, BF16) for i in range(2)]
        ATs = [av("h_ATs%d" % i, 64, 4 * C, BF16) for i in range(2)]
        osq = av("h_osq", 128, NT, BF16)
        lbc, omc, hnc = PC["lb"], PC["oml"], PC["hgrn_norm"]
        c3 = lambda ap: ap.rearrange("p (n c) -> p n c", c=C)

        def consume(ci, pb, rows):
            grp, h = ci // 4, ci % 4
            if grp == 0:
                P.op(ACT, lambda e: e.activation(qf[h][:, :], pb[:, 0:NT], AF.Silu), r=[pb], w=[qf[h]])
            elif grp == 1:
                t0, t1, t2 = T[0], T[1], T[2]
                P.op(ACT, lambda e: e.activation(t0[:, :], pb[:, 0:NT], AF.Sigmoid), r=[pb], w=[t0])
                P.op(DVE, lambda e: e.tensor_scalar(t0[:, :], t0[:, :], pcol[:, omc + h:omc + h + 1], pcol[:, lbc + h:lbc + h + 1],
                                                    ALU.mult, ALU.add), r=[t0, pcol], w=[t0])
                P.op(ACT, lambda e: e.activation(t1[:, :], t0[:, :], AF.Ln), r=[t0], w=[t1])
                P.op(DVE, lambda e: e.tensor_scalar(t0[:, :], t0[:, :], -1.0, 1.0, ALU.mult, ALU.add), r=[t0], w=[t0])
                P.op(DVE, lambda e: e.tensor_tensor_scan(t2[:, :], rst[:, 0:NT], t1[:, :], 0.0, ALU.mult, ALU.add),
                     r=[cst, t1], w=[t2])
                P.op(ACT, lambda e: e.activation(t1[:, :], t2[:, :], AF.Exp), r=[t2], w=[t1])
                P.op(DVE, lambda e: e.tensor_tensor(qt[h][:, :], qf[h][:, :], t1[:, :], ALU.mult), r=[qf[h], t1], w=[qt[h]])
                P.op(ACT, lambda e: e.activation(t1[:, :], t2[:, :], AF.Exp, scale=-1.0), r=[t2], w=[t1])
                P.op(DVE, lambda e: e.tensor_tensor(kt[h][:, :], t0[:, :], t1[:, :], ALU.mult), r=[t0, t1], w=[kt[h]])
                cum3 = c3(t2[:, :])
                P.op(ACT, lambda e: e.activation(egC[h][:, 0:nch], cum3[:, :, C - 1], AF.Exp), r=[t2], w=[egC[h]])
                P.op(DVE, lambda e: e.tensor_tensor(c3(t1[:, :]), cum3[:, :, C - 1:C].to_broadcast([128, nch, C]), cum3, ALU.subtract),
                     r=[t2], w=[t1])
                P.op(ACT, lambda e: e.activation(t1[:, :], t1[:, :], AF.Exp), r=[t1], w=[t1])
                P.op(DVE, lambda e: e.tensor_tensor(kh[h][:, :], t0[:, :], t1[:, :], ALU.mult), r=[t0, t1], w=[kh[h]])
            elif grp == 2:
                P.op(ACT, lambda e: e.copy(vb[h][:, :], pb[:, 0:NT]), r=[pb], w=[vb[h]])
            else:
                P.op(ACT, lambda e: e.activation(gate[h][:, :], pb[:, 0:NT], AF.Silu), r=[pb], w=[gate[h]])
        inproj_fm(l, C_H, 4 * G, NT, consume)

        for c in range(nch):
            cs = slice(c * C, (c + 1) * C)
            tk, tv, at = TMK[c % 2], TMV[c % 2], ATs[c % 2]
            if kind == "s":
                P.dma(hS[:, :, :], st_hgrn.t[l, c].rearrange("h k v -> k h v"), r=[st_hgrn], w=[hS])
                P.op(ACT, lambda e: e.copy(hSb[:, :, :], hS[:, :, :]), r=[hS], w=[hSb])
            elif tok0 == 0 and c == 0:
                P.op(DVE, lambda e: e.memset(hS[:, :, :], 0.0), w=[hS])
                P.op(ACT, lambda e: e.copy(hSb[:, :, :], hS[:, :, :]), r=[hS], w=[hSb])
            ptk = next_ptr()
            for h in range(4):
                P.tr(ptk[0:C, h * 128:(h + 1) * 128], kh[h][:, cs], CB("ident"), r=[kh[h], cstb], w=[ptk])
            P.op(DVE, lambda e, ptk=ptk, tk=tk: e.tensor_copy(tk[0:C, :], ptk[0:C, :]), r=[ptk], w=[tk])
            ptv = next_ptr()
            for h in range(4):
                P.tr(ptv[0:C, h * 128:(h + 1) * 128], vb[h][:, cs], CB("ident"), r=[vb[h], cstb], w=[ptv])
            P.op(ACT, lambda e, ptv=ptv, tv=tv: e.copy(tv[0:C, :], ptv[0:C, :]), r=[ptv], w=[tv])
            for h in range(4):
                P.mm(px[0:C, h * C:(h + 1) * C], kt[h][:, cs], qt[h][:, cs], True, True, r=[kt[h], qt[h]], w=[px])
            iu = CF("IU", C, 0, C)
            P.op(DVE, lambda e, at=at, iu=iu: e.tensor_tensor(at[0:C, :].rearrange("p (h c) -> p h c", h=4),
                                                              px[0:C, 0:4 * C].rearrange("p (h c) -> p h c", h=4),
                                                              iu.unsqueeze(1).to_broadcast([C, 4, C]), ALU.mult), r=[px, cst], w=[at])
            for h in range(4):
                P.mm(pg[:, h * C:(h + 1) * C], tv[0:C, h * 128:(h + 1) * 128], at[0:C, h * C:(h + 1) * C], True, False,
                     r=[tv, at], w=[pg])
                P.mm(pg[:, h * C:(h + 1) * C], hSb[:, h, :], qt[h][:, cs], False, True, r=[hSb, qt[h]], w=[pg])
            P.op(ACT, lambda e, cs=cs: e.copy(Osb[:, :].rearrange("p (h t) -> p h t", h=4)[:, :, cs],
                                              pg[:, 0:4 * C].rearrange("p (h c) -> p h c", h=4)), r=[pg], w=[Osb])
            for h in range(4):
                P.mm(pu[:, h * 128:(h + 1) * 128], tk[0:C, h * 128:(h + 1) * 128], tv[0:C, h * 128:(h + 1) * 128], True, True,
                     r=[tk, tv], w=[pu])
            for h in range(4):
                P.op(DVE, lambda e, h=h, c=c: e.scalar_tensor_tensor(hS[:, h, :], hS[:, h, :], egC[h][:, c:c + 1],
                                                                    pu[:, h * 128:(h + 1) * 128], ALU.mult, ALU.add),
                     r=[hS, egC[h], pu], w=[hS])
            P.op(ACT, lambda e: e.copy(hSb[:, :, :], hS[:, :, :]), r=[hS], w=[hSb])
            if kind == "s":
                P.dma(hgrn_s.t[l, c].rearrange("h k v -> k h v"), hS[:, :, :], r=[hS], w=[hgrn_s], out=True)
            elif tok0 + NT == SEQ and c == nch - 1:
                P.dma(hgrn_p.t[l].rearrange("h k v -> k h v"), hS[:, :, :], r=[hS], w=[hgrn_p], out=True)
        for h in range(4):
            o_ap = Osb[:, h * NT:(h + 1) * NT]
            P.op(DVE, lambda e, o_ap=o_ap: e.tensor_tensor(osq[:, :], o_ap, o_ap, ALU.mult), r=[Osb], w=[osq])
            pb = next_pbig()
            P.mm(pb[:, 0:NT], CB("ones"), osq[:, :], True, True, r=[cstb, osq], w=[pb])
            t0 = T[0]
            P.op(ACT, lambda e, pb=pb, t0=t0: e.activation(t0[:, :], pb[:, 0:NT], AF.Sqrt, bias=epsn[:, 0:1], scale=1.0 / 128),
                 r=[pb, epsn], w=[t0])
            P.op(DVE, lambda e, t0=t0: e.reciprocal(t0[:, :], t0[:, :]), r=[t0], w=[t0])
            P.op(DVE, lambda e, t0=t0, o_ap=o_ap: e.tensor_tensor(t0[:, :], t0[:, :], o_ap, ALU.mult), r=[t0, Osb], w=[t0])
            P.op(DVE, lambda e, t0=t0, h=h: e.scalar_tensor_tensor(yT[:, 8 + h, 0:NT], t0[:, :], pcol[:, hnc + h:hnc + h + 1],
                                                                   gate[h][:, :], ALU.mult, ALU.mult), r=[t0, pcol, gate[h]], w=[yT])

    rS = P.sb("rS", [64, 8, 64])
    rSb = P.sb("rSb", [64, 8, 64], BF16)
    shcar = P.sb("shcar", [128, 16])

    def mixer_rwkv(l, kind, tok0, NT):
        C = 64 if kind == "p" else LS
        nch = NT // C
        nseq, L = (1, NT) if kind == "p" else (NSEQ_S, LS)
        E = L + 1
        C2 = 2 * C
        rst = CF("rst64") if kind == "p" else CF("rst4")
        last_blk = (kind == "p" and tok0 + NT == SEQ)
        c3 = lambda ap: ap.rearrange("p (n c) -> p n c", c=C)
        AR = [av("r_AR%d" % j, 128, 2 * NT, BF16) for j in range(4)]
        BK = [av("r_BK%d" % j, 128, 2 * NT, BF16) for j in range(4)]
        Bh = [av("r_Bh%d" % j, 128, NT, BF16) for j in range(4)]
        Kh = [av("r_Kh%d" % j, 128, NT, BF16) for j in range(4)]
        Vb = [av("r_Vb%d" % j, 128, NT, BF16) for j in range(4)]
        gsb = [av("r_g%d" % j, 128, NT) for j in range(4)]
        bon = [av("r_bon%d" % j, 128, NT) for j in range(4)]
        gC = [av("r_gC%d" % j, 128, 16) for j in range(4)]
        YN = av("r_YN", 128, 4 * NT, BF16)
        mark = aoff[0]
        lw = av("r_lw", 32, G, BF16)
        la = av("r_la", 32, G, BF16)
        lg = av("r_lg", 96, G, BF16)
        txw = av("r_txw", 32, NT, BF16)
        xab = av("r_xab", 32, NT, BF16)
        sgb = av("r_sgb", 96, NT, BF16)
        Zl = av("r_Zl", 128, nseq * E)
        Zr = av("r_Zr", 128, nseq * E)
        Zk = av("r_Zk", 128, nseq * E)
        Zv = av("r_Zv", 128, nseq * E)
        T = [av("r_T%d" % i, 128, NT) for i in range(5)]
        sqb = av("r_sqb", 128, NT, BF16)
        P.dma(lw[:, :], W["rwkv_w_lora"].t[l], r=[W["rwkv_w_lora"]], w=[lw], eng=POOL)
        P.dma(la[:, :], W["rwkv_a_lora"].t[l], r=[W["rwkv_a_lora"]], w=[la], eng=POOL)
        P.dma(lg[:, :], W["rwkv_g_lora"].t[l], r=[W["rwkv_g_lora"]], w=[lg], eng=POOL)
        mu0 = PC["mu"]

        def zl(Z, rows):
            return Z[0:rows, :].rearrange("p (s e) -> p s e", s=nseq)

        def zc(Z, rows=128):
            if kind == "p":
                return Z[0:rows, 1:E].rearrange("p (n c) -> p n c", c=C)
            return zl(Z, rows)[:, :, 1:E]

        def lerp(Z, rows, ci, c0, pb):
            z3 = zl(Z, rows)
            if kind == "p":
                if tok0 == 0:
                    P.op(DVE, lambda e: e.memset(z3[:, :, 0:1], 0.0), w=[Z])
                else:
                    P.op(DVE, lambda e: e.tensor_copy(z3[:, 0, 0:1], shcar[0:rows, ci:ci + 1]), r=[shcar], w=[Z])
            else:
                P.dma(z3[:, :, 0], st_shift.t[l, :, c0:c0 + rows].rearrange("s c -> c s"), r=[st_shift], w=[Z],
                      allow_slow_non_contiguous=True)
            P.op(ACT, lambda e: e.copy(z3[:, :, 1:E], pb[0:rows, 0:NT].rearrange("p (s t) -> p s t", s=nseq)), r=[pb], w=[Z])
            if kind == "p":
                if not last_blk:
                    P.op(ACT, lambda e: e.copy(shcar[0:rows, ci:ci + 1], z3[:, 0, L:E]), r=[Z], w=[shcar])
                else:
                    P.dma(shift_p.t[l, c0:c0 + rows].rearrange("(c o) -> c o", o=1), z3[:, 0, L:E], r=[Z], w=[shift_p], out=True,
                          allow_slow_non_contiguous=True)
            else:
                P.dma(shift_s.t[l, :, c0:c0 + rows].rearrange("s c -> c s"), z3[:, :, L], r=[Z], w=[shift_s], out=True,
                      allow_slow_non_contiguous=True)
            t4 = T[4][0:rows, :].rearrange("p (s t) -> p s t", s=nseq)
            mu = pcol[0:rows, mu0 + ci:mu0 + ci + 1]
            P.op(DVE, lambda e: e.tensor_tensor(t4, z3[:, :, 0:L], z3[:, :, 1:E], ALU.subtract), r=[Z], w=[T[4]])
            P.op(DVE, lambda e: e.scalar_tensor_tensor(z3[:, :, 1:E], t4, mu, z3[:, :, 1:E], ALU.mult, ALU.add),
                 r=[T[4], pcol, Z], w=[Z])

        sl_ = load_w(W["w_in"], l, 0, 16, 1536, 160)
        for (q0, rows, ci, func, dst) in ((0, 32, 12, AF.Tanh, txw), (32, 32, 13, AF.Copy, xab), (64, 96, 14, AF.Sigmoid, sgb)):
            pb = next_pbig()
            for kc in range(16):
                P.mm(pb[0:rows, 0:NT], sl_[:, kc, q0:q0 + rows], hT[:, kc, 0:NT], kc == 0, kc == 15, r=[sl_, hT], w=[pb])
            lerp(Zl, rows, ci, 1536 + q0, pb)
            zcv = zc(Zl, rows)
            P.op(ACT, lambda e, func=func, dst=dst, zcv=zcv, rows=rows: e.activation(c3(dst[0:rows, :]), zcv, func), r=[Zl], w=[dst])
        sr = load_w(W["w_in"], l, 0, 16, 0, 512)
        sk = load_w(W["w_in"], l, 0, 16, 512, 512)
        sv = load_w(W["w_in"], l, 0, 16, 1024, 512)
        bo_b = CB("blockones")
        for j in range(4):
            js = slice(j * 128, (j + 1) * 128)
            for (Z, s_, ci) in ((Zr, sr, j), (Zk, sk, 4 + j), (Zv, sv, 8 + j)):
                pb = next_pbig()
                for kc in range(16):
                    P.mm(pb[:, 0:NT], s_[:, kc, js], hT[:, kc, 0:NT], kc == 0, kc == 15, r=[s_, hT], w=[pb])
                lerp(Z, 128, ci, ci * 128, pb)
            rm, km, vm = zc(Zr), zc(Zk), zc(Zv)
            T0, T1, T2, T3, T4 = T
            col = lambda nm: pcol[:, PC[nm] + j:PC[nm] + j + 1]
            P.mm(px[:, 0:NT], lw[:, js], txw[:, :], True, True, r=[lw, txw], w=[px])
            P.op(ACT, lambda e, b=col("w0"): e.activation(T0[:, :], px[:, 0:NT], AF.Sigmoid, bias=b), r=[px, pcol], w=[T0])
            P.op(DVE, lambda e: e.tensor_scalar(T0[:, :], T0[:, :], NEG_E05, None, ALU.mult), r=[T0], w=[T0])
            P.op(DVE, lambda e: e.tensor_tensor_scan(T1[:, :], rst[:, 0:NT], T0[:, :], 0.0, ALU.mult, ALU.add), r=[cst, T0], w=[T1])
            P.op(ACT, lambda e, j=j: e.activation(gC[j][:, 0:nch], c3(T1[:, :])[:, :, C - 1], AF.Exp), r=[T1], w=[gC[j]])
            P.mm(px[:, 0:NT], la[:, js], xab[:, :], True, True, r=[la, xab], w=[px])
            P.op(ACT, lambda e, b=col("a0"): e.activation(T2[:, :], px[:, 0:NT], AF.Sigmoid, bias=b), r=[px, pcol], w=[T2])
            P.mm(px[:, 0:NT], lg[:, js], sgb[:, :], True, True, r=[lg, sgb], w=[px])
            P.op(ACT, lambda e, j=j: e.copy(gsb[j][:, :], px[:, 0:NT]), r=[px], w=[gsb[j]])
            P.op(DVE, lambda e, km=km, s_=col("k_k"): e.tensor_scalar(c3(T3[:, :]), km, s_, None, ALU.mult), r=[Zk, pcol], w=[T3])
            P.op(DVE, lambda e: e.tensor_tensor(sqb[:, :], T3[:, :], T3[:, :], ALU.mult), r=[T3], w=[sqb])
            P.mm(px[:, 0:NT], bo_b, sqb[:, :], True, True, r=[cstb, sqb], w=[px])
            P.op(ACT, lambda e: e.activation(T4[:, :], px[:, 0:NT], AF.Sqrt), r=[px], w=[T4])
            P.op(DVE, lambda e: e.tensor_scalar(T4[:, :], T4[:, :], 1e-12, None, ALU.max), r=[T4], w=[T4])
            P.op(DVE, lambda e: e.reciprocal(T4[:, :], T4[:, :]), r=[T4], w=[T4])
            P.op(DVE, lambda e: e.tensor_tensor(T3[:, :], T3[:, :], T4[:, :], ALU.mult), r=[T3, T4], w=[T3])
            P.op(DVE, lambda e, s1=col("k_a"), s2=col("omk"): e.tensor_scalar(T4[:, :], T2[:, :], s1, s2, ALU.mult, ALU.add),
                 r=[T2, pcol], w=[T4])
            P.op(DVE, lambda e, km=km: e.tensor_tensor(km, km, c3(T4[:, :]), ALU.mult), r=[Zk, T4], w=[Zk])
            P.op(DVE, lambda e: e.tensor_tensor(T4[:, :], T3[:, :], T2[:, :], ALU.mult), r=[T3, T2], w=[T4])
            P.op(DVE, lambda e, rm=rm, km=km: e.tensor_tensor(c3(T2[:, :]), rm, km, ALU.mult), r=[Zr, Zk], w=[T2])
            P.op(DVE, lambda e, s_=col("r_k"): e.tensor_scalar(sqb[:, :], T2[:, :], s_, None, ALU.mult), r=[T2, pcol], w=[sqb])
            P.mm(px[:, 0:NT], bo_b, sqb[:, :], True, True, r=[cstb, sqb], w=[px])
            P.op(DVE, lambda e, j=j, vm=vm: e.tensor_tensor(c3(bon[j][:, :]), c3(px[:, 0:NT]), vm, ALU.mult), r=[px, Zv], w=[bon[j]])
            P.op(ACT, lambda e, j=j, vm=vm: e.copy(c3(Vb[j][:, :]), vm), r=[Zv], w=[Vb[j]])
            AR4 = AR[j][:, :].rearrange("p (n two c) -> p n two c", two=2, c=C)
            BK4 = BK[j][:, :].rearrange("p (n two c) -> p n two c", two=2, c=C)
            P.op(ACT, lambda e: e.activation(T2[:, :], T1[:, :], AF.Exp), r=[T1], w=[T2])
            P.op(DVE, lambda e, rm=rm, o=AR4[:, :, 1, :]: e.tensor_tensor(o, rm, c3(T2[:, :]), ALU.mult), r=[Zr, T2], w=[AR[j]])
            P.op(ACT, lambda e: e.activation(T2[:, :], T1[:, :], AF.Exp, scale=-1.0), r=[T1], w=[T2])
            P.op(DVE, lambda e, o=BK4[:, :, 0, :]: e.tensor_tensor(o, c3(T4[:, :]), c3(T2[:, :]), ALU.mult), r=[T4, T2], w=[BK[j]])
            P.op(DVE, lambda e, km=km, o=BK4[:, :, 1, :]: e.tensor_tensor(o, km, c3(T2[:, :]), ALU.mult), r=[Zk, T2], w=[BK[j]])
            P.op(DVE, lambda e: e.tensor_tensor(T2[:, :], T1[:, :], T0[:, :], ALU.subtract), r=[T1, T0], w=[T2])
            P.op(ACT, lambda e: e.activation(T2[:, :], T2[:, :], AF.Exp), r=[T2], w=[T2])
            P.op(DVE, lambda e, o=AR4[:, :, 0, :]: e.scalar_tensor_tensor(o, c3(T3[:, :]), -1.0, c3(T2[:, :]), ALU.mult, ALU.mult),
                 r=[T3, T2], w=[AR[j]])
            cum3 = c3(T1[:, :])
            P.op(DVE, lambda e, cum3=cum3: e.tensor_tensor(c3(T2[:, :]), cum3[:, :, C - 1:C].to_broadcast([128, nch, C]), cum3, ALU.subtract),
                 r=[T1], w=[T2])
            P.op(ACT, lambda e: e.activation(T2[:, :], T2[:, :], AF.Exp), r=[T2], w=[T2])
            P.op(DVE, lambda e, j=j: e.tensor_tensor(Bh[j][:, :], T4[:, :], T2[:, :], ALU.mult), r=[T4, T2], w=[Bh[j]])
            P.op(DVE, lambda e, j=j, km=km: e.tensor_tensor(c3(Kh[j][:, :]), km, c3(T2[:, :]), ALU.mult), r=[Zk, T2], w=[Kh[j]])

        if _RW_LEVEL < 2:
            for j in range(4):
                P.op(DVE, lambda e, j=j: e.tensor_scalar(yT[:, j, 0:NT], hT[:, j, 0:NT], 0.0, None, ALU.mult), r=[hT], w=[yT])
            return
        arena_fence()
        aoff[0] = mark
        TMB = [av("r_TMB%d" % i, 64, 512, BF16) for i in range(2)]
        TMK = [av("r_TMK%d" % i, 64, 512, BF16) for i in range(2)]
        TMV = [av("r_TMV%d" % i, 64, 512, BF16) for i in range(2)]
        A1s = av("r_A1s", 64, 8 * C2, BF16)
        A2s = av("r_A2s", 64, 8 * C2, BF16)
        NTs = av("r_NTs", 64, 8 * C, BF16)
        Xb = [av("r_X%d" % i, 64, 8 * C, BF16) for i in range(2)]
        XTb = [av("r_XT%d" % i, 64, 8 * C, BF16) for i in range(2)]
        Tm = av("r_Tm", 64, 8 * C, BF16)
        TTm = av("r_TTm", 64, 8 * C, BF16)
        XtS = av("r_XtS", 64, 512, BF16)
        UtS = av("r_UtS", 64, 512, BF16)
        ysb = av("r_ysb", 64, 512)
        ynb = av("r_ynb", 64, 512, BF16)
        gst = av("r_gst", 64, 32)
        Sld = av("r_Sld", 64, 512)
        Sout = Sld
        fin = av("r_fin", 128, 512)
        ysq = fin
        su_m = CF("SU", C, 0, C).unsqueeze(1)
        iu_m = CF("IU", C, 0, C).unsqueeze(1)
        sl_m = CF("SL", C, 0, C).unsqueeze(1)
        id_m = CF("ident", C, 0, C).unsqueeze(1)
        idb = CB("ident")
        nlev = {64: 5, 4: 1}[C] if _RW_LEVEL >= 3 else 0
        hv = lambda ap, w_: ap.rearrange("p (h c) -> p h c", c=w_)
        gCo = av("r_gCo", 64, 64)
        ARo = [hT[0:64, 2 * j:2 * j + 2, :].rearrange("p a b -> p (a b)") for j in range(4)]
        BKo = [hT[0:64, 8 + 2 * j:10 + 2 * j, :].rearrange("p a b -> p (a b)") for j in range(4)]
        for j in range(4):
            P.dma(ARo[j][:, 0:2 * NT], AR[j][64:128, :], r=[AR[j]], w=[hT])
            P.dma(BKo[j][:, 0:2 * NT], BK[j][64:128, :], r=[BK[j]], w=[hT])
            P.dma(gCo[:, j * 16:j * 16 + nch], gC[j][64:128, 0:nch], r=[gC[j]], w=[gCo])

        def opA(j, e_, c0, c1):
            return (AR[j][0:64, c0:c1], AR[j]) if e_ == 0 else (ARo[j][:, c0:c1], hT)

        def opB(j, e_, c0, c1):
            return (BK[j][0:64, c0:c1], BK[j]) if e_ == 0 else (BKo[j][:, c0:c1], hT)

        def decay(j, e_, c):
            return (gC[j][0:64, c:c + 1], gC[j]) if e_ == 0 else (gCo[:, j * 16 + c:j * 16 + c + 1], gCo)

        for c in range(nch):
            cs = slice(c * C, (c + 1) * C)
            tb, tk, tv = TMB[c % 2], TMK[c % 2], TMV[c % 2]
            if kind == "s":
                P.dma(Sld[:, :].rearrange("p (h k) -> p h k", h=8), st_wkv.t[l, c].rearrange("h v k -> v h k"), r=[st_wkv], w=[Sld])
                for h in range(8):
                    P.tr(px[0:64, h * 64:(h + 1) * 64], Sld[:, h * 64:(h + 1) * 64], CF("ident", 64, 0, 64), r=[Sld, cst], w=[px])
                P.op(DVE, lambda e: e.tensor_copy(rS[:, :, :], hv(px[0:64, :], 64)), r=[px], w=[rS])
                P.op(ACT, lambda e: e.copy(rSb[:, :, :], rS[:, :, :]), r=[rS], w=[rSb])
            elif tok0 == 0 and c == 0:
                P.op(DVE, lambda e: e.memset(rS[:, :, :], 0.0), w=[rS])
                P.op(ACT, lambda e: e.copy(rSb[:, :, :], rS[:, :, :]), r=[rS], w=[rSb])
            for (srcs, dst, eng) in ((Bh, tb, DVE), (Kh, tk, ACT), (Vb, tv, DVE)):
                pt = next_ptr()
                for j in range(4):
                    P.tr(pt[0:C, j * 128:(j + 1) * 128], srcs[j][:, cs], idb, r=[srcs[j], cstb], w=[pt])
                if eng == DVE:
                    P.op(DVE, lambda e, pt=pt, dst=dst: e.tensor_copy(dst[0:C, :], pt[0:C, :]), r=[pt], w=[dst])
                else:
                    P.op(ACT, lambda e, pt=pt, dst=dst: e.copy(dst[0:C, :], pt[0:C, :]), r=[pt], w=[dst])
            if _RW_LEVEL < 2.1:
                continue
            pA1 = [pbig[0], pbig[1]]
            pA2 = [pbig[2], pg]
            for h in range(8):
                j, e_ = h // 2, h % 2
                ar, arR = opA(j, e_, c * C2, (c + 1) * C2)
                bt, bkR = opB(j, e_, c * C2, c * C2 + C)
                kt_, _ = opB(j, e_, c * C2 + C, (c + 1) * C2)
                at_, _ = opA(j, e_, c * C2, c * C2 + C)
                hh = h % 4
                P.mm(pA1[h // 4][0:C, hh * C2:(hh + 1) * C2], bt, ar, True, True, r=[bkR, arR], w=[pA1[h // 4]])
                P.mm(pA2[h // 4][0:C, hh * C2:(hh + 1) * C2], kt_, ar, True, True, r=[bkR, arR], w=[pA2[h // 4]])
                P.mm(pu[0:C, h * C:(h + 1) * C], at_, bt, True, True, r=[arR, bkR], w=[pu])
            if _RW_LEVEL < 2.12:
                continue
            for (ps2, dsts) in ((pA1, A1s), (pA2, A2s)):
                for half in range(2):
                    src3 = hv(ps2[half][0:C, 0:4 * C2], C2)
                    dst3 = hv(dsts[0:C, half * 4 * C2:(half + 1) * 4 * C2], C2)
                    P.op(DVE, lambda e, src3=src3, dst3=dst3: e.tensor_tensor(dst3[:, :, 0:C], src3[:, :, 0:C],
                                                                             su_m.to_broadcast([C, 4, C]), ALU.mult),
                         r=[ps2[half], cst], w=[dsts])
                    P.op(DVE, lambda e, src3=src3, dst3=dst3: e.tensor_tensor(dst3[:, :, C:C2], src3[:, :, C:C2],
                                                                             iu_m.to_broadcast([C, 4, C]), ALU.mult),
                         r=[ps2[half], cst], w=[dsts])
            P.op(DVE, lambda e: e.tensor_tensor(hv(NTs[0:C, :], C), hv(pu[0:C, 0:8 * C], C), sl_m.to_broadcast([C, 8, C]), ALU.mult),
                 r=[pu, cst], w=[NTs])
            if _RW_LEVEL < 2.2:
                continue
            A1v = hv(A1s[0:C, :], C2)
            P.op(DVE, lambda e: e.tensor_tensor(hv(Tm[0:C, :], C), A1v[:, :, 0:C], id_m.to_broadcast([C, 8, C]), ALU.add),
                 r=[A1s, cst], w=[Tm])
            P.op(DVE, lambda e: e.tensor_tensor(hv(TTm[0:C, :], C), hv(NTs[0:C, :], C), id_m.to_broadcast([C, 8, C]), ALU.add),
                 r=[NTs, cst], w=[TTm])
            Xc = (A1s, lambda h: A1s[0:C, h * C2:h * C2 + C])
            XTc = (NTs, lambda h: NTs[0:C, h * C:(h + 1) * C])
            for lev in range(nlev):
                Xn, XTn = Xb[lev % 2], XTb[lev % 2]
                lastl = (lev == nlev - 1)
                for h in range(8):
                    P.mm(px[0:C, h * C:(h + 1) * C], XTc[1](h), Xc[1](h), True, True, r=[XTc[0], Xc[0]], w=[px])
                P.op(ACT, lambda e, Xn=Xn: e.copy(Xn[0:C, :], px[0:C, 0:8 * C]), r=[px], w=[Xn])
                if not lastl:
                    for h in range(8):
                        P.mm(pu[0:C, h * C:(h + 1) * C], Xc[1](h), XTc[1](h), True, True, r=[XTc[0], Xc[0]], w=[pu])
                    P.op(DVE, lambda e, XTn=XTn: e.tensor_copy(XTn[0:C, :], pu[0:C, 0:8 * C]), r=[pu], w=[XTn])
                for h in range(8):
                    P.mm(pg[0:C, h * C:(h + 1) * C], TTm[0:C, h * C:(h + 1) * C], Xn[0:C, h * C:(h + 1) * C], True, True,
                         r=[TTm, Xn], w=[pg])
                if not lastl:
                    pq = pbig[lev % 3]
                    for h in range(8):
                        P.mm(pq[0:C, h * C:(h + 1) * C], Xn[0:C, h * C:(h + 1) * C], TTm[0:C, h * C:(h + 1) * C], True, True,
                             r=[TTm, Xn], w=[pq])
                P.op(DVE, lambda e: e.tensor_tensor(Tm[0:C, :], Tm[0:C, :], pg[0:C, 0:8 * C], ALU.add), r=[Tm, pg], w=[Tm])
                if not lastl:
                    P.op(DVE, lambda e, pq=pq: e.tensor_tensor(TTm[0:C, :], TTm[0:C, :], pq[0:C, 0:8 * C], ALU.add), r=[TTm, pq], w=[TTm])
                    Xc = (Xn, lambda h, Xn=Xn: Xn[0:C, h * C:(h + 1) * C])
                    XTc = (XTn, lambda h, XTn=XTn: XTn[0:C, h * C:(h + 1) * C])
            if _RW_LEVEL < 2.3:
                continue
            for h in range(8):
                j, e_ = h // 2, h % 2
                at_, arR = opA(j, e_, c * C2, c * C2 + C)
                P.mm(px[0:C, h * 64:(h + 1) * 64], at_, rSb[:, h, :], True, False, r=[arR, rSb], w=[px])
                P.mm(px[0:C, h * 64:(h + 1) * 64], A2s[0:C, h * C2:h * C2 + C], tv[0:C, h * 64:(h + 1) * 64], False, True,
                     r=[A2s, tv], w=[px])
            P.op(ACT, lambda e: e.copy(XtS[0:C, :], px[0:C, :]), r=[px], w=[XtS])
            for h in range(8):
                P.mm(pu[0:C, h * 64:(h + 1) * 64], Tm[0:C, h * C:(h + 1) * C], XtS[0:C, h * 64:(h + 1) * 64], True, True,
                     r=[Tm, XtS], w=[pu])
            P.op(DVE, lambda e: e.tensor_copy(UtS[0:C, :], pu[0:C, :]), r=[pu], w=[UtS])
            if _RW_LEVEL < 2.4:
                continue
            for h in range(8):
                j, e_ = h // 2, h % 2
                rt_, arR = opA(j, e_, c * C2 + C, (c + 1) * C2)
                o_ = pg[0:C, h * 64:(h + 1) * 64]
                P.mm(o_, rt_, rSb[:, h, :], True, False, r=[arR, rSb], w=[pg])
                P.mm(o_, A1s[0:C, h * C2 + C:(h + 1) * C2], UtS[0:C, h * 64:(h + 1) * 64], False, False, r=[A1s, UtS], w=[pg])
                P.mm(o_, A2s[0:C, h * C2 + C:(h + 1) * C2], tv[0:C, h * 64:(h + 1) * 64], False, True, r=[A2s, tv], w=[pg])
            P.op(ACT, lambda e: e.copy(ysb[0:C, :], pg[0:C, :]), r=[pg], w=[ysb])
            if _RW_LEVEL < 2.5:
                continue
            pS = pbig[c % 3]
            for h in range(8):
                hs = slice(h * 64, (h + 1) * 64)
                P.mm(pS[0:64, hs], tb[0:C, hs], UtS[0:C, hs], True, False, r=[tb, UtS], w=[pS])
                P.mm(pS[0:64, hs], tk[0:C, hs], tv[0:C, hs], False, True, r=[tk, tv], w=[pS])
            for h in range(8):
                dc, dcR = decay(h // 2, h % 2, c)
                P.op(DVE, lambda e, h=h, pS=pS, dc=dc: e.scalar_tensor_tensor(
                    rS[:, h, :], rS[:, h, :], dc, pS[0:64, h * 64:(h + 1) * 64], ALU.mult, ALU.add), r=[rS, dcR, pS], w=[rS])
            P.op(ACT, lambda e: e.copy(rSb[:, :, :], rS[:, :, :]), r=[rS], w=[rSb])
            if _RW_LEVEL < 2.6:
                continue
            y3 = hv(ysb[0:C, :], 64)
            P.op(DVE, lambda e, y3=y3: e.tensor_reduce(gst[0:C, 0:8], y3, AX.X, ALU.add), r=[ysb], w=[gst])
            P.op(DVE, lambda e: e.tensor_tensor(ysq[0:C, :], ysb[0:C, :], ysb[0:C, :], ALU.mult), r=[ysb], w=[ysq])
            P.op(DVE, lambda e: e.tensor_reduce(gst[0:C, 8:16], hv(ysq[0:C, :], 64), AX.X, ALU.add), r=[ysq], w=[gst])
            P.op(DVE, lambda e: e.tensor_scalar(gst[0:C, 16:32], gst[0:C, 0:16], 1.0 / 64, None, ALU.mult), r=[gst], w=[gst])
            P.op(DVE, lambda e: e.tensor_tensor(gst[0:C, 0:8], gst[0:C, 16:24], gst[0:C, 16:24], ALU.mult), r=[gst], w=[gst])
            P.op(DVE, lambda e: e.tensor_tensor(gst[0:C, 8:16], gst[0:C, 24:32], gst[0:C, 0:8], ALU.subtract), r=[gst], w=[gst])
            P.op(DVE, lambda e: e.tensor_scalar(gst[0:C, 8:16], gst[0:C, 8:16], 64e-5, None, ALU.add), r=[gst], w=[gst])
            P.op(ACT, lambda e: e.activation(gst[0:C, 0:8], gst[0:C, 8:16], AF.Sqrt), r=[gst], w=[gst])
            P.op(DVE, lambda e: e.reciprocal(gst[0:C, 8:16], gst[0:C, 0:8]), r=[gst], w=[gst])
            P.op(DVE, lambda e, y3=y3: e.tensor_tensor(y3, y3, gst[0:C, 16:24].unsqueeze(2).to_broadcast([C, 8, 64]), ALU.subtract),
                 r=[ysb, gst], w=[ysb])
            P.op(DVE, lambda e, y3=y3: e.tensor_tensor(hv(ynb[0:C, :], 64), y3, gst[0:C, 8:16].unsqueeze(2).to_broadcast([C, 8, 64]), ALU.mult),
                 r=[ysb, gst], w=[ynb])
            pt = next_ptr()
            for j in range(4):
                P.tr(pt[:, j * C:(j + 1) * C], ynb[0:C, j * 128:(j + 1) * 128], CB("ident", C, 0, C), r=[ynb, cstb], w=[pt])
            P.op(ACT, lambda e, pt=pt, cs=cs: e.copy(YN[:, :].rearrange("p (j t) -> p j t", j=4)[:, :, cs], hv(pt[:, 0:4 * C], C)),
                 r=[pt], w=[YN])
            if _RW_LEVEL < 2.7:
                continue
            if kind == "s" or (last_blk and c == nch - 1):
                for h in range(8):
                    P.tr(px[0:64, h * 64:(h + 1) * 64], rS[:, h, :], CF("ident", 64, 0, 64), r=[rS, cst], w=[px])
                P.op(ACT, lambda e: e.copy(Sout[:, :], px[0:64, :]), r=[px], w=[Sout])
                dst_ = (wkv_s.t[l, c] if kind == "s" else wkv_p.t[l]).rearrange("h v k -> v h k")
                P.dma(dst_, Sout[:, :].rearrange("p (h k) -> p h k", h=8), r=[Sout], w=[wkv_s if kind == "s" else wkv_p], out=True)
        for j in range(4):
            col = lambda nm: pcol[:, PC[nm] + j:PC[nm] + j + 1]
            P.op(DVE, lambda e, j=j, s1=col("gn_w"), s2=col("gn_b"): e.tensor_scalar(fin[:, 0:NT], YN[:, j * NT:(j + 1) * NT], s1, s2,
                                                                                  ALU.mult, ALU.add), r=[YN, pcol], w=[fin])
            P.op(DVE, lambda e, j=j: e.tensor_tensor(fin[:, 0:NT], fin[:, 0:NT], bon[j][:, :], ALU.add), r=[fin, bon[j]], w=[fin])
            P.op(DVE, lambda e, j=j: e.tensor_tensor(yT[:, j, 0:NT], fin[:, 0:NT], gsb[j][:, :], ALU.mult), r=[fin, gsb[j]], w=[yT])

    blocks = [("p", i * 512, 512) for i in range(4)] + [("s", 0, 64)]
    if mini:
        blocks = mini
    for l in range(1 if mini else DEPTH):
        arena_fence()
        arena_reset()
        layer_params(l)
        for (kind, tok0, NT) in blocks:
            tiles = [(i * 128, 128) for i in range(NT // 128)] if kind == "p" else [(0, 64)]
            if l == 0:
                xsrc = xp if kind == "p" else xs
            else:
                xsrc = xmid_p if kind == "p" else xmid_s
            if l == DEPTH - 1:
                xdst = y_p if kind == "p" else y_s
            else:
                xdst = xmid_p if kind == "p" else xmid_s
            pl = pp if kind == "p" else psm
            bcast_load(gbc, W["ln_mix_pre"], l)
            for (o, m) in tiles:
                P.dma(xa[0:m, :], xsrc[tok0 + o:tok0 + o + m, :], r=[xsrc], w=[xa])
                rs = rms_rstd(xa, xa[0:m, :], m, D, 0, xb, xb[0:m, :], 1.0 / D, epsn)
                P.op(DVE, lambda e, m=m, rs=rs: e.scalar_tensor_tensor(xnb[0:m, :], xa[0:m, :], rs, gbc[0:m, :],
                                                                    ALU.mult, ALU.mult), r=[xa, stat, gbc], w=[xnb])
                to_featmajor(xnb, lambda kc, m=m: xnb[0:m, kc * 128:(kc + 1) * 128], m, o, hT)
            if stage < 4:
                for kc in range(16):
                    P.op(DVE, lambda e, kc=kc: e.tensor_scalar(yT[:, kc, :], hT[:, kc, :], 0.0, None, ALU.mult), r=[hT], w=[yT])
            if mini:
                arena_fence()
                arena_reset()
                mixer_rwkv(l, kind, tok0, NT)
                arena_fence()
                for j in range(4):
                    P.op(ACT, lambda e, j=j, NT=NT: e.copy(xa[:, j * 512:j * 512 + NT], yT[:, j, 0:NT]), r=[yT], w=[xa])
                P.dma(dbg_d[0, :, :], xa[:, :], r=[xa], w=[dbg_d], out=True)
                continue
            if stage >= 1:
                arena_fence()
                arena_reset()
                mixer_pool(l, kind, tok0, NT)
            if stage >= 2:
                arena_fence()
                arena_reset()
                mixer_sgu(l, kind, tok0, NT, tiles)
            if stage >= 3:
                arena_fence()
                arena_reset()
                mixer_hgrn(l, kind, tok0, NT)
            if stage >= 4:
                arena_fence()
                arena_reset()
                mixer_rwkv(l, kind, tok0, NT)
            arena_fence()
            for db in range(4):
                s = load_w(W["w_out"], l, 0, 16, db * 512, 512)
                for ti, (o, m) in enumerate(tiles):
                    pb = next_pbig()
                    for kc in range(16):
                        P.mm(pb[0:m, :], yT[:, kc, o:o + m], s[:, kc, :], kc == 0, kc == 15, r=[yT, s], w=[pb])
                    P.op(ACT, lambda e, pb=pb, ti=ti, db=db, m=m: e.copy(big[0:m, ti, db * 512:(db + 1) * 512], pb[0:m, :]),
                         r=[pb], w=[big])
            bcast_load(gbc, W["ln_mix_post"], l)
            for ti, (o, m) in enumerate(tiles):
                dbg_on = False
                if dbg_on:
                    P.dma(dbg_d[0, :, :], big[:, 0, :], r=[big], w=[dbg_d])
                    for kc in range(4):
                        P.op(ACT, lambda e, kc=kc: e.copy(xa[:, kc * 512:(kc + 1) * 512], yT[:, kc * 4, :]), r=[yT], w=[xa])
                    P.dma(dbg_d[5, :, :], xa[:, :], r=[xa], w=[dbg_d])
                rs = rms_rstd(big, big[0:m, ti, :], m, D, 0, xb, xb[0:m, :], 1.0 / D, epsn)
                P.dma(xa[0:m, :], xsrc[tok0 + o:tok0 + o + m, :], r=[xsrc], w=[xa])
                P.op(DVE, lambda e, m=m, ti=ti, rs=rs: e.scalar_tensor_tensor(xb[0:m, :], big[0:m, ti, :], rs, gbc[0:m, :],
                                                                          ALU.mult, ALU.mult), r=[big, stat, gbc], w=[xb])
                if dbg_on:
                    P.dma(dbg_d[1, :, :], xb[:, :], r=[xb], w=[dbg_d])
                    P.dma(dbg_d[2, :, :], xa[:, :], r=[xa], w=[dbg_d])
                    P.dma(dbg_d[4, :, 0:8], stat[:, :], r=[stat], w=[dbg_d])
                P.op(DVE, lambda e, m=m, ti=ti: e.tensor_tensor(big[0:m, ti, :], xa[0:m, :], xb[0:m, :], ALU.add),
                     r=[xa, xb], w=[big])
                if dbg_on:
                    P.dma(dbg_d[3, :, :], big[:, 0, :], r=[big], w=[dbg_d])
                P.dma(xf1_d[o:o + m, :], big[0:m, ti, :], r=[big], w=[xf1_d])
            bcast_load(gbc, W["ln_ffn_pre"], l)
            for ti, (o, m) in enumerate(tiles):
                rs = rms_rstd(big, big[0:m, ti, :], m, D, 4, xb, xb[0:m, :], 1.0 / D, epsn)
                P.op(DVE, lambda e, m=m, ti=ti, rs=rs: e.scalar_tensor_tensor(xnb[0:m, :], big[0:m, ti, :], rs, gbc[0:m, :],
                                                                          ALU.mult, ALU.mult), r=[big, stat, gbc], w=[xnb])
                to_featmajor(xnb, lambda kc, m=m: xnb[0:m, kc * 128:(kc + 1) * 128], m, o, hT)
            dbg4 = _DEBUG and l == 0 and kind == "p" and tok0 == 0
            if dbg4:
                P.op(ACT, lambda e: e.copy(xa[:, 0:512], hT[:, 0, :]), r=[hT], w=[xa])
                P.op(ACT, lambda e: e.copy(xa[:, 512:1024], hT[:, 5, :]), r=[hT], w=[xa])
                P.dma(dbg_d[0, :, 0:1024], xa[:, 0:1024], r=[xa], w=[dbg_d])
            for fg in range(DFF // 512):
                sg = load_w(W["ffn_w_gu"], l, 0, 16, fg * 512, 512)
                su = load_w(W["ffn_w_gu"], l, 0, 16, DFF + fg * 512, 512)
                for fc in range(4):
                    for kc in range(16):
                        P.mm(pg[:, 0:NT], sg[:, kc, fc * 128:(fc + 1) * 128], hT[:, kc, 0:NT], kc == 0, kc == 15,
                             r=[sg, hT], w=[pg])
                    for kc in range(16):
                        P.mm(pu[:, 0:NT], su[:, kc, fc * 128:(fc + 1) * 128], hT[:, kc, 0:NT], kc == 0, kc == 15,
                             r=[su, hT], w=[pu])
                    P.op(ACT, lambda e, NT=NT: e.activation(silu_t[:, 0:NT], pg[:, 0:NT], AF.Silu), r=[pg], w=[silu_t])
                    fidx = fg * 4 + fc
                    P.op(DVE, lambda e, fidx=fidx, NT=NT: e.tensor_tensor(actT[:, fidx, 0:NT], silu_t[:, 0:NT], pu[:, 0:NT], ALU.mult),
                         r=[silu_t, pu], w=[actT])
            if dbg4:
                P.op(ACT, lambda e: e.copy(xa[:, 0:512], actT[:, 0, :]), r=[actT], w=[xa])
                P.op(ACT, lambda e: e.copy(xa[:, 512:1024], actT[:, 43, :]), r=[actT], w=[xa])
                P.op(ACT, lambda e: e.copy(xa[:, 1024:1536], silu_t[:, :]), r=[silu_t], w=[xa])
                P.op(ACT, lambda e: e.copy(xa[:, 1536:2048], pu[:, :]), r=[pu], w=[xa])
                P.dma(dbg_d[1, :, :], xa[:, :], r=[xa], w=[dbg_d])
            for db in range(4):
                ss = [load_w(W["ffn_w_down"], l, k0, nk, db * 512, 512) for (k0, nk) in ((0, 16), (16, 16), (32, 12))]
                for ti, (o, m) in enumerate(tiles):
                    pb = next_pbig()
                    for fc in range(44):
                        s = ss[fc // 16]
                        P.mm(pb[0:m, :], actT[:, fc, o:o + m], s[:, fc % 16, :], fc == 0, fc == 43, r=[actT, s], w=[pb])
                    P.op(ACT, lambda e, pb=pb, ti=ti, db=db, m=m: e.copy(big[0:m, ti, db * 512:(db + 1) * 512], pb[0:m, :]),
                         r=[pb], w=[big])
            arena_fence()
            bcast_load(gbc, W["ln_ffn_post"], l)
            for ti, (o, m) in enumerate(tiles):
                rs = rms_rstd(big, big[0:m, ti, :], m, D, 0, xb, xb[0:m, :], 1.0 / D, epsn)
                P.dma(xa[0:m, :], xf1_d[o:o + m, :], r=[xf1_d], w=[xa])
                dbg6 = False
                if dbg6:
                    P.dma(dbg_d[0, :, :], big[:, ti, :], r=[big], w=[dbg_d])
                    P.dma(dbg_d[2, :, :], xa[:, :], r=[xa], w=[dbg_d])
                P.op(DVE, lambda e, m=m, ti=ti, rs=rs: e.scalar_tensor_tensor(xb[0:m, :], big[0:m, ti, :], rs, gbc[0:m, :],
                                                                          ALU.mult, ALU.mult), r=[big, stat, gbc], w=[xb])
                if dbg6:
                    P.dma(dbg_d[1, :, :], xb[:, :], r=[xb], w=[dbg_d])
                    P.dma(dbg_d[4, :, 0:8], stat[:, :], r=[stat], w=[dbg_d])
                    P.dma(dbg_d[5, :, :], gbc[:, :], r=[gbc], w=[dbg_d])
                P.op(DVE, lambda e, m=m, ti=ti: e.tensor_tensor(big[0:m, ti, :], xa[0:m, :], xb[0:m, :], ALU.add),
                     r=[xa, xb], w=[big])
                if dbg6:
                    P.dma(dbg_d[3, :, :], big[:, ti, :], r=[big], w=[dbg_d])
                P.op(ACT, lambda e, m=m, ti=ti: e.copy(xnb[0:m, :], big[0:m, ti, :]), r=[big], w=[xnb])
                to_featmajor(xnb, lambda kc, m=m: xnb[0:m, kc * 128:(kc + 1) * 128], m, o, yT)
                P.dma(xa[0:m, 0:PLE], pl[l, tok0 + o:tok0 + o + m, :], r=[pl], w=[xa])
                P.op(ACT, lambda e, m=m: e.copy(xnb[0:m, 0:PLE], xa[0:m, 0:PLE]), r=[xa], w=[xnb])
                to_featmajor(xnb, lambda kc, m=m: xnb[0:m, kc * 128:(kc + 1) * 128], m, o, pT, nkc=2)
            for db in range(4):
                sgt = load_w(W["ple_gate"], l, 0, 16, db * 512, 512)
                spj = load_w(W["ple_proj"], l, 0, 2, db * 512, 512)
                for ti, (o, m) in enumerate(tiles):
                    pb = next_pbig()
                    for kc in range(16):
                        P.mm(pb[0:m, :], yT[:, kc, o:o + m], sgt[:, kc, :], kc == 0, kc == 15, r=[yT, sgt], w=[pb])
                    pb2 = next_pbig()
                    for kc in range(2):
                        P.mm(pb2[0:m, :], pT[:, kc, o:o + m], spj[:, kc, :], kc == 0, kc == 1, r=[pT, spj], w=[pb2])
                    P.op(ACT, lambda e, pb=pb, m=m: e.activation(silu_t[0:m, :], pb[0:m, :], AF.Sigmoid), r=[pb], w=[silu_t])
                    P.op(DVE, lambda e, pb2=pb2, m=m: e.tensor_tensor(silu_t[0:m, :], silu_t[0:m, :], pb2[0:m, :], ALU.mult),
                         r=[silu_t, pb2], w=[silu_t])
                    P.op(DVE, lambda e, m=m, ti=ti, db=db: e.tensor_tensor(big[0:m, ti, db * 512:(db + 1) * 512],
                                                                         big[0:m, ti, db * 512:(db + 1) * 512],
                                                                         silu_t[0:m, :], ALU.add), r=[big, silu_t], w=[big])
            for ti, (o, m) in enumerate(tiles):
                P.dma(xdst[tok0 + o:tok0 + o + m, :], big[0:m, ti, :], r=[big], w=[xdst], out=(l == DEPTH - 1))
    stats = P.emit()
    return nc, stats


def kernel(**inp):
    if "nc" not in _BUILT:
        _BUILT["nc"], _BUILT["stats"] = build()
    nc = _BUILT["nc"]
    f = lambda a: np.ascontiguousarray(np.asarray(a, dtype=np.float32))
    wkeys = ["ln_mix_pre", "ln_mix_post", "ln_ffn_pre", "ln_ffn_post", "w_in", "rwkv_mu", "rwkv_w_lora", "rwkv_w0",
             "rwkv_a_lora", "rwkv_a0", "rwkv_g_lora", "rwkv_k_k", "rwkv_k_a", "rwkv_r_k", "rwkv_gn_w", "rwkv_gn_b",
             "sgu_ln_w", "sgu_ln_b", "sgu_w", "sgu_b", "sgu_norm", "hgrn_lb_logits", "hgrn_norm", "pool_w",
             "pool_scale", "w_out", "ffn_w_gu", "ffn_w_down", "ple_gate", "ple_proj"]
    shared = {k: f(inp[k]) for k in wkeys}
    shared["rwkv_r_k"] = shared["rwkv_r_k"].reshape(DEPTH, G)
    shared["consts"] = CONSTS
    in_maps = []
    for c in range(8):
        b = c % 4
        sl = slice(c * NSEQ_S, (c + 1) * NSEQ_S)
        m = dict(shared)
        m["xp"] = f(inp["x_prompt"][b])
        m["xs"] = f(inp["x_sample"][sl]).reshape(64, D)
        m["pp"] = f(inp["p_prompt"][:, b])
        m["psm"] = f(inp["p_sample"][:, sl]).reshape(DEPTH, 64, PLE)
        m["st_wkv"] = f(inp["state_rwkv_wkv"][:, sl])
        m["st_shift"] = f(inp["state_rwkv_shift"][:, sl])
        m["st_hgrn"] = f(inp["state_hgrn"][:, sl])
        m["st_pool"] = f(inp["state_pool"][:, sl])
        in_maps.append(m)
    res = run_bass_kernel_spmd(nc, in_maps, core_ids=list(range(8)))
    R = res.results
    _BUILT["R"] = R
    y_prompt = np.stack([R[b]["y_p"] for b in range(4)], 0)
    y_sample = np.concatenate([R[c]["y_s"].reshape(NSEQ_S, LS, D) for c in range(8)], 0)
    pst = lambda k: np.stack([R[b][k] for b in range(4)], 1)
    sst = lambda k: np.concatenate([R[c][k] for c in range(8)], 1)
    out = (y_prompt, y_sample, pst("wkv_p"), pst("shift_p"), pst("hgrn_p"), pst("pool_p"),
           sst("wkv_s"), sst("shift_s"), sst("hgrn_s"), sst("pool_s"), sst("sguv_s"))
    return tuple(np.ascontiguousarray(o, dtype=np.float32) for o in out)
```

```python
import numpy as np
import concourse.bass as bass
import concourse.mybir as mybir
from concourse.bass_utils import run_bass_kernel_spmd

F32 = mybir.dt.float32
BF16 = mybir.dt.bfloat16
AF = mybir.ActivationFunctionType
ALU = mybir.AluOpType
AX = mybir.AxisListType
PE, DVE, ACT, POOL, SP = "tensor", "vector", "scalar", "gpsimd", "sync"

D = 2048
DEPTH = 2
SEQ = 2048
NSEQ_S = 16
LS = 4
G = 512
RC = 1696
IN_COLS = 5280
DFF = 5632
PLE = 256
C_R, C_S, C_H, C_P = 0, 1696, 2720, 4768
NEG_E05 = -0.6065306597126334


class Res:
    __slots__ = ("name", "t", "last_w", "readers", "const", "dma_w")

    def __init__(self, name, t, const=False):
        self.name = name
        self.t = t
        self.last_w = None
        self.readers = []
        self.const = const
        self.dma_w = []

    def __getitem__(self, idx):
        return self.t[idx]


class Op:
    __slots__ = ("eng", "fn", "deps", "signal", "sigidx", "is_dma", "sem", "target")

    def __init__(self, eng, fn, is_dma):
        self.eng = eng
        self.fn = fn
        self.deps = []
        self.signal = False
        self.sigidx = 0
        self.is_dma = is_dma
        self.sem = None
        self.target = 0


class Prog:
    NDMASEM = 48

    def __init__(self, nc):
        self.nc = nc
        self.ops = []
        self.dma_last = [None] * self.NDMASEM
        self.dma_uses = [0] * self.NDMASEM
        self.dma_rr = 0
        self.out_dmas = []

    def sb(self, name, shape, dt=F32):
        return Res(name, self.nc.alloc_sbuf_tensor(name, list(shape), dt))

    def ps(self, name, shape, dt=F32):
        return Res(name, self.nc.alloc_psum_tensor(name, list(shape), dt))

    def dram(self, name, shape, dt=F32, kind="Internal"):
        return Res(name, self.nc.dram_tensor(name, list(shape), dt, kind=kind).ap())

    def view(self, name, ap):
        return Res(name, ap)

    def op(self, eng, fn, r=(), w=(), dma=False, out=False):
        o = Op(eng, fn, dma)
        deps = {}
        for res in r:
            if res.last_w is not None:
                deps[id(res.last_w)] = (res.last_w, "RAW")
            for dw in res.dma_w:
                deps[id(dw)] = (dw, "RAW")
        for res in w:
            if res.last_w is not None and id(res.last_w) not in deps:
                deps[id(res.last_w)] = (res.last_w, "WAW")
            for dw in res.dma_w:
                if id(dw) not in deps:
                    deps[id(dw)] = (dw, "WAW")
            for rd in res.readers:
                if id(rd) not in deps:
                    deps[id(rd)] = (rd, "WAR")
        for p, kind in deps.values():
            if (not p.is_dma) and (not dma) and p.eng == eng:
                if eng == PE or kind != "RAW":
                    continue
            o.deps.append(p)
        if dma:
            j = self.dma_rr
            self.dma_rr = (j + 1) % self.NDMASEM
            prev = self.dma_last[j]
            if prev is not None and all(prev is not d for d in o.deps):
                o.deps.append(prev)
            self.dma_uses[j] += 1
            o.sem = j
            o.target = 16 * self.dma_uses[j]
            self.dma_last[j] = o
            if out:
                self.out_dmas.append(o)
        for res in r:
            if not res.const:
                if not dma:
                    res.readers = [x for x in res.readers if x.is_dma or x.eng != eng]
                res.readers.append(o)
        for res in w:
            res.last_w = o
            res.readers = []
            if dma:
                res.dma_w.append(o)
                if len(res.dma_w) > 40:
                    res.dma_w = res.dma_w[-40:]
            else:
                res.dma_w = []
        self.ops.append(o)
        return o

    def fence(self, eng, fn, ress):
        return self.op(eng, fn, r=(), w=list(ress))

    def mm(self, out_ap, lhsT, rhs, start, stop, r, w):
        return self.op(PE, lambda e: e.matmul(out_ap, lhsT, rhs, start=start, stop=stop), r=r, w=w)

    def tr(self, out_ap, in_ap, ident_ap, r, w):
        return self.op(PE, lambda e: e.transpose(out_ap, in_ap, ident_ap), r=r, w=w)

    def dma(self, out_ap, in_ap, r, w, eng=SP, out=False, **kw):
        return self.op(eng, lambda e: e.dma_start(out=out_ap, in_=in_ap, **kw), r=r, w=w, dma=True, out=out)

    def emit(self):
        nc = self.nc
        ops = self.ops
        fin = Op(SP, None, False)
        fin.deps = list(self.out_dmas)
        ops.append(fin)
        for o in ops:
            for d in o.deps:
                d.signal = True
        engs = [PE, DVE, ACT, POOL, SP]
        cnt = {e: 0 for e in engs}
        for o in ops:
            if o.signal and not o.is_dma:
                cnt[o.eng] += 1
                o.sigidx = cnt[o.eng]
        esem = {e: nc.alloc_semaphore("es_" + e) for e in engs}
        dsem = [nc.alloc_semaphore("ds_%d" % j) for j in range(self.NDMASEM)]
        per = {e: [o for o in ops if o.eng == e] for e in engs}
        NS = self.NDMASEM

        nw = {x: 0 for x in engs}

        def run(e, engobj):
            seen = {x: 0 for x in engs}
            seend = [0] * NS
            for o in per[e]:
                nw[e] += sum(1 for d in o.deps if ((seend[d.sem] < d.target) if d.is_dma else (seen[d.eng] < d.sigidx)))
                for d in o.deps:
                    if d.is_dma:
                        if seend[d.sem] >= d.target:
                            continue
                        engobj.wait_ge(dsem[d.sem], d.target)
                        seend[d.sem] = d.target
                    else:
                        if seen[d.eng] >= d.sigidx:
                            continue
                        engobj.wait_ge(esem[d.eng], d.sigidx)
                        seen[d.eng] = d.sigidx
                if o.fn is None:
                    continue
                ins = o.fn(engobj)
                if o.is_dma:
                    ins.then_inc(dsem[o.sem], 16)
                elif o.signal:
                    ins.then_inc(esem[e], 1)

        with nc.Block() as block:
            @block.tensor
            def _(e):
                run(PE, e)

            @block.vector
            def _(e):
                run(DVE, e)

            @block.scalar
            def _(e):
                run(ACT, e)

            @block.gpsimd
            def _(e):
                run(POOL, e)

            @block.sync
            def _(e):
                run(SP, e)
        return dict(n_ops=len(ops), per={e: len(per[e]) for e in engs}, sig=cnt, waits=nw)


CONST_COLS = {}


def _make_consts():
    cols = []
    off = [0]

    def add(name, arr):
        a = np.zeros((128, arr.shape[1]), np.float32)
        a[:arr.shape[0]] = arr
        CONST_COLS[name] = (off[0], arr.shape[1])
        off[0] += arr.shape[1]
        cols.append(a)

    add("ident", np.eye(128, dtype=np.float32))
    bo = np.zeros((128, 128), np.float32)
    bo[:64, :64] = 1
    bo[64:, 64:] = 1
    add("blockones", bo)
    add("ones", np.ones((128, 128), np.float32))
    i = np.arange(64)
    add("SU", (i[:, None] < i[None, :]).astype(np.float32))
    add("IU", (i[:, None] <= i[None, :]).astype(np.float32))
    add("SL", (i[:, None] > i[None, :]).astype(np.float32))
    r64 = np.ones((128, 512), np.float32)
    r64[:, ::64] = 0
    add("rst64", r64)
    r4 = np.ones((128, 64), np.float32)
    r4[:, ::4] = 0
    add("rst4", r4)
    t128 = np.arange(128)
    add("TRIL", (t128[:, None] >= t128[None, :]).astype(np.float32))
    pc = np.ones((128, 64), np.float32)
    for gi, win in enumerate((2, 4, 8, 16)):
        pos = np.arange(16)
        pc[:, gi * 16:(gi + 1) * 16] = (win / np.minimum(pos + 1, win))[None, :]
    add("poolc", pc)
    bd = np.zeros((64, 64), np.float32)
    for s in range(16):
        bd[4 * s:4 * s + 4, 4 * s:4 * s + 4] = np.tril(np.ones((4, 4))).T
    add("BD4T", bd.T.copy())
    return np.concatenate(cols, axis=1)


CONSTS = _make_consts()
NCONST = CONSTS.shape[1]

_BUILT = {}
_DEBUG = False
_RW_LEVEL = 3


def build(stage=99, mini=None):
    nc = bass.Bass("TRN2", target_bir_lowering=False)
    P = Prog(nc)
    EI, EO = "ExternalInput", "ExternalOutput"
    d_in = {}

    def din(name, shape):
        d_in[name] = P.dram(name, shape, F32, kind=EI)
        d_in[name].const = True
        return d_in[name]

    xp = din("xp", [SEQ, D])
    xs = din("xs", [64, D])
    pp = din("pp", [DEPTH, SEQ, PLE])
    psm = din("psm", [DEPTH, 64, PLE])
    st_wkv = din("st_wkv", [DEPTH, NSEQ_S, 8, 64, 64])
    st_shift = din("st_shift", [DEPTH, NSEQ_S, RC])
    st_hgrn = din("st_hgrn", [DEPTH, NSEQ_S, 4, 128, 128])
    st_pool = din("st_pool", [DEPTH, NSEQ_S, 15, G])
    consts_d = din("consts", [128, NCONST])
    wnames = dict(ln_mix_pre=[DEPTH, D], ln_mix_post=[DEPTH, D], ln_ffn_pre=[DEPTH, D], ln_ffn_post=[DEPTH, D],
                  w_in=[DEPTH, D, IN_COLS], rwkv_mu=[DEPTH, RC], rwkv_w_lora=[DEPTH, 32, G], rwkv_w0=[DEPTH, G],
                  rwkv_a_lora=[DEPTH, 32, G], rwkv_a0=[DEPTH, G], rwkv_g_lora=[DEPTH, 96, G], rwkv_k_k=[DEPTH, G],
                  rwkv_k_a=[DEPTH, G], rwkv_r_k=[DEPTH, G], rwkv_gn_w=[DEPTH, G], rwkv_gn_b=[DEPTH, G],
                  sgu_ln_w=[DEPTH, G], sgu_ln_b=[DEPTH, G], sgu_w=[DEPTH, 4, 128, 128], sgu_b=[DEPTH, 4, 128],
                  sgu_norm=[DEPTH, G], hgrn_lb_logits=[DEPTH, G], hgrn_norm=[DEPTH, G], pool_w=[DEPTH, 4, 128, 128],
                  pool_scale=[DEPTH, G], w_out=[DEPTH, D, D], ffn_w_gu=[DEPTH, D, 2 * DFF], ffn_w_down=[DEPTH, DFF, D],
                  ple_gate=[DEPTH, D, D], ple_proj=[DEPTH, PLE, D])
    W = {k: din(k, v) for k, v in wnames.items()}
    d_out = {}

    def dout(name, shape):
        d_out[name] = P.dram(name, shape, F32, kind=EO)
        return d_out[name]

    y_p = dout("y_p", [SEQ, D])
    y_s = dout("y_s", [64, D])
    wkv_p = dout("wkv_p", [DEPTH, 8, 64, 64])
    shift_p = dout("shift_p", [DEPTH, RC])
    hgrn_p = dout("hgrn_p", [DEPTH, 4, 128, 128])
    pool_p = dout("pool_p", [DEPTH, 15, G])
    wkv_s = dout("wkv_s", [DEPTH, NSEQ_S, 8, 64, 64])
    shift_s = dout("shift_s", [DEPTH, NSEQ_S, RC])
    hgrn_s = dout("hgrn_s", [DEPTH, NSEQ_S, 4, 128, 128])
    pool_s = dout("pool_s", [DEPTH, NSEQ_S, 15, G])
    sguv_s = dout("sguv_s", [DEPTH, NSEQ_S, LS, G])
    DBG = EO if _DEBUG else "Internal"
    xmid_p = P.dram("xmid_p", [SEQ, D], kind=DBG)
    xmid_s = P.dram("xmid_s", [64, D], kind=DBG)
    xf1_d = P.dram("xf1_d", [512, D], kind=DBG)
    dbg_d = P.dram("dbg_d", [6, 128, D], kind=DBG)

    cst = P.sb("cst", [128, NCONST])
    cstb = P.sb("cstb", [128, 384], BF16)
    P.dma(cst[:, :], consts_d[:, :], r=[consts_d], w=[cst])
    P.dma(cstb[:, :], consts_d[:, 0:384], r=[consts_d], w=[cstb], eng=POOL)
    cst.const = True
    cstb.const = True

    def CF(name, rows=128, c0=0, c1=None):
        o, n = CONST_COLS[name]
        c1 = n if c1 is None else c1
        return cst[0:rows, o + c0:o + c1]

    def CB(name, rows=128, c0=0, c1=None):
        o, n = CONST_COLS[name]
        c1 = n if c1 is None else c1
        return cstb[0:rows, o + c0:o + c1]

    NSLOT = 3
    wslot = [P.sb("wslot%d" % i, [128, 16, 512], BF16) for i in range(NSLOT)]
    slot_rr = [0]
    hT = P.sb("hT", [128, 16, 512], BF16)
    yT = P.sb("yT", [128, 16, 512], BF16)
    xa = P.sb("xa", [128, D])
    xb = P.sb("xb", [128, D])
    xnb = P.sb("xnb", [128, D], BF16)
    gbc = P.sb("gbc", [128, D])
    stat = P.sb("stat", [128, 8])
    epsn = P.sb("epsn", [128, 1])
    P.op(DVE, lambda e: e.memset(epsn[:, :], 1e-6), w=[epsn])
    epsn.const = True
    ARENA_F = 4 * D + 44 * 256
    arena = nc.alloc_sbuf_tensor("arena", [128, ARENA_F], F32)
    big = P.view("big", arena[:, 0:4 * D].rearrange("p (a b) -> p a b", a=4))
    actT = P.view("actT", arena[:, 4 * D:ARENA_F].bitcast(BF16).rearrange("p (a b) -> p a b", a=44))
    pT = P.view("pT", arena[:, 4 * D:4 * D + 512].bitcast(BF16).rearrange("p (a b) -> p a b", a=2))
    aoff = [0]
    mix_views = []
    last_fence = [None]

    def arena_reset():
        aoff[0] = 0
        del mix_views[:]

    def av(name, rows, cols, dt=F32):
        n32 = cols if dt == F32 else (cols + 1) // 2
        a0 = aoff[0]
        aoff[0] += n32
        assert aoff[0] <= ARENA_F, (name, aoff[0])
        ap = arena[0:rows, a0:a0 + n32]
        if dt != F32:
            ap = ap.bitcast(dt)
        r_ = P.view(name, ap)
        r_.last_w = last_fence[0]
        mix_views.append(r_)
        return r_

    fence_t = P.sb("fence_t", [128, 1])

    def arena_fence():
        last_fence[0] = P.fence(DVE, lambda e: e.memset(fence_t[:, :], 0.0), [fence_t, big, actT, pT, xb, silu_b] + mix_views)
    silu_t = P.sb("silu_t", [128, 512])
    silu_b = P.view("silu_b", xb[:, 0:512])
    pbig = [P.ps("pbig%d" % i, [128, 512]) for i in range(3)]
    ptr = [P.ps("ptr%d" % i, [128, 512], BF16) for i in range(2)]
    pg = P.ps("pg", [128, 512])
    pu = P.ps("pu", [128, 512])
    rr = {"pbig": 0, "ptr": 0}

    def next_pbig():
        rr["pbig"] = (rr["pbig"] + 1) % 3
        return pbig[rr["pbig"]]

    def next_ptr():
        rr["ptr"] = (rr["ptr"] + 1) % 2
        return ptr[rr["ptr"]]

    def load_w(wres, l, k0, nk, c0, ncols):
        s = wslot[slot_rr[0]]
        slot_rr[0] = (slot_rr[0] + 1) % NSLOT
        src = wres.t[l].rearrange("(kc p) c -> p kc c", p=128)[:, k0:k0 + nk, c0:c0 + ncols]
        P.dma(s[:, 0:nk, 0:ncols], src, r=[wres], w=[s], eng=POOL)
        return s

    def bcast_load(dst, vec_res, l, n=D, c0=0):
        v = vec_res.t[l]
        src = bass.AP(tensor=v.tensor, offset=v.offset + c0, ap=[[0, 128], [1, n]])
        P.dma(dst[:, 0:n], src, r=[vec_res], w=[dst])

    def rms_rstd(src_res, src_ap, m, ncol, col, scratch_res, scratch_ap, inv_n, eps_res):
        P.op(DVE, lambda e: e.memset(stat[0:m, col:col + 1], 0.0), w=[stat])
        P.op(ACT, lambda e: e.activation(scratch_ap, src_ap, AF.Square, accum_out=stat[0:m, col:col + 1]),
             r=[src_res, stat], w=[scratch_res, stat])
        P.op(ACT, lambda e: e.activation(stat[0:m, col + 1:col + 2], stat[0:m, col:col + 1], AF.Sqrt,
                                         bias=eps_res[0:m, 0:1], scale=inv_n), r=[stat, eps_res], w=[stat])
        P.op(DVE, lambda e: e.reciprocal(stat[0:m, col + 2:col + 3], stat[0:m, col + 1:col + 2]), r=[stat], w=[stat])
        return stat[0:m, col + 2:col + 3]

    def to_featmajor(src_res, src_bf_ap_fn, m, t0, dstT, nkc=16, kc0=0):
        for g4 in range(0, nkc, 4):
            n4 = min(4, nkc - g4)
            pt = next_ptr()
            for q in range(n4):
                kc = g4 + q
                P.tr(pt[:, q * 128:q * 128 + m], src_bf_ap_fn(kc), CB("ident", m, 0, m), r=[src_res, cstb], w=[pt])
            o_ap = dstT[:, kc0 + g4:kc0 + g4 + n4, t0:t0 + m]
            i_ap = pt[:, 0:n4 * 128].rearrange("p (a b) -> p a b", a=n4)[:, :, 0:m]
            if (g4 // 4) % 2 == 0:
                P.op(DVE, lambda e, o_ap=o_ap, i_ap=i_ap: e.tensor_copy(o_ap, i_ap), r=[pt], w=[dstT])
            else:
                P.op(ACT, lambda e, o_ap=o_ap, i_ap=i_ap: e.copy(o_ap, i_ap), r=[pt], w=[dstT])


    NCOLP = 80
    pcol = P.sb("pcol", [128, NCOLP])
    PC = {}

    def load_cols(name, vec_res, l, base, n=G):
        nchk = n // 128
        src = vec_res.t[l, 0:n].rearrange("(j p) -> p j", p=128)
        P.dma(pcol[:, base:base + nchk], src, r=[vec_res], w=[pcol], allow_slow_non_contiguous=True)
        PC[name] = base

    pw_sb = P.sb("pw_sb", [128, 4, 128], BF16)
    pool_carry = P.sb("pool_carry", [128, 4, 15])
    sgu_bc = P.sb("sgu_bc", [128, 3, G])
    wmT = P.sb("wmT", [128, 4, 128], BF16)
    bd4 = P.sb("bd4", [64, 4, 64], BF16)
    sgub_p = P.sb("sgub_p", [128, 4])
    sgub_s = P.sb("sgub_s", [64, 4])
    px = P.ps("px", [128, 512])

    def layer_params(l):
        load_cols("pool_scale", W["pool_scale"], l, 0)
        load_cols("hgrn_norm", W["hgrn_norm"], l, 4)
        load_cols("lg0", W["hgrn_lb_logits"], 0, 8)
        load_cols("lg1", W["hgrn_lb_logits"], 1, 12)
        PC["lb"], PC["oml"] = 16, 20
        load_cols("mu", W["rwkv_mu"], l, 24, n=1536)
        for q, (a_, b_) in enumerate(((1536, 1568), (1568, 1600), (1600, 1696))):
            P.dma(pcol[0:b_ - a_, 36 + q:37 + q], W["rwkv_mu"].t[l, a_:b_].rearrange("(c o) -> c o", o=1), r=[W["rwkv_mu"]], w=[pcol],
                  allow_slow_non_contiguous=True)
        for q, nm in enumerate(("w0", "a0", "k_k", "k_a", "r_k", "gn_w", "gn_b")):
            load_cols(nm, W["rwkv_" + nm], l, 40 + 4 * q)
        PC["omk"] = 68
        P.op(DVE, lambda e: e.tensor_scalar(pcol[:, 68:72], pcol[:, PC["k_a"]:PC["k_a"] + 4], -1.0, 1.0, ALU.mult, ALU.add), r=[pcol], w=[pcol])
        if l == 0:
            P.op(DVE, lambda e: e.memset(pcol[:, 16:20], 0.0), w=[pcol])
            P.op(DVE, lambda e: e.memset(pcol[:, 20:24], 1.0), w=[pcol])
        else:
            P.op(DVE, lambda e: e.tensor_tensor(pcol[:, 16:20], pcol[:, 12:16], pcol[:, 8:12], ALU.subtract), r=[pcol], w=[pcol])
            P.op(ACT, lambda e: e.activation(pcol[:, 16:20], pcol[:, 16:20], AF.Sigmoid), r=[pcol], w=[pcol])
            P.op(DVE, lambda e: e.tensor_scalar(pcol[:, 20:24], pcol[:, 16:20], -1.0, 1.0, ALU.mult, ALU.add), r=[pcol], w=[pcol])
        P.dma(pw_sb[:, :, :], W["pool_w"].t[l].rearrange("g c d -> c g d"), r=[W["pool_w"]], w=[pw_sb], eng=POOL)
        for i, nm in enumerate(("sgu_ln_w", "sgu_ln_b", "sgu_norm")):
            v = W[nm].t[l]
            P.dma(sgu_bc[:, i, :], bass.AP(tensor=v.tensor, offset=v.offset, ap=[[0, 128], [1, G]]), r=[W[nm]], w=[sgu_bc])
        wtmp = P.view("wtmp", arena[:, 0:512].rearrange("p (a b) -> p a b", a=4))
        wtmp.last_w = last_fence[0]
        wtb = P.view("wtb", arena[:, 512:768].bitcast(BF16).rearrange("p (a b) -> p a b", a=4))
        wtb.last_w = last_fence[0]
        mix_views.extend([wtmp, wtb])
        P.dma(wtmp[:, :, :], W["sgu_w"].t[l].rearrange("h t s -> t h s"), r=[W["sgu_w"]], w=[wtmp])
        tril = CF("TRIL")
        P.op(DVE, lambda e: e.tensor_tensor(wtb[:, :, :], wtmp[:, :, :], tril.unsqueeze(1).to_broadcast([128, 4, 128]), ALU.mult),
             r=[wtmp, cst], w=[wtb])
        pt = next_ptr()
        for h in range(4):
            P.tr(pt[:, h * 128:(h + 1) * 128], wtb[:, h, :], CB("ident"), r=[wtb, cstb], w=[pt])
        P.op(DVE, lambda e: e.tensor_copy(wmT[:, :, :], pt[:, :].rearrange("p (a b) -> p a b", a=4)), r=[pt], w=[wmT])
        P.dma(sgub_p[:, :], W["sgu_b"].t[l].rearrange("h t -> t h"), r=[W["sgu_b"]], w=[sgub_p], allow_slow_non_contiguous=True)
        w4 = P.view("w4", arena[0:64, 768:1024].rearrange("p (a b) -> p a b", a=4))
        w4.last_w = last_fence[0]
        w4b = P.view("w4b", arena[0:64, 1024:1152].bitcast(BF16).rearrange("p (a b) -> p a b", a=4))
        w4b.last_w = last_fence[0]
        mix_views.extend([w4, w4b])
        P.op(DVE, lambda e: e.memset(w4[:, :, :], 0.0), w=[w4])
        for sq in range(NSEQ_S):
            P.dma(w4[4 * sq:4 * sq + 4, :, 4 * sq:4 * sq + 4], W["sgu_w"].t[l, :, 0:4, 0:4].rearrange("h t s -> t h s"),
                  r=[W["sgu_w"]], w=[w4], allow_slow_non_contiguous=True)
            P.dma(sgub_s[4 * sq:4 * sq + 4, :], W["sgu_b"].t[l, :, 0:4].rearrange("h t -> t h"), r=[W["sgu_b"]], w=[sgub_s],
                  allow_slow_non_contiguous=True)
        bdt = CF("BD4T", 64)
        P.op(DVE, lambda e: e.tensor_tensor(w4b[:, :, :], w4[:, :, :], bdt.unsqueeze(1).to_broadcast([64, 4, 64]), ALU.mult),
             r=[w4, cst], w=[w4b])
        pt2 = next_ptr()
        for h in range(4):
            P.tr(pt2[0:64, h * 64:(h + 1) * 64], w4b[:, h, :], CB("ident", 64, 0, 64), r=[w4b, cstb], w=[pt2])
        P.op(DVE, lambda e: e.tensor_copy(bd4[:, :, :], pt2[0:64, 0:256].rearrange("p (a b) -> p a b", a=4)), r=[pt2], w=[bd4])

    def inproj_fm(l, col0, ncols, NT, consume):
        c = 0
        ci = 0
        while c < ncols:
            n = min(512, ncols - c)
            s = load_w(W["w_in"], l, 0, 16, col0 + c, n)
            for q in range(0, n, 128):
                rows = min(128, n - q)
                pb = next_pbig()
                for kc in range(16):
                    P.mm(pb[0:rows, 0:NT], s[:, kc, q:q + rows], hT[:, kc, 0:NT], kc == 0, kc == 15, r=[s, hT], w=[pb])
                consume(ci, pb, rows)
                ci += 1
            c += n

    def mixer_pool(l, kind, tok0, NT):
        nseq, L = (1, NT) if kind == "p" else (NSEQ_S, LS)
        E = 15 + L
        Zx = [av("poolZ%d" % g, 128, nseq * E) for g in range(4)]
        Ea = av("poolEa", 128, nseq * E)
        Eb = av("poolEb", 128, nseq * E)
        dbf = av("pooldbf", 128, NT, BF16)
        v3 = lambda r_: r_[:, :].rearrange("p (s e) -> p s e", s=nseq)
        for g in range(4):
            z3 = v3(Zx[g])
            if kind == "p":
                if tok0 == 0:
                    P.op(DVE, lambda e, z3=z3: e.memset(z3[:, :, 0:15], 0.0), w=[Zx[g]])
                else:
                    P.op(DVE, lambda e, z3=z3, g=g: e.tensor_copy(z3[:, 0, 0:15], pool_carry[:, g, :]), r=[pool_carry], w=[Zx[g]])
        if kind == "s":
            rows_ap = st_pool.t[l].rearrange("s p c -> (s p) c")
            for half in range(2):
                P.dma(xa[0:120, half * 512:(half + 1) * 512], rows_ap[half * 120:(half + 1) * 120, :], r=[st_pool], w=[xa])
            for g in range(4):
                z3 = v3(Zx[g])
                for half in range(2):
                    P.tr(px[:, 0:120], xa[0:120, half * 512 + g * 128:half * 512 + (g + 1) * 128], CF("ident", 120, 0, 120),
                         r=[xa, cst], w=[px])
                    P.op(ACT, lambda e, z3=z3, half=half: e.copy(z3[:, half * 8:(half + 1) * 8, 0:15],
                                                                px[:, 0:120].rearrange("p (s e) -> p s e", s=8)), r=[px], w=[Zx[g]])

        def consume(ci, pb, rows):
            z3 = v3(Zx[ci])
            P.op(ACT, lambda e: e.copy(z3[:, :, 15:E], pb[:, 0:NT].rearrange("p (s e) -> p s e", s=nseq)), r=[pb], w=[Zx[ci]])
        inproj_fm(l, C_P, G, NT, consume)
        for g, win in enumerate((2, 4, 8, 16)):
            src = Zx[g]
            bufs = [Ea, Eb]
            for k in range(1, g + 2):
                sft = 1 << (k - 1)
                lo = (1 << k) - 1
                dst = bufs[k % 2]
                s3, d3 = v3(src), v3(dst)
                P.op(DVE, lambda e, s3=s3, d3=d3, lo=lo, sft=sft: e.tensor_tensor(d3[:, :, lo:E], s3[:, :, lo:E], s3[:, :, lo - sft:E - sft], ALU.add),
                     r=[src], w=[dst])
                src = dst
            s3, z3 = v3(src), v3(Zx[g])
            d3 = dbf[:, :].rearrange("p (s e) -> p s e", s=nseq)
            P.op(DVE, lambda e, s3=s3, z3=z3, d3=d3, win=win: e.scalar_tensor_tensor(d3, s3[:, :, 15:E], 1.0 / win, z3[:, :, 15:E], ALU.mult, ALU.subtract),
                 r=[src, Zx[g]], w=[dbf])
            if kind == "p" and tok0 == 0:
                tmpc = bufs[(g + 2) % 2]
                pcv = CF("poolc", 128, g * 16, (g + 1) * 16)
                P.op(DVE, lambda e, s3=s3, tmpc=tmpc, pcv=pcv: e.tensor_tensor(tmpc[:, 0:16], s3[:, 0, 15:31], pcv, ALU.mult), r=[src, cst], w=[tmpc])
                P.op(DVE, lambda e, tmpc=tmpc, z3=z3, win=win: e.scalar_tensor_tensor(dbf[:, 0:16], tmpc[:, 0:16], 1.0 / win, z3[:, 0, 15:31], ALU.mult, ALU.subtract),
                     r=[tmpc, Zx[g]], w=[dbf])
            P.mm(px[:, 0:NT], pw_sb[:, g, :], dbf[:, 0:NT], True, True, r=[pw_sb, dbf], w=[px])
            sc = pcol[:, PC["pool_scale"] + g:PC["pool_scale"] + g + 1]
            P.op(ACT, lambda e, g=g, sc=sc: e.activation(yT[:, 12 + g, 0:NT], px[:, 0:NT], AF.Copy, scale=sc), r=[px, pcol], w=[yT])
            if kind == "p":
                if tok0 + NT < SEQ:
                    P.op(ACT, lambda e, z3=z3, g=g: e.copy(pool_carry[:, g, :], z3[:, 0, L:E]), r=[Zx[g]], w=[pool_carry])
                else:
                    P.dma(pool_p.t[l, :, g * 128:(g + 1) * 128].rearrange("p c -> c p"), z3[:, 0, L:E], r=[Zx[g]], w=[pool_p],
                          out=True, allow_slow_non_contiguous=True)
            else:
                for half in range(2):
                    P.op(ACT, lambda e, z3=z3, half=half: e.copy(Ea[:, 0:120].rearrange("p (s e) -> p s e", s=8),
                                                                z3[:, half * 8:(half + 1) * 8, L:E]), r=[Zx[g]], w=[Ea])
                    P.tr(px[0:120, 0:128], Ea[:, 0:120], CF("ident"), r=[Ea, cst], w=[px])
                    P.op(ACT, lambda e, g=g, half=half: e.copy(xb[0:120, half * 512 + g * 128:half * 512 + (g + 1) * 128], px[0:120, 0:128]),
                         r=[px], w=[xb])
        if kind == "s":
            orows = pool_s.t[l].rearrange("s p c -> (s p) c")
            for half in range(2):
                P.dma(orows[half * 120:(half + 1) * 120, :], xb[0:120, half * 512:(half + 1) * 512], r=[xb], w=[pool_s], out=True)

    def mixer_sgu(l, kind, tok0, NT, tiles):
        su_ = load_w(W["w_in"], l, 0, 16, C_S, 512)
        sv_ = load_w(W["w_in"], l, 0, 16, C_S + 512, 512)
        u_sb = av("sgu_u", 128, G)
        v_sb = av("sgu_v", 128, G)
        t_sb = av("sgu_t", 128, G)
        vnb = av("sgu_vnb", 128, G, BF16)
        ynb = av("sgu_ynb", 128, G, BF16)
        for (o, m) in tiles:
            pbu = next_pbig()
            for kc in range(16):
                P.mm(pbu[0:m, :], hT[:, kc, o:o + m], su_[:, kc, :], kc == 0, kc == 15, r=[hT, su_], w=[pbu])
            pbv = next_pbig()
            for kc in range(16):
                P.mm(pbv[0:m, :], hT[:, kc, o:o + m], sv_[:, kc, :], kc == 0, kc == 15, r=[hT, sv_], w=[pbv])
            P.op(DVE, lambda e, m=m: e.memset(stat[0:m, 0:8], 0.0), w=[stat])
            P.op(ACT, lambda e, m=m, pbu=pbu: e.activation(u_sb[0:m, :], pbu[0:m, :], AF.Gelu), r=[pbu], w=[u_sb])
            P.op(ACT, lambda e, m=m, pbv=pbv: e.activation(v_sb[0:m, :], pbv[0:m, :], AF.Gelu, accum_out=stat[0:m, 0:1]),
                 r=[pbv, stat], w=[v_sb, stat])
            P.op(ACT, lambda e, m=m: e.activation(t_sb[0:m, :], v_sb[0:m, :], AF.Square, accum_out=stat[0:m, 1:2]),
                 r=[v_sb, stat], w=[t_sb, stat])
            P.op(DVE, lambda e, m=m: e.tensor_scalar(stat[0:m, 2:4], stat[0:m, 0:2], 1.0 / G, None, ALU.mult), r=[stat], w=[stat])
            P.op(DVE, lambda e, m=m: e.tensor_tensor(stat[0:m, 4:5], stat[0:m, 2:3], stat[0:m, 2:3], ALU.mult), r=[stat], w=[stat])
            P.op(DVE, lambda e, m=m: e.tensor_tensor(stat[0:m, 5:6], stat[0:m, 3:4], stat[0:m, 4:5], ALU.subtract), r=[stat], w=[stat])
            P.op(DVE, lambda e, m=m: e.tensor_scalar(stat[0:m, 5:6], stat[0:m, 5:6], 1e-5, None, ALU.add), r=[stat], w=[stat])
            P.op(ACT, lambda e, m=m: e.activation(stat[0:m, 6:7], stat[0:m, 5:6], AF.Sqrt), r=[stat], w=[stat])
            P.op(DVE, lambda e, m=m: e.reciprocal(stat[0:m, 7:8], stat[0:m, 6:7]), r=[stat], w=[stat])
            P.op(DVE, lambda e, m=m: e.tensor_scalar(v_sb[0:m, :], v_sb[0:m, :], stat[0:m, 2:3], stat[0:m, 7:8], ALU.subtract, ALU.mult),
                 r=[v_sb, stat], w=[v_sb])
            P.op(DVE, lambda e, m=m: e.tensor_tensor(v_sb[0:m, :], v_sb[0:m, :], sgu_bc[0:m, 0, :], ALU.mult), r=[v_sb, sgu_bc], w=[v_sb])
            P.op(DVE, lambda e, m=m: e.tensor_tensor(v_sb[0:m, :], v_sb[0:m, :], sgu_bc[0:m, 1, :], ALU.add), r=[v_sb, sgu_bc], w=[v_sb])
            if kind == "s":
                P.dma(sguv_s.t[l].rearrange("s t c -> (s t) c"), v_sb[0:m, :], r=[v_sb], w=[sguv_s], out=True)
            P.op(ACT, lambda e, m=m: e.copy(vnb[0:m, :], v_sb[0:m, :]), r=[v_sb], w=[vnb])
            for h in range(4):
                lh = wmT[:, h, :] if kind == "p" else bd4[:, h, :]
                P.mm(px[0:m, h * 128:(h + 1) * 128], lh, vnb[0:m, h * 128:(h + 1) * 128], True, True,
                     r=[wmT if kind == "p" else bd4, vnb], w=[px])
            sbc = (sgub_p if kind == "p" else sgub_s)
            sb_ap = sbc[0:m, 0:4].unsqueeze(2).to_broadcast([m, 4, 128])
            P.op(DVE, lambda e, m=m, sb_ap=sb_ap: e.tensor_tensor(t_sb[0:m, :].rearrange("p (a b) -> p a b", a=4),
                                                                 px[0:m, :].rearrange("p (a b) -> p a b", a=4), sb_ap, ALU.add),
                 r=[px, sbc], w=[t_sb])
            P.op(DVE, lambda e, m=m: e.tensor_tensor(u_sb[0:m, :], u_sb[0:m, :], t_sb[0:m, :], ALU.mult), r=[u_sb, t_sb], w=[u_sb])
            rs = rms_rstd(u_sb, u_sb[0:m, :], m, G, 0, t_sb, t_sb[0:m, :], 1.0 / G, epsn)
            P.op(DVE, lambda e, m=m, rs=rs: e.scalar_tensor_tensor(ynb[0:m, :], u_sb[0:m, :], rs, sgu_bc[0:m, 2, :], ALU.mult, ALU.mult),
                 r=[u_sb, stat, sgu_bc], w=[ynb])
            to_featmajor(ynb, lambda kc, m=m: ynb[0:m, kc * 128:(kc + 1) * 128], m, o, yT, nkc=4, kc0=4)

    hS = P.sb("hS", [128, 4, 128])
    hSb = P.sb("hSb", [128, 4, 128], BF16)

    def mixer_hgrn(l, kind, tok0, NT):
        C = 64 if kind == "p" else LS
        nch = NT // C
        rst = CF("rst64") if kind == "p" else CF("rst4")
        qf = [av("h_q%d" % h, 128, NT) for h in range(4)]
        gate = [av("h_g%d" % h, 128, NT) for h in range(4)]
        qt = [av("h_qt%d" % h, 128, NT, BF16) for h in range(4)]
        kt = [av("h_kt%d" % h, 128, NT, BF16) for h in range(4)]
        kh = [av("h_kh%d" % h, 128, NT, BF16) for h in range(4)]
        vb = [av("h_vb%d" % h, 128, NT, BF16) for h in range(4)]
        egC = [av("h_egC%d" % h, 128, 16) for h in range(4)]
        T = [av("h_T%d" % i, 128, NT) for i in range(4)]
        Osb = av("h_O", 128, 4 * NT)
        TMK = [av("h_TMK%d" % i, 64, 512, BF16) for i in range(2)]
        TMV = [av("h_TMV%d" % i, 64, 512, BF16) for i in range(2)]
        ATs = [av("h_ATs%d" % i, 64, 4 * C, BF16) for i in range(2)]
        osq = av("h_osq", 128, NT, BF16)
        lbc, omc, hnc = PC["lb"], PC["oml"], PC["hgrn_norm"]
        c3 = lambda ap: ap.rearrange("p (n c) -> p n c", c=C)

        def consume(ci, pb, rows):
            grp, h = ci // 4, ci % 4
            if grp == 0:
                P.op(ACT, lambda e: e.activation(qf[h][:, :], pb[:, 0:NT], AF.Silu), r=[pb], w=[qf[h]])
            elif grp == 1:
                t0, t1, t2 = T[0], T[1], T[2]
                P.op(ACT, lambda e: e.activation(t0[:, :], pb[:, 0:NT], AF.Sigmoid), r=[pb], w=[t0])
                P.op(DVE, lambda e: e.tensor_scalar(t0[:, :], t0[:, :], pcol[:, omc + h:omc + h + 1], pcol[:, lbc + h:lbc + h + 1],
                                                    ALU.mult, ALU.add), r=[t0, pcol], w=[t0])
                P.op(ACT, lambda e: e.activation(t1[:, :], t0[:, :], AF.Ln), r=[t0], w=[t1])
                P.op(DVE, lambda e: e.tensor_scalar(t0[:, :], t0[:, :], -1.0, 1.0, ALU.mult, ALU.add), r=[t0], w=[t0])
                P.op(DVE, lambda e: e.tensor_tensor_scan(t2[:, :], rst[:, 0:NT], t1[:, :], 0.0, ALU.mult, ALU.add),
                     r=[cst, t1], w=[t2])
                P.op(ACT, lambda e: e.activation(t1[:, :], t2[:, :], AF.Exp), r=[t2], w=[t1])
                P.op(DVE, lambda e: e.tensor_tensor(qt[h][:, :], qf[h][:, :], t1[:, :], ALU.mult), r=[qf[h], t1], w=[qt[h]])
                P.op(ACT, lambda e: e.activation(t1[:, :], t2[:, :], AF.Exp, scale=-1.0), r=[t2], w=[t1])
                P.op(DVE, lambda e: e.tensor_tensor(kt[h][:, :], t0[:, :], t1[:, :], ALU.mult), r=[t0, t1], w=[kt[h]])
                cum3 = c3(t2[:, :])
                P.op(ACT, lambda e: e.activation(egC[h][:, 0:nch], cum3[:, :, C - 1], AF.Exp), r=[t2], w=[egC[h]])
                P.op(DVE, lambda e: e.tensor_tensor(c3(t1[:, :]), cum3[:, :, C - 1:C].to_broadcast([128, nch, C]), cum3, ALU.subtract),
                     r=[t2], w=[t1])
                P.op(ACT, lambda e: e.activation(t1[:, :], t1[:, :], AF.Exp), r=[t1], w=[t1])
                P.op(DVE, lambda e: e.tensor_tensor(kh[h][:, :], t0[:, :], t1[:, :], ALU.mult), r=[t0, t1], w=[kh[h]])
            elif grp == 2:
                P.op(ACT, lambda e: e.copy(vb[h][:, :], pb[:, 0:NT]), r=[pb], w=[vb[h]])
            else:
                P.op(ACT, lambda e: e.activation(gate[h][:, :], pb[:, 0:NT], AF.Silu), r=[pb], w=[gate[h]])
        inproj_fm(l, C_H, 4 * G, NT, consume)

        for c in range(nch):
            cs = slice(c * C, (c + 1) * C)
            tk, tv, at = TMK[c % 2], TMV[c % 2], ATs[c % 2]
            if kind == "s":
                P.dma(hS[:, :, :], st_hgrn.t[l, c].rearrange("h k v -> k h v"), r=[st_hgrn], w=[hS])
                P.op(ACT, lambda e: e.copy(hSb[:, :, :], hS[:, :, :]), r=[hS], w=[hSb])
            elif tok0 == 0 and c == 0:
                P.op(DVE, lambda e: e.memset(hS[:, :, :], 0.0), w=[hS])
                P.op(ACT, lambda e: e.copy(hSb[:, :, :], hS[:, :, :]), r=[hS], w=[hSb])
            ptk = next_ptr()
            for h in range(4):
                P.tr(ptk[0:C, h * 128:(h + 1) * 128], kh[h][:, cs], CB("ident"), r=[kh[h], cstb], w=[ptk])
            P.op(DVE, lambda e, ptk=ptk, tk=tk: e.tensor_copy(tk[0:C, :], ptk[0:C, :]), r=[ptk], w=[tk])
            ptv = next_ptr()
            for h in range(4):
                P.tr(ptv[0:C, h * 128:(h + 1) * 128], vb[h][:, cs], CB("ident"), r=[vb[h], cstb], w=[ptv])
            P.op(ACT, lambda e, ptv=ptv, tv=tv: e.copy(tv[0:C, :], ptv[0:C, :]), r=[ptv], w=[tv])
            for h in range(4):
                P.mm(px[0:C, h * C:(h + 1) * C], kt[h][:, cs], qt[h][:, cs], True, True, r=[kt[h], qt[h]], w=[px])
            iu = CF("IU", C, 0, C)
            P.op(DVE, lambda e, at=at, iu=iu: e.tensor_tensor(at[0:C, :].rearrange("p (h c) -> p h c", h=4),
                                                              px[0:C, 0:4 * C].rearrange("p (h c) -> p h c", h=4),
                                                              iu.unsqueeze(1).to_broadcast([C, 4, C]), ALU.mult), r=[px, cst], w=[at])
            for h in range(4):
                P.mm(pg[:, h * C:(h + 1) * C], tv[0:C, h * 128:(h + 1) * 128], at[0:C, h * C:(h + 1) * C], True, False,
                     r=[tv, at], w=[pg])
                P.mm(pg[:, h * C:(h + 1) * C], hSb[:, h, :], qt[h][:, cs], False, True, r=[hSb, qt[h]], w=[pg])
            P.op(ACT, lambda e, cs=cs: e.copy(Osb[:, :].rearrange("p (h t) -> p h t", h=4)[:, :, cs],
                                              pg[:, 0:4 * C].rearrange("p (h c) -> p h c", h=4)), r=[pg], w=[Osb])
            for h in range(4):
                P.mm(pu[:, h * 128:(h + 1) * 128], tk[0:C, h * 128:(h + 1) * 128], tv[0:C, h * 128:(h + 1) * 128], True, True,
                     r=[tk, tv], w=[pu])
            for h in range(4):
                P.op(DVE, lambda e, h=h, c=c: e.scalar_tensor_tensor(hS[:, h, :], hS[:, h, :], egC[h][:, c:c + 1],
                                                                    pu[:, h * 128:(h + 1) * 128], ALU.mult, ALU.add),
                     r=[hS, egC[h], pu], w=[hS])
            P.op(ACT, lambda e: e.copy(hSb[:, :, :], hS[:, :, :]), r=[hS], w=[hSb])
            if kind == "s":
                P.dma(hgrn_s.t[l, c].rearrange("h k v -> k h v"), hS[:, :, :], r=[hS], w=[hgrn_s], out=True)
            elif tok0 + NT == SEQ and c == nch - 1:
                P.dma(hgrn_p.t[l].rearrange("h k v -> k h v"), hS[:, :, :], r=[hS], w=[hgrn_p], out=True)
        for h in range(4):
            o_ap = Osb[:, h * NT:(h + 1) * NT]
            P.op(DVE, lambda e, o_ap=o_ap: e.tensor_tensor(osq[:, :], o_ap, o_ap, ALU.mult), r=[Osb], w=[osq])
            pb = next_pbig()
            P.mm(pb[:, 0:NT], CB("ones"), osq[:, :], True, True, r=[cstb, osq], w=[pb])
            t0 = T[0]
            P.op(ACT, lambda e, pb=pb, t0=t0: e.activation(t0[:, :], pb[:, 0:NT], AF.Sqrt, bias=epsn[:, 0:1], scale=1.0 / 128),
                 r=[pb, epsn], w=[t0])
            P.op(DVE, lambda e, t0=t0: e.reciprocal(t0[:, :], t0[:, :]), r=[t0], w=[t0])
            P.op(DVE, lambda e, t0=t0, o_ap=o_ap: e.tensor_tensor(t0[:, :], t0[:, :], o_ap, ALU.mult), r=[t0, Osb], w=[t0])
            P.op(DVE, lambda e, t0=t0, h=h: e.scalar_tensor_tensor(yT[:, 8 + h, 0:NT], t0[:, :], pcol[:, hnc + h:hnc + h + 1],
                                                                   gate[h][:, :], ALU.mult, ALU.mult), r=[t0, pcol, gate[h]], w=[yT])

    rS = P.sb("rS", [64, 8, 64])
    rSb = P.sb("rSb", [64, 8, 64], BF16)
    shcar = P.sb("shcar", [128, 16])

    def mixer_rwkv(l, kind, tok0, NT):
        C = 64 if kind == "p" else LS
        nch = NT // C
        nseq, L = (1, NT) if kind == "p" else (NSEQ_S, LS)
        E = L + 1
        C2 = 2 * C
        rst = CF("rst64") if kind == "p" else CF("rst4")
        last_blk = (kind == "p" and tok0 + NT == SEQ)
        c3 = lambda ap: ap.rearrange("p (n c) -> p n c", c=C)
        AR = [av("r_AR%d" % j, 128, 2 * NT, BF16) for j in range(4)]
        BK = [av("r_BK%d" % j, 128, 2 * NT, BF16) for j in range(4)]
        Bh = [av("r_Bh%d" % j, 128, NT, BF16) for j in range(4)]
        Kh = [av("r_Kh%d" % j, 128, NT, BF16) for j in range(4)]
        Vb = [av("r_Vb%d" % j, 128, NT, BF16) for j in range(4)]
        gsb = [av("r_g%d" % j, 128, NT) for j in range(4)]
        bon = [av("r_bon%d" % j, 128, NT) for j in range(4)]
        gC = [av("r_gC%d" % j, 128, 16) for j in range(4)]
        YN = av("r_YN", 128, 4 * NT, BF16)
        mark = aoff[0]
        lw = av("r_lw", 32, G, BF16)
        la = av("r_la", 32, G, BF16)
        lg = av("r_lg", 96, G, BF16)
        txw = av("r_txw", 32, NT, BF16)
        xab = av("r_xab", 32, NT, BF16)
        sgb = av("r_sgb", 96, NT, BF16)
        Zl = av("r_Zl", 128, nseq * E)
        Zr = av("r_Zr", 128, nseq * E)
        Zk = av("r_Zk", 128, nseq * E)
        Zv = av("r_Zv", 128, nseq * E)
        T = [av("r_T%d" % i, 128, NT) for i in range(5)]
        sqb = av("r_sqb", 128, NT, BF16)
        P.dma(lw[:, :], W["rwkv_w_lora"].t[l], r=[W["rwkv_w_lora"]], w=[lw], eng=POOL)
        P.dma(la[:, :], W["rwkv_a_lora"].t[l], r=[W["rwkv_a_lora"]], w=[la], eng=POOL)
        P.dma(lg[:, :], W["rwkv_g_lora"].t[l], r=[W["rwkv_g_lora"]], w=[lg], eng=POOL)
        mu0 = PC["mu"]

        def zl(Z, rows):
            return Z[0:rows, :].rearrange("p (s e) -> p s e", s=nseq)

        def zc(Z, rows=128):
            if kind == "p":
                return Z[0:rows, 1:E].rearrange("p (n c) -> p n c", c=C)
            return zl(Z, rows)[:, :, 1:E]

        def lerp(Z, rows, ci, c0, pb):
            z3 = zl(Z, rows)
            if kind == "p":
                if tok0 == 0:
                    P.op(DVE, lambda e: e.memset(z3[:, :, 0:1], 0.0), w=[Z])
                else:
                    P.op(DVE, lambda e: e.tensor_copy(z3[:, 0, 0:1], shcar[0:rows, ci:ci + 1]), r=[shcar], w=[Z])
            else:
                P.tr(px[0:rows, 0:16], xa[0:16, c0:c0 + rows], CF("ident", 16, 0, 16), r=[xa, cst], w=[px])
                P.op(ACT, lambda e: e.copy(z3[:, :, 0], px[0:rows, 0:16]), r=[px], w=[Z])
            P.op(ACT, lambda e: e.copy(z3[:, :, 1:E], pb[0:rows, 0:NT].rearrange("p (s t) -> p s t", s=nseq)), r=[pb], w=[Z])
            if kind == "p":
                if not last_blk:
                    P.op(ACT, lambda e: e.copy(shcar[0:rows, ci:ci + 1], z3[:, 0, L:E]), r=[Z], w=[shcar])
                else:
                    P.dma(shift_p.t[l, c0:c0 + rows].rearrange("(c o) -> c o", o=1), z3[:, 0, L:E], r=[Z], w=[shift_p], out=True,
                          allow_slow_non_contiguous=True)
            else:
                P.op(ACT, lambda e: e.copy(T[3][0:rows, 0:16], z3[:, :, L]), r=[Z], w=[T[3]])
                P.tr(px[0:16, 0:rows], T[3][0:rows, 0:16], CF("ident", rows, 0, rows), r=[T[3], cst], w=[px])
                P.op(ACT, lambda e: e.copy(xb[0:16, c0:c0 + rows], px[0:16, 0:rows]), r=[px], w=[xb])
            t4 = T[4][0:rows, :].rearrange("p (s t) -> p s t", s=nseq)
            mu = pcol[0:rows, mu0 + ci:mu0 + ci + 1]
            P.op(DVE, lambda e: e.tensor_tensor(t4, z3[:, :, 0:L], z3[:, :, 1:E], ALU.subtract), r=[Z], w=[T[4]])
            P.op(DVE, lambda e: e.scalar_tensor_tensor(z3[:, :, 1:E], t4, mu, z3[:, :, 1:E], ALU.mult, ALU.add),
                 r=[T[4], pcol, Z], w=[Z])

        if kind == "s":
            P.dma(xa[0:16, 0:RC], st_shift.t[l], r=[st_shift], w=[xa])
        sl_ = load_w(W["w_in"], l, 0, 16, 1536, 160)
        for (q0, rows, ci, func, dst) in ((0, 32, 12, AF.Tanh, txw), (32, 32, 13, AF.Copy, xab), (64, 96, 14, AF.Sigmoid, sgb)):
            pb = next_pbig()
            for kc in range(16):
                P.mm(pb[0:rows, 0:NT], sl_[:, kc, q0:q0 + rows], hT[:, kc, 0:NT], kc == 0, kc == 15, r=[sl_, hT], w=[pb])
            lerp(Zl, rows, ci, 1536 + q0, pb)
            zcv = zc(Zl, rows)
            P.op(ACT, lambda e, func=func, dst=dst, zcv=zcv, rows=rows: e.activation(c3(dst[0:rows, :]), zcv, func), r=[Zl], w=[dst])
        sr = load_w(W["w_in"], l, 0, 16, 0, 512)
        sk = load_w(W["w_in"], l, 0, 16, 512, 512)
        sv = load_w(W["w_in"], l, 0, 16, 1024, 512)
        bo_b = CB("blockones")
        for j in range(4):
            js = slice(j * 128, (j + 1) * 128)
            for (Z, s_, ci) in ((Zr, sr, j), (Zk, sk, 4 + j), (Zv, sv, 8 + j)):
                pb = next_pbig()
                for kc in range(16):
                    P.mm(pb[:, 0:NT], s_[:, kc, js], hT[:, kc, 0:NT], kc == 0, kc == 15, r=[s_, hT], w=[pb])
                lerp(Z, 128, ci, ci * 128, pb)
            rm, km, vm = zc(Zr), zc(Zk), zc(Zv)
            T0, T1, T2, T3, T4 = T
            col = lambda nm: pcol[:, PC[nm] + j:PC[nm] + j + 1]
            P.mm(px[:, 0:NT], lw[:, js], txw[:, :], True, True, r=[lw, txw], w=[px])
            P.op(ACT, lambda e, b=col("w0"): e.activation(T0[:, :], px[:, 0:NT], AF.Sigmoid, bias=b), r=[px, pcol], w=[T0])
            P.op(DVE, lambda e: e.tensor_scalar(T0[:, :], T0[:, :], NEG_E05, None, ALU.mult), r=[T0], w=[T0])
            P.op(DVE, lambda e: e.tensor_tensor_scan(T1[:, :], rst[:, 0:NT], T0[:, :], 0.0, ALU.mult, ALU.add), r=[cst, T0], w=[T1])
            P.op(ACT, lambda e, j=j: e.activation(gC[j][:, 0:nch], c3(T1[:, :])[:, :, C - 1], AF.Exp), r=[T1], w=[gC[j]])
            P.mm(px[:, 0:NT], la[:, js], xab[:, :], True, True, r=[la, xab], w=[px])
            P.op(ACT, lambda e, b=col("a0"): e.activation(T2[:, :], px[:, 0:NT], AF.Sigmoid, bias=b), r=[px, pcol], w=[T2])
            P.mm(px[:, 0:NT], lg[:, js], sgb[:, :], True, True, r=[lg, sgb], w=[px])
            P.op(ACT, lambda e, j=j: e.copy(gsb[j][:, :], px[:, 0:NT]), r=[px], w=[gsb[j]])
            P.op(DVE, lambda e, km=km, s_=col("k_k"): e.tensor_scalar(c3(T3[:, :]), km, s_, None, ALU.mult), r=[Zk, pcol], w=[T3])
            P.op(DVE, lambda e: e.tensor_tensor(sqb[:, :], T3[:, :], T3[:, :], ALU.mult), r=[T3], w=[sqb])
            P.mm(px[:, 0:NT], bo_b, sqb[:, :], True, True, r=[cstb, sqb], w=[px])
            P.op(ACT, lambda e: e.activation(T4[:, :], px[:, 0:NT], AF.Sqrt), r=[px], w=[T4])
            P.op(DVE, lambda e: e.tensor_scalar(T4[:, :], T4[:, :], 1e-12, None, ALU.max), r=[T4], w=[T4])
            P.op(DVE, lambda e: e.reciprocal(T4[:, :], T4[:, :]), r=[T4], w=[T4])
            P.op(DVE, lambda e: e.tensor_tensor(T3[:, :], T3[:, :], T4[:, :], ALU.mult), r=[T3, T4], w=[T3])
            P.op(DVE, lambda e, s1=col("k_a"), s2=col("omk"): e.tensor_scalar(T4[:, :], T2[:, :], s1, s2, ALU.mult, ALU.add),
                 r=[T2, pcol], w=[T4])
            P.op(DVE, lambda e, km=km: e.tensor_tensor(km, km, c3(T4[:, :]), ALU.mult), r=[Zk, T4], w=[Zk])
            P.op(DVE, lambda e: e.tensor_tensor(T4[:, :], T3[:, :], T2[:, :], ALU.mult), r=[T3, T2], w=[T4])
            P.op(DVE, lambda e, rm=rm, km=km: e.tensor_tensor(c3(T2[:, :]), rm, km, ALU.mult), r=[Zr, Zk], w=[T2])
            P.op(DVE, lambda e, s_=col("r_k"): e.tensor_scalar(sqb[:, :], T2[:, :], s_, None, ALU.mult), r=[T2, pcol], w=[sqb])
            P.mm(px[:, 0:NT], bo_b, sqb[:, :], True, True, r=[cstb, sqb], w=[px])
            P.op(DVE, lambda e, j=j, vm=vm: e.tensor_tensor(c3(bon[j][:, :]), c3(px[:, 0:NT]), vm, ALU.mult), r=[px, Zv], w=[bon[j]])
            P.op(ACT, lambda e, j=j, vm=vm: e.copy(c3(Vb[j][:, :]), vm), r=[Zv], w=[Vb[j]])
            AR4 = AR[j][:, :].rearrange("p (n two c) -> p n two c", two=2, c=C)
            BK4 = BK[j][:, :].rearrange("p (n two c) -> p n two c", two=2, c=C)
            P.op(ACT, lambda e: e.activation(T2[:, :], T1[:, :], AF.Exp), r=[T1], w=[T2])
            P.op(DVE, lambda e, rm=rm, o=AR4[:, :, 1, :]: e.tensor_tensor(o, rm, c3(T2[:, :]), ALU.mult), r=[Zr, T2], w=[AR[j]])
            P.op(ACT, lambda e: e.activation(T2[:, :], T1[:, :], AF.Exp, scale=-1.0), r=[T1], w=[T2])
            P.op(DVE, lambda e, o=BK4[:, :, 0, :]: e.tensor_tensor(o, c3(T4[:, :]), c3(T2[:, :]), ALU.mult), r=[T4, T2], w=[BK[j]])
            P.op(DVE, lambda e, km=km, o=BK4[:, :, 1, :]: e.tensor_tensor(o, km, c3(T2[:, :]), ALU.mult), r=[Zk, T2], w=[BK[j]])
            P.op(DVE, lambda e: e.tensor_tensor(T2[:, :], T1[:, :], T0[:, :], ALU.subtract), r=[T1, T0], w=[T2])
            P.op(ACT, lambda e: e.activation(T2[:, :], T2[:, :], AF.Exp), r=[T2], w=[T2])
            P.op(DVE, lambda e, o=AR4[:, :, 0, :]: e.scalar_tensor_tensor(o, c3(T3[:, :]), -1.0, c3(T2[:, :]), ALU.mult, ALU.mult),
                 r=[T3, T2], w=[AR[j]])
            cum3 = c3(T1[:, :])
            P.op(DVE, lambda e, cum3=cum3: e.tensor_tensor(c3(T2[:, :]), cum3[:, :, C - 1:C].to_broadcast([128, nch, C]), cum3, ALU.subtract),
                 r=[T1], w=[T2])
            P.op(ACT, lambda e: e.activation(T2[:, :], T2[:, :], AF.Exp), r=[T2], w=[T2])
            P.op(DVE, lambda e, j=j: e.tensor_tensor(Bh[j][:, :], T4[:, :], T2[:, :], ALU.mult), r=[T4, T2], w=[Bh[j]])
            P.op(DVE, lambda e, j=j, km=km: e.tensor_tensor(c3(Kh[j][:, :]), km, c3(T2[:, :]), ALU.mult), r=[Zk, T2], w=[Kh[j]])

        if kind == "s":
            P.dma(shift_s.t[l], xb[0:16, 0:RC], r=[xb], w=[shift_s], out=True)
        if _RW_LEVEL < 2:
            for j in range(4):
                P.op(DVE, lambda e, j=j: e.tensor_scalar(yT[:, j, 0:NT], hT[:, j, 0:NT], 0.0, None, ALU.mult), r=[hT], w=[yT])
            return
        arena_fence()
        aoff[0] = mark
        TMB = [av("r_TMB%d" % i, 64, 512, BF16) for i in range(2)]
        TMK = [av("r_TMK%d" % i, 64, 512, BF16) for i in range(2)]
        TMV = [av("r_TMV%d" % i, 64, 512, BF16) for i in range(2)]
        A1s = av("r_A1s", 64, 8 * C2, BF16)
        A2s = av("r_A2s", 64, 8 * C2, BF16)
        NTs = av("r_NTs", 64, 8 * C, BF16)
        Xb = [av("r_X%d" % i, 64, 8 * C, BF16) for i in range(2)]
        XTb = [av("r_XT%d" % i, 64, 8 * C, BF16) for i in range(2)]
        Tm = av("r_Tm", 64, 8 * C, BF16)
        TTm = av("r_TTm", 64, 8 * C, BF16)
        XtS = av("r_XtS", 64, 512, BF16)
        UtS = av("r_UtS", 64, 512, BF16)
        ysb = av("r_ysb", 64, 512)
        ynb = av("r_ynb", 64, 512, BF16)
        gst = av("r_gst", 64, 32)
        Sld = av("r_Sld", 64, 512)
        Sout = Sld
        fin = av("r_fin", 128, 512)
        ysq = fin
        su_m = CF("SU", C, 0, C).unsqueeze(1)
        iu_m = CF("IU", C, 0, C).unsqueeze(1)
        sl_m = CF("SL", C, 0, C).unsqueeze(1)
        id_m = CF("ident", C, 0, C).unsqueeze(1)
        idb = CB("ident")
        nlev = {64: 5, 4: 1}[C] if _RW_LEVEL >= 3 else 0
        hv = lambda ap, w_: ap.rearrange("p (h c) -> p h c", c=w_)
        gCo = av("r_gCo", 64, 64)
        ARo = [hT[0:64, 2 * j:2 * j + 2, :].rearrange("p a b -> p (a b)") for j in range(4)]
        BKo = [hT[0:64, 8 + 2 * j:10 + 2 * j, :].rearrange("p a b -> p (a b)") for j in range(4)]
        for j in range(4):
            P.dma(ARo[j][:, 0:2 * NT], AR[j][64:128, :], r=[AR[j]], w=[hT])
            P.dma(BKo[j][:, 0:2 * NT], BK[j][64:128, :], r=[BK[j]], w=[hT])
            P.dma(gCo[:, j * 16:j * 16 + nch], gC[j][64:128, 0:nch], r=[gC[j]], w=[gCo])

        def opA(j, e_, c0, c1):
            return (AR[j][0:64, c0:c1], AR[j]) if e_ == 0 else (ARo[j][:, c0:c1], hT)

        def opB(j, e_, c0, c1):
            return (BK[j][0:64, c0:c1], BK[j]) if e_ == 0 else (BKo[j][:, c0:c1], hT)

        def decay(j, e_, c):
            return (gC[j][0:64, c:c + 1], gC[j]) if e_ == 0 else (gCo[:, j * 16 + c:j * 16 + c + 1], gCo)

        for c in range(nch):
            cs = slice(c * C, (c + 1) * C)
            tb, tk, tv = TMB[c % 2], TMK[c % 2], TMV[c % 2]
            if kind == "s":
                P.dma(Sld[:, :].rearrange("p (h k) -> p h k", h=8), st_wkv.t[l, c].rearrange("h v k -> v h k"), r=[st_wkv], w=[Sld])
                for h in range(8):
                    P.tr(px[0:64, h * 64:(h + 1) * 64], Sld[:, h * 64:(h + 1) * 64], CF("ident", 64, 0, 64), r=[Sld, cst], w=[px])
                P.op(DVE, lambda e: e.tensor_copy(rS[:, :, :], hv(px[0:64, :], 64)), r=[px], w=[rS])
                P.op(ACT, lambda e: e.copy(rSb[:, :, :], rS[:, :, :]), r=[rS], w=[rSb])
            elif tok0 == 0 and c == 0:
                P.op(DVE, lambda e: e.memset(rS[:, :, :], 0.0), w=[rS])
                P.op(ACT, lambda e: e.copy(rSb[:, :, :], rS[:, :, :]), r=[rS], w=[rSb])
            for (srcs, dst, eng) in ((Bh, tb, DVE), (Kh, tk, ACT), (Vb, tv, DVE)):
                pt = next_ptr()
                for j in range(4):
                    P.tr(pt[0:C, j * 128:(j + 1) * 128], srcs[j][:, cs], idb, r=[srcs[j], cstb], w=[pt])
                if eng == DVE:
                    P.op(DVE, lambda e, pt=pt, dst=dst: e.tensor_copy(dst[0:C, :], pt[0:C, :]), r=[pt], w=[dst])
                else:
                    P.op(ACT, lambda e, pt=pt, dst=dst: e.copy(dst[0:C, :], pt[0:C, :]), r=[pt], w=[dst])
            if _RW_LEVEL < 2.1:
                continue
            pA1 = [pbig[0], pbig[1]]
            pA2 = [pbig[2], pg]
            for h in range(8):
                j, e_ = h // 2, h % 2
                ar, arR = opA(j, e_, c * C2, (c + 1) * C2)
                bt, bkR = opB(j, e_, c * C2, c * C2 + C)
                kt_, _ = opB(j, e_, c * C2 + C, (c + 1) * C2)
                at_, _ = opA(j, e_, c * C2, c * C2 + C)
                hh = h % 4
                P.mm(pA1[h // 4][0:C, hh * C2:(hh + 1) * C2], bt, ar, True, True, r=[bkR, arR], w=[pA1[h // 4]])
                P.mm(pA2[h // 4][0:C, hh * C2:(hh + 1) * C2], kt_, ar, True, True, r=[bkR, arR], w=[pA2[h // 4]])
                P.mm(pu[0:C, h * C:(h + 1) * C], at_, bt, True, True, r=[arR, bkR], w=[pu])
            if _RW_LEVEL < 2.12:
                continue
            for (ps2, dsts) in ((pA1, A1s), (pA2, A2s)):
                for half in range(2):
                    src3 = hv(ps2[half][0:C, 0:4 * C2], C2)
                    dst3 = hv(dsts[0:C, half * 4 * C2:(half + 1) * 4 * C2], C2)
                    P.op(DVE, lambda e, src3=src3, dst3=dst3: e.tensor_tensor(dst3[:, :, 0:C], src3[:, :, 0:C],
                                                                             su_m.to_broadcast([C, 4, C]), ALU.mult),
                         r=[ps2[half], cst], w=[dsts])
                    P.op(DVE, lambda e, src3=src3, dst3=dst3: e.tensor_tensor(dst3[:, :, C:C2], src3[:, :, C:C2],
                                                                             iu_m.to_broadcast([C, 4, C]), ALU.mult),
                         r=[ps2[half], cst], w=[dsts])
            P.op(DVE, lambda e: e.tensor_tensor(hv(NTs[0:C, :], C), hv(pu[0:C, 0:8 * C], C), sl_m.to_broadcast([C, 8, C]), ALU.mult),
                 r=[pu, cst], w=[NTs])
            if _RW_LEVEL < 2.2:
                continue
            A1v = hv(A1s[0:C, :], C2)
            P.op(DVE, lambda e: e.tensor_tensor(hv(Tm[0:C, :], C), A1v[:, :, 0:C], id_m.to_broadcast([C, 8, C]), ALU.add),
                 r=[A1s, cst], w=[Tm])
            P.op(DVE, lambda e: e.tensor_tensor(hv(TTm[0:C, :], C), hv(NTs[0:C, :], C), id_m.to_broadcast([C, 8, C]), ALU.add),
                 r=[NTs, cst], w=[TTm])
            Xc = (A1s, lambda h: A1s[0:C, h * C2:h * C2 + C])
            XTc = (NTs, lambda h: NTs[0:C, h * C:(h + 1) * C])
            for lev in range(nlev):
                Xn, XTn = Xb[lev % 2], XTb[lev % 2]
                lastl = (lev == nlev - 1)
                for h in range(8):
                    P.mm(px[0:C, h * C:(h + 1) * C], XTc[1](h), Xc[1](h), True, True, r=[XTc[0], Xc[0]], w=[px])
                P.op(ACT, lambda e, Xn=Xn: e.copy(Xn[0:C, :], px[0:C, 0:8 * C]), r=[px], w=[Xn])
                if not lastl:
                    for h in range(8):
                        P.mm(pu[0:C, h * C:(h + 1) * C], Xc[1](h), XTc[1](h), True, True, r=[XTc[0], Xc[0]], w=[pu])
                    P.op(DVE, lambda e, XTn=XTn: e.tensor_copy(XTn[0:C, :], pu[0:C, 0:8 * C]), r=[pu], w=[XTn])
                for h in range(8):
                    P.mm(pg[0:C, h * C:(h + 1) * C], TTm[0:C, h * C:(h + 1) * C], Xn[0:C, h * C:(h + 1) * C], True, True,
                         r=[TTm, Xn], w=[pg])
                if not lastl:
                    pq = pbig[lev % 3]
                    for h in range(8):
                        P.mm(pq[0:C, h * C:(h + 1) * C], Xn[0:C, h * C:(h + 1) * C], TTm[0:C, h * C:(h + 1) * C], True, True,
                             r=[TTm, Xn], w=[pq])
                P.op(DVE, lambda e: e.tensor_tensor(Tm[0:C, :], Tm[0:C, :], pg[0:C, 0:8 * C], ALU.add), r=[Tm, pg], w=[Tm])
                if not lastl:
                    P.op(DVE, lambda e, pq=pq: e.tensor_tensor(TTm[0:C, :], TTm[0:C, :], pq[0:C, 0:8 * C], ALU.add), r=[TTm, pq], w=[TTm])
                    Xc = (Xn, lambda h, Xn=Xn: Xn[0:C, h * C:(h + 1) * C])
                    XTc = (XTn, lambda h, XTn=XTn: XTn[0:C, h * C:(h + 1) * C])
            if _RW_LEVEL < 2.3:
                continue
            for h in range(8):
                j, e_ = h // 2, h % 2
                at_, arR = opA(j, e_, c * C2, c * C2 + C)
                P.mm(px[0:C, h * 64:(h + 1) * 64], at_, rSb[:, h, :], True, False, r=[arR, rSb], w=[px])
                P.mm(px[0:C, h * 64:(h + 1) * 64], A2s[0:C, h * C2:h * C2 + C], tv[0:C, h * 64:(h + 1) * 64], False, True,
                     r=[A2s, tv], w=[px])
            P.op(ACT, lambda e: e.copy(XtS[0:C, :], px[0:C, :]), r=[px], w=[XtS])
            for h in range(8):
                P.mm(pu[0:C, h * 64:(h + 1) * 64], Tm[0:C, h * C:(h + 1) * C], XtS[0:C, h * 64:(h + 1) * 64], True, True,
                     r=[Tm, XtS], w=[pu])
            P.op(DVE, lambda e: e.tensor_copy(UtS[0:C, :], pu[0:C, :]), r=[pu], w=[UtS])
            if _RW_LEVEL < 2.4:
                continue
            for h in range(8):
                j, e_ = h // 2, h % 2
                rt_, arR = opA(j, e_, c * C2 + C, (c + 1) * C2)
                o_ = pg[0:C, h * 64:(h + 1) * 64]
                P.mm(o_, rt_, rSb[:, h, :], True, False, r=[arR, rSb], w=[pg])
                P.mm(o_, A1s[0:C, h * C2 + C:(h + 1) * C2], UtS[0:C, h * 64:(h + 1) * 64], False, False, r=[A1s, UtS], w=[pg])
                P.mm(o_, A2s[0:C, h * C2 + C:(h + 1) * C2], tv[0:C, h * 64:(h + 1) * 64], False, True, r=[A2s, tv], w=[pg])
            P.op(ACT, lambda e: e.copy(ysb[0:C, :], pg[0:C, :]), r=[pg], w=[ysb])
            if _RW_LEVEL < 2.5:
                continue
            pS = pbig[c % 3]
            for h in range(8):
                hs = slice(h * 64, (h + 1) * 64)
                P.mm(pS[0:64, hs], tb[0:C, hs], UtS[0:C, hs], True, False, r=[tb, UtS], w=[pS])
                P.mm(pS[0:64, hs], tk[0:C, hs], tv[0:C, hs], False, True, r=[tk, tv], w=[pS])
            for h in range(8):
                dc, dcR = decay(h // 2, h % 2, c)
                P.op(DVE, lambda e, h=h, pS=pS, dc=dc: e.scalar_tensor_tensor(
                    rS[:, h, :], rS[:, h, :], dc, pS[0:64, h * 64:(h + 1) * 64], ALU.mult, ALU.add), r=[rS, dcR, pS], w=[rS])
            P.op(ACT, lambda e: e.copy(rSb[:, :, :], rS[:, :, :]), r=[rS], w=[rSb])
            if _RW_LEVEL < 2.6:
                continue
            y3 = hv(ysb[0:C, :], 64)
            P.op(DVE, lambda e, y3=y3: e.tensor_reduce(gst[0:C, 0:8], y3, AX.X, ALU.add), r=[ysb], w=[gst])
            P.op(DVE, lambda e: e.tensor_tensor(ysq[0:C, :], ysb[0:C, :], ysb[0:C, :], ALU.mult), r=[ysb], w=[ysq])
            P.op(DVE, lambda e: e.tensor_reduce(gst[0:C, 8:16], hv(ysq[0:C, :], 64), AX.X, ALU.add), r=[ysq], w=[gst])
            P.op(DVE, lambda e: e.tensor_scalar(gst[0:C, 16:32], gst[0:C, 0:16], 1.0 / 64, None, ALU.mult), r=[gst], w=[gst])
            P.op(DVE, lambda e: e.tensor_tensor(gst[0:C, 0:8], gst[0:C, 16:24], gst[0:C, 16:24], ALU.mult), r=[gst], w=[gst])
            P.op(DVE, lambda e: e.tensor_tensor(gst[0:C, 8:16], gst[0:C, 24:32], gst[0:C, 0:8], ALU.subtract), r=[gst], w=[gst])
            P.op(DVE, lambda e: e.tensor_scalar(gst[0:C, 8:16], gst[0:C, 8:16], 64e-5, None, ALU.add), r=[gst], w=[gst])
            P.op(ACT, lambda e: e.activation(gst[0:C, 0:8], gst[0:C, 8:16], AF.Sqrt), r=[gst], w=[gst])
            P.op(DVE, lambda e: e.reciprocal(gst[0:C, 8:16], gst[0:C, 0:8]), r=[gst], w=[gst])
            P.op(DVE, lambda e, y3=y3: e.tensor_tensor(y3, y3, gst[0:C, 16:24].unsqueeze(2).to_broadcast([C, 8, 64]), ALU.subtract),
                 r=[ysb, gst], w=[ysb])
            P.op(DVE, lambda e, y3=y3: e.tensor_tensor(hv(ynb[0:C, :], 64), y3, gst[0:C, 8:16].unsqueeze(2).to_broadcast([C, 8, 64]), ALU.mult),
                 r=[ysb, gst], w=[ynb])
            pt = next_ptr()
            for j in range(4):
                P.tr(pt[:, j * C:(j + 1) * C], ynb[0:C, j * 128:(j + 1) * 128], CB("ident", C, 0, C), r=[ynb, cstb], w=[pt])
            P.op(ACT, lambda e, pt=pt, cs=cs: e.copy(YN[:, :].rearrange("p (j t) -> p j t", j=4)[:, :, cs], hv(pt[:, 0:4 * C], C)),
                 r=[pt], w=[YN])
            if _RW_LEVEL < 2.7:
                continue
            if kind == "s" or (last_blk and c == nch - 1):
                for h in range(8):
                    P.tr(px[0:64, h * 64:(h + 1) * 64], rS[:, h, :], CF("ident", 64, 0, 64), r=[rS, cst], w=[px])
                P.op(ACT, lambda e: e.copy(Sout[:, :], px[0:64, :]), r=[px], w=[Sout])
                dst_ = (wkv_s.t[l, c] if kind == "s" else wkv_p.t[l]).rearrange("h v k -> v h k")
                P.dma(dst_, Sout[:, :].rearrange("p (h k) -> p h k", h=8), r=[Sout], w=[wkv_s if kind == "s" else wkv_p], out=True)
        for j in range(4):
            col = lambda nm: pcol[:, PC[nm] + j:PC[nm] + j + 1]
            P.op(DVE, lambda e, j=j, s1=col("gn_w"), s2=col("gn_b"): e.tensor_scalar(fin[:, 0:NT], YN[:, j * NT:(j + 1) * NT], s1, s2,
                                                                                  ALU.mult, ALU.add), r=[YN, pcol], w=[fin])
            P.op(DVE, lambda e, j=j: e.tensor_tensor(fin[:, 0:NT], fin[:, 0:NT], bon[j][:, :], ALU.add), r=[fin, bon[j]], w=[fin])
            P.op(DVE, lambda e, j=j: e.tensor_tensor(yT[:, j, 0:NT], fin[:, 0:NT], gsb[j][:, :], ALU.mult), r=[fin, gsb[j]], w=[yT])

    blocks = [("p", i * 512, 512) for i in range(4)] + [("s", 0, 64)]
    if mini:
        blocks = mini
    for l in range(1 if mini else DEPTH):
        arena_fence()
        arena_reset()
        layer_params(l)
        for (kind, tok0, NT) in blocks:
            tiles = [(i * 128, 128) for i in range(NT // 128)] if kind == "p" else [(0, 64)]
            if l == 0:
                xsrc = xp if kind == "p" else xs
            else:
                xsrc = xmid_p if kind == "p" else xmid_s
            if l == DEPTH - 1:
                xdst = y_p if kind == "p" else y_s
            else:
                xdst = xmid_p if kind == "p" else xmid_s
            pl = pp if kind == "p" else psm
            bcast_load(gbc, W["ln_mix_pre"], l)
            for (o, m) in tiles:
                P.dma(xa[0:m, :], xsrc[tok0 + o:tok0 + o + m, :], r=[xsrc], w=[xa])
                rs = rms_rstd(xa, xa[0:m, :], m, D, 0, xb, xb[0:m, :], 1.0 / D, epsn)
                P.op(DVE, lambda e, m=m, rs=rs: e.scalar_tensor_tensor(xnb[0:m, :], xa[0:m, :], rs, gbc[0:m, :],
                                                                    ALU.mult, ALU.mult), r=[xa, stat, gbc], w=[xnb])
                to_featmajor(xnb, lambda kc, m=m: xnb[0:m, kc * 128:(kc + 1) * 128], m, o, hT)
            if stage < 4:
                for kc in range(16):
                    P.op(DVE, lambda e, kc=kc: e.tensor_scalar(yT[:, kc, :], hT[:, kc, :], 0.0, None, ALU.mult), r=[hT], w=[yT])
            if mini:
                arena_fence()
                arena_reset()
                mixer_rwkv(l, kind, tok0, NT)
                arena_fence()
                for j in range(4):
                    P.op(ACT, lambda e, j=j, NT=NT: e.copy(xa[:, j * 512:j * 512 + NT], yT[:, j, 0:NT]), r=[yT], w=[xa])
                P.dma(dbg_d[0, :, :], xa[:, :], r=[xa], w=[dbg_d], out=True)
                continue
            if stage >= 1:
                arena_fence()
                arena_reset()
                mixer_pool(l, kind, tok0, NT)
            if stage >= 2:
                arena_fence()
                arena_reset()
                mixer_sgu(l, kind, tok0, NT, tiles)
            if stage >= 3:
                arena_fence()
                arena_reset()
                mixer_hgrn(l, kind, tok0, NT)
            if stage >= 4:
                arena_fence()
                arena_reset()
                mixer_rwkv(l, kind, tok0, NT)
            arena_fence()
            for db in range(4):
                s = load_w(W["w_out"], l, 0, 16, db * 512, 512)
                for ti, (o, m) in enumerate(tiles):
                    pb = next_pbig()
                    for kc in range(16):
                        P.mm(pb[0:m, :], yT[:, kc, o:o + m], s[:, kc, :], kc == 0, kc == 15, r=[yT, s], w=[pb])
                    P.op(ACT, lambda e, pb=pb, ti=ti, db=db, m=m: e.copy(big[0:m, ti, db * 512:(db + 1) * 512], pb[0:m, :]),
                         r=[pb], w=[big])
            bcast_load(gbc, W["ln_mix_post"], l)
            for ti, (o, m) in enumerate(tiles):
                dbg_on = False
                if dbg_on:
                    P.dma(dbg_d[0, :, :], big[:, 0, :], r=[big], w=[dbg_d])
                    for kc in range(4):
                        P.op(ACT, lambda e, kc=kc: e.copy(xa[:, kc * 512:(kc + 1) * 512], yT[:, kc * 4, :]), r=[yT], w=[xa])
                    P.dma(dbg_d[5, :, :], xa[:, :], r=[xa], w=[dbg_d])
                rs = rms_rstd(big, big[0:m, ti, :], m, D, 0, xb, xb[0:m, :], 1.0 / D, epsn)
                P.dma(xa[0:m, :], xsrc[tok0 + o:tok0 + o + m, :], r=[xsrc], w=[xa])
                P.op(DVE, lambda e, m=m, ti=ti, rs=rs: e.scalar_tensor_tensor(xb[0:m, :], big[0:m, ti, :], rs, gbc[0:m, :],
                                                                          ALU.mult, ALU.mult), r=[big, stat, gbc], w=[xb])
                if dbg_on:
                    P.dma(dbg_d[1, :, :], xb[:, :], r=[xb], w=[dbg_d])
                    P.dma(dbg_d[2, :, :], xa[:, :], r=[xa], w=[dbg_d])
                    P.dma(dbg_d[4, :, 0:8], stat[:, :], r=[stat], w=[dbg_d])
                P.op(DVE, lambda e, m=m, ti=ti: e.tensor_tensor(big[0:m, ti, :], xa[0:m, :], xb[0:m, :], ALU.add),
                     r=[xa, xb], w=[big])
                if dbg_on:
                    P.dma(dbg_d[3, :, :], big[:, 0, :], r=[big], w=[dbg_d])
                P.dma(xf1_d[o:o + m, :], big[0:m, ti, :], r=[big], w=[xf1_d])
            bcast_load(gbc, W["ln_ffn_pre"], l)
            for ti, (o, m) in enumerate(tiles):
                rs = rms_rstd(big, big[0:m, ti, :], m, D, 4, xb, xb[0:m, :], 1.0 / D, epsn)
                P.op(DVE, lambda e, m=m, ti=ti, rs=rs: e.scalar_tensor_tensor(xnb[0:m, :], big[0:m, ti, :], rs, gbc[0:m, :],
                                                                          ALU.mult, ALU.mult), r=[big, stat, gbc], w=[xnb])
                to_featmajor(xnb, lambda kc, m=m: xnb[0:m, kc * 128:(kc + 1) * 128], m, o, hT)
            dbg4 = _DEBUG and l == 0 and kind == "p" and tok0 == 0
            if dbg4:
                P.op(ACT, lambda e: e.copy(xa[:, 0:512], hT[:, 0, :]), r=[hT], w=[xa])
                P.op(ACT, lambda e: e.copy(xa[:, 512:1024], hT[:, 5, :]), r=[hT], w=[xa])
                P.dma(dbg_d[0, :, 0:1024], xa[:, 0:1024], r=[xa], w=[dbg_d])
            arena_fence()
            ppairs = [(pg, pu), (pbig[0], pbig[1])]
            sbufs = [silu_t, silu_b]
            for fs in range(DFF // 256):
                sgu_ = wslot[slot_rr[0]]
                slot_rr[0] = (slot_rr[0] + 1) % NSLOT
                wsrc = W["ffn_w_gu"].t[l].rearrange("(kc p) c -> p kc c", p=128)
                P.dma(sgu_[:, :, 0:256], wsrc[:, :, fs * 256:(fs + 1) * 256], r=[W["ffn_w_gu"]], w=[sgu_], eng=POOL)
                P.dma(sgu_[:, :, 256:512], wsrc[:, :, DFF + fs * 256:DFF + (fs + 1) * 256], r=[W["ffn_w_gu"]], w=[sgu_], eng=POOL)
                for fc in range(2):
                    fidx = fs * 2 + fc
                    pgx, pux = ppairs[fidx % 2]
                    sbx = sbufs[fidx % 2]
                    for kc in range(16):
                        P.mm(pgx[:, 0:NT], sgu_[:, kc, fc * 128:(fc + 1) * 128], hT[:, kc, 0:NT], kc == 0, kc == 15,
                             r=[sgu_, hT], w=[pgx])
                    for kc in range(16):
                        P.mm(pux[:, 0:NT], sgu_[:, kc, 256 + fc * 128:256 + (fc + 1) * 128], hT[:, kc, 0:NT], kc == 0, kc == 15,
                             r=[sgu_, hT], w=[pux])
                    P.op(ACT, lambda e, NT=NT, sbx=sbx, pgx=pgx: e.activation(sbx[:, 0:NT], pgx[:, 0:NT], AF.Silu), r=[pgx], w=[sbx])
                    P.op(DVE, lambda e, fidx=fidx, NT=NT, sbx=sbx, pux=pux: e.tensor_tensor(actT[:, fidx, 0:NT], sbx[:, 0:NT], pux[:, 0:NT], ALU.mult),
                         r=[sbx, pux], w=[actT])
            if dbg4:
                P.op(ACT, lambda e: e.copy(xa[:, 0:512], actT[:, 0, :]), r=[actT], w=[xa])
                P.op(ACT, lambda e: e.copy(xa[:, 512:1024], actT[:, 43, :]), r=[actT], w=[xa])
                P.op(ACT, lambda e: e.copy(xa[:, 1024:1536], silu_t[:, :]), r=[silu_t], w=[xa])
                P.op(ACT, lambda e: e.copy(xa[:, 1536:2048], pu[:, :]), r=[pu], w=[xa])
                P.dma(dbg_d[1, :, :], xa[:, :], r=[xa], w=[dbg_d])
            for db in range(4):
                ss = [load_w(W["ffn_w_down"], l, k0, nk, db * 512, 512) for (k0, nk) in ((0, 16), (16, 16), (32, 12))]
                for ti, (o, m) in enumerate(tiles):
                    pb = next_pbig()
                    for fc in range(44):
                        s = ss[fc // 16]
                        P.mm(pb[0:m, :], actT[:, fc, o:o + m], s[:, fc % 16, :], fc == 0, fc == 43, r=[actT, s], w=[pb])
                    P.op(ACT, lambda e, pb=pb, ti=ti, db=db, m=m: e.copy(big[0:m, ti, db * 512:(db + 1) * 512], pb[0:m, :]),
                         r=[pb], w=[big])
            arena_fence()
            bcast_load(gbc, W["ln_ffn_post"], l)
            for ti, (o, m) in enumerate(tiles):
                rs = rms_rstd(big, big[0:m, ti, :], m, D, 0, xb, xb[0:m, :], 1.0 / D, epsn)
                P.dma(xa[0:m, :], xf1_d[o:o + m, :], r=[xf1_d], w=[xa])
                dbg6 = False
                if dbg6:
                    P.dma(dbg_d[0, :, :], big[:, ti, :], r=[big], w=[dbg_d])
                    P.dma(dbg_d[2, :, :], xa[:, :], r=[xa], w=[dbg_d])
                P.op(DVE, lambda e, m=m, ti=ti, rs=rs: e.scalar_tensor_tensor(xb[0:m, :], big[0:m, ti, :], rs, gbc[0:m, :],
                                                                          ALU.mult, ALU.mult), r=[big, stat, gbc], w=[xb])
                if dbg6:
                    P.dma(dbg_d[1, :, :], xb[:, :], r=[xb], w=[dbg_d])
                    P.dma(dbg_d[4, :, 0:8], stat[:, :], r=[stat], w=[dbg_d])
                    P.dma(dbg_d[5, :, :], gbc[:, :], r=[gbc], w=[dbg_d])
                P.op(DVE, lambda e, m=m, ti=ti: e.tensor_tensor(big[0:m, ti, :], xa[0:m, :], xb[0:m, :], ALU.add),
                     r=[xa, xb], w=[big])
                if dbg6:
                    P.dma(dbg_d[3, :, :], big[:, ti, :], r=[big], w=[dbg_d])
                P.op(ACT, lambda e, m=m, ti=ti: e.copy(xnb[0:m, :], big[0:m, ti, :]), r=[big], w=[xnb])
                to_featmajor(xnb, lambda kc, m=m: xnb[0:m, kc * 128:(kc + 1) * 128], m, o, yT)
                P.dma(xa[0:m, 0:PLE], pl[l, tok0 + o:tok0 + o + m, :], r=[pl], w=[xa])
                P.op(ACT, lambda e, m=m: e.copy(xnb[0:m, 0:PLE], xa[0:m, 0:PLE]), r=[xa], w=[xnb])
                to_featmajor(xnb, lambda kc, m=m: xnb[0:m, kc * 128:(kc + 1) * 128], m, o, pT, nkc=2)
            for db in range(4):
                sgt = load_w(W["ple_gate"], l, 0, 16, db * 512, 512)
                spj = load_w(W["ple_proj"], l, 0, 2, db * 512, 512)
                for ti, (o, m) in enumerate(tiles):
                    pb = next_pbig()
                    for kc in range(16):
                        P.mm(pb[0:m, :], yT[:, kc, o:o + m], sgt[:, kc, :], kc == 0, kc == 15, r=[yT, sgt], w=[pb])
                    pb2 = next_pbig()
                    for kc in range(2):
                        P.mm(pb2[0:m, :], pT[:, kc, o:o + m], spj[:, kc, :], kc == 0, kc == 1, r=[pT, spj], w=[pb2])
                    P.op(ACT, lambda e, pb=pb, m=m: e.activation(silu_t[0:m, :], pb[0:m, :], AF.Sigmoid), r=[pb], w=[silu_t])
                    P.op(DVE, lambda e, pb2=pb2, m=m: e.tensor_tensor(silu_t[0:m, :], silu_t[0:m, :], pb2[0:m, :], ALU.mult),
                         r=[silu_t, pb2], w=[silu_t])
                    P.op(DVE, lambda e, m=m, ti=ti, db=db: e.tensor_tensor(big[0:m, ti, db * 512:(db + 1) * 512],
                                                                         big[0:m, ti, db * 512:(db + 1) * 512],
                                                                         silu_t[0:m, :], ALU.add), r=[big, silu_t], w=[big])
            for ti, (o, m) in enumerate(tiles):
                P.dma(xdst[tok0 + o:tok0 + o + m, :], big[0:m, ti, :], r=[big], w=[xdst], out=(l == DEPTH - 1))
    stats = P.emit()
    return nc, stats


def kernel(**inp):
    if "nc" not in _BUILT:
        _BUILT["nc"], _BUILT["stats"] = build()
    nc = _BUILT["nc"]
    f = lambda a: np.ascontiguousarray(np.asarray(a, dtype=np.float32))
    wkeys = ["ln_mix_pre", "ln_mix_post", "ln_ffn_pre", "ln_ffn_post", "w_in", "rwkv_mu", "rwkv_w_lora", "rwkv_w0",
             "rwkv_a_lora", "rwkv_a0", "rwkv_g_lora", "rwkv_k_k", "rwkv_k_a", "rwkv_r_k", "rwkv_gn_w", "rwkv_gn_b",
             "sgu_ln_w", "sgu_ln_b", "sgu_w", "sgu_b", "sgu_norm", "hgrn_lb_logits", "hgrn_norm", "pool_w",
             "pool_scale", "w_out", "ffn_w_gu", "ffn_w_down", "ple_gate", "ple_proj"]
    shared = {k: f(inp[k]) for k in wkeys}
    shared["rwkv_r_k"] = shared["rwkv_r_k"].reshape(DEPTH, G)
    shared["consts"] = CONSTS
    in_maps = []
    for c in range(8):
        b = c % 4
        sl = slice(c * NSEQ_S, (c + 1) * NSEQ_S)
        m = dict(shared)
        m["xp"] = f(inp["x_prompt"][b])
        m["xs"] = f(inp["x_sample"][sl]).reshape(64, D)
        m["pp"] = f(inp["p_prompt"][:, b])
        m["psm"] = f(inp["p_sample"][:, sl]).reshape(DEPTH, 64, PLE)
        m["st_wkv"] = f(inp["state_rwkv_wkv"][:, sl])
        m["st_shift"] = f(inp["state_rwkv_shift"][:, sl])
        m["st_hgrn"] = f(inp["state_hgrn"][:, sl])
        m["st_pool"] = f(inp["state_pool"][:, sl])
        in_maps.append(m)
    res = run_bass_kernel_spmd(nc, in_maps, core_ids=list(range(8)))
    R = res.results
    _BUILT["R"] = R
    y_prompt = np.stack([R[b]["y_p"] for b in range(4)], 0)
    y_sample = np.concatenate([R[c]["y_s"].reshape(NSEQ_S, LS, D) for c in range(8)], 0)
    pst = lambda k: np.stack([R[b][k] for b in range(4)], 1)
    sst = lambda k: np.concatenate([R[c][k] for c in range(8)], 1)
    out = (y_prompt, y_sample, pst("wkv_p"), pst("shift_p"), pst("hgrn_p"), pst("pool_p"),
           sst("wkv_s"), sst("shift_s"), sst("hgrn_s"), sst("pool_s"), sst("sguv_s"))
    return tuple(np.ascontiguousarray(o, dtype=np.float32) for o in out)
```

```python
import numpy as np
import concourse.bass as bass
import concourse.mybir as mybir
from concourse.bass_utils import run_bass_kernel_spmd

F32 = mybir.dt.float32
BF16 = mybir.dt.bfloat16
AF = mybir.ActivationFunctionType
ALU = mybir.AluOpType
AX = mybir.AxisListType
PE, DVE, ACT, POOL, SP = "tensor", "vector", "scalar", "gpsimd", "sync"

D = 2048
DEPTH = 2
SEQ = 2048
NSEQ_S = 16
LS = 4
G = 512
RC = 1696
IN_COLS = 5280
DFF = 5632
PLE = 256
C_R, C_S, C_H, C_P = 0, 1696, 2720, 4768
NEG_E05 = -0.6065306597126334


class Res:
    __slots__ = ("name", "t", "last_w", "readers", "const", "dma_w")

    def __init__(self, name, t, const=False):
        self.name = name
        self.t = t
        self.last_w = None
        self.readers = []
        self.const = const
        self.dma_w = []

    def __getitem__(self, idx):
        return self.t[idx]


class Op:
    __slots__ = ("eng", "fn", "deps", "signal", "sigidx", "is_dma", "sem", "target")

    def __init__(self, eng, fn, is_dma):
        self.eng = eng
        self.fn = fn
        self.deps = []
        self.signal = False
        self.sigidx = 0
        self.is_dma = is_dma
        self.sem = None
        self.target = 0


class Prog:
    NDMASEM = 48

    def __init__(self, nc):
        self.nc = nc
        self.ops = []
        self.dma_last = [None] * self.NDMASEM
        self.dma_uses = [0] * self.NDMASEM
        self.dma_rr = 0
        self.out_dmas = []

    def sb(self, name, shape, dt=F32):
        return Res(name, self.nc.alloc_sbuf_tensor(name, list(shape), dt))

    def ps(self, name, shape, dt=F32):
        return Res(name, self.nc.alloc_psum_tensor(name, list(shape), dt))

    def dram(self, name, shape, dt=F32, kind="Internal"):
        return Res(name, self.nc.dram_tensor(name, list(shape), dt, kind=kind).ap())

    def view(self, name, ap):
        return Res(name, ap)

    def op(self, eng, fn, r=(), w=(), dma=False, out=False):
        o = Op(eng, fn, dma)
        deps = {}
        for res in r:
            if res.last_w is not None:
                deps[id(res.last_w)] = (res.last_w, "RAW")
            for dw in res.dma_w:
                deps[id(dw)] = (dw, "RAW")
        for res in w:
            if res.last_w is not None and id(res.last_w) not in deps:
                deps[id(res.last_w)] = (res.last_w, "WAW")
            for dw in res.dma_w:
                if id(dw) not in deps:
                    deps[id(dw)] = (dw, "WAW")
            for rd in res.readers:
                if id(rd) not in deps:
                    deps[id(rd)] = (rd, "WAR")
        for p, kind in deps.values():
            if (not p.is_dma) and (not dma) and p.eng == eng:
                if eng == PE or kind != "RAW":
                    continue
            o.deps.append(p)
        if dma:
            j = self.dma_rr
            self.dma_rr = (j + 1) % self.NDMASEM
            prev = self.dma_last[j]
            if prev is not None and all(prev is not d for d in o.deps):
                o.deps.append(prev)
            self.dma_uses[j] += 1
            o.sem = j
            o.target = 16 * self.dma_uses[j]
            self.dma_last[j] = o
            if out:
                self.out_dmas.append(o)
        for res in r:
            if not res.const:
                if not dma:
                    res.readers = [x for x in res.readers if x.is_dma or x.eng != eng]
                res.readers.append(o)
        for res in w:
            res.last_w = o
            res.readers = []
            if dma:
                res.dma_w.append(o)
                if len(res.dma_w) > 40:
                    res.dma_w = res.dma_w[-40:]
            else:
                res.dma_w = []
        self.ops.append(o)
        return o

    def fence(self, eng, fn, ress):
        return self.op(eng, fn, r=(), w=list(ress))

    def mm(self, out_ap, lhsT, rhs, start, stop, r, w):
        return self.op(PE, lambda e: e.matmul(out_ap, lhsT, rhs, start=start, stop=stop), r=r, w=w)

    def tr(self, out_ap, in_ap, ident_ap, r, w):
        return self.op(PE, lambda e: e.transpose(out_ap, in_ap, ident_ap), r=r, w=w)

    def dma(self, out_ap, in_ap, r, w, eng=SP, out=False, **kw):
        return self.op(eng, lambda e: e.dma_start(out=out_ap, in_=in_ap, **kw), r=r, w=w, dma=True, out=out)

    def emit(self):
        nc = self.nc
        ops = self.ops
        fin = Op(SP, None, False)
        fin.deps = list(self.out_dmas)
        ops.append(fin)
        for o in ops:
            for d in o.deps:
                d.signal = True
        engs = [PE, DVE, ACT, POOL, SP]
        cnt = {e: 0 for e in engs}
        for o in ops:
            if o.signal and not o.is_dma:
                cnt[o.eng] += 1
                o.sigidx = cnt[o.eng]
        esem = {e: nc.alloc_semaphore("es_" + e) for e in engs}
        dsem = [nc.alloc_semaphore("ds_%d" % j) for j in range(self.NDMASEM)]
        per = {e: [o for o in ops if o.eng == e] for e in engs}
        NS = self.NDMASEM

        nw = {x: 0 for x in engs}

        def run(e, engobj):
            seen = {x: 0 for x in engs}
            seend = [0] * NS
            for o in per[e]:
                nw[e] += sum(1 for d in o.deps if ((seend[d.sem] < d.target) if d.is_dma else (seen[d.eng] < d.sigidx)))
                for d in o.deps:
                    if d.is_dma:
                        if seend[d.sem] >= d.target:
                            continue
                        engobj.wait_ge(dsem[d.sem], d.target)
                        seend[d.sem] = d.target
                    else:
                        if seen[d.eng] >= d.sigidx:
                            continue
                        engobj.wait_ge(esem[d.eng], d.sigidx)
                        seen[d.eng] = d.sigidx
                if o.fn is None:
                    continue
                ins = o.fn(engobj)
                if o.is_dma:
                    ins.then_inc(dsem[o.sem], 16)
                elif o.signal:
                    ins.then_inc(esem[e], 1)

        with nc.Block() as block:
            @block.tensor
            def _(e):
                run(PE, e)

            @block.vector
            def _(e):
                run(DVE, e)

            @block.scalar
            def _(e):
                run(ACT, e)

            @block.gpsimd
            def _(e):
                run(POOL, e)

            @block.sync
            def _(e):
                run(SP, e)
        return dict(n_ops=len(ops), per={e: len(per[e]) for e in engs}, sig=cnt, waits=nw)


CONST_COLS = {}


def _make_consts():
    cols = []
    off = [0]

    def add(name, arr):
        a = np.zeros((128, arr.shape[1]), np.float32)
        a[:arr.shape[0]] = arr
        CONST_COLS[name] = (off[0], arr.shape[1])
        off[0] += arr.shape[1]
        cols.append(a)

    add("ident", np.eye(128, dtype=np.float32))
    bo = np.zeros((128, 128), np.float32)
    bo[:64, :64] = 1
    bo[64:, 64:] = 1
    add("blockones", bo)
    add("ones", np.ones((128, 128), np.float32))
    i = np.arange(64)
    add("SU", (i[:, None] < i[None, :]).astype(np.float32))
    add("IU", (i[:, None] <= i[None, :]).astype(np.float32))
    add("SL", (i[:, None] > i[None, :]).astype(np.float32))
    r64 = np.ones((128, 512), np.float32)
    r64[:, ::64] = 0
    add("rst64", r64)
    r4 = np.ones((128, 64), np.float32)
    r4[:, ::4] = 0
    add("rst4", r4)
    t128 = np.arange(128)
    add("TRIL", (t128[:, None] >= t128[None, :]).astype(np.float32))
    pc = np.ones((128, 64), np.float32)
    for gi, win in enumerate((2, 4, 8, 16)):
        pos = np.arange(16)
        pc[:, gi * 16:(gi + 1) * 16] = (win / np.minimum(pos + 1, win))[None, :]
    add("poolc", pc)
    bd = np.zeros((64, 64), np.float32)
    for s in range(16):
        bd[4 * s:4 * s + 4, 4 * s:4 * s + 4] = np.tril(np.ones((4, 4))).T
    add("BD4T", bd.T.copy())
    return np.concatenate(cols, axis=1)


CONSTS = _make_consts()
NCONST = CONSTS.shape[1]

_BUILT = {}
_DEBUG = False
_RW_LEVEL = 3


def build(stage=99, mini=None):
    nc = bass.Bass("TRN2", target_bir_lowering=False)
    P = Prog(nc)
    EI, EO = "ExternalInput", "ExternalOutput"
    d_in = {}

    def din(name, shape):
        d_in[name] = P.dram(name, shape, F32, kind=EI)
        d_in[name].const = True
        return d_in[name]

    xp = din("xp", [SEQ, D])
    xs = din("xs", [64, D])
    pp = din("pp", [DEPTH, SEQ, PLE])
    psm = din("psm", [DEPTH, 64, PLE])
    st_wkv = din("st_wkv", [DEPTH, NSEQ_S, 8, 64, 64])
    st_shift = din("st_shift", [DEPTH, NSEQ_S, RC])
    st_hgrn = din("st_hgrn", [DEPTH, NSEQ_S, 4, 128, 128])
    st_pool = din("st_pool", [DEPTH, NSEQ_S, 15, G])
    consts_d = din("consts", [128, NCONST])
    wnames = dict(ln_mix_pre=[DEPTH, D], ln_mix_post=[DEPTH, D], ln_ffn_pre=[DEPTH, D], ln_ffn_post=[DEPTH, D],
                  w_in=[DEPTH, D, IN_COLS], rwkv_mu=[DEPTH, RC], rwkv_w_lora=[DEPTH, 32, G], rwkv_w0=[DEPTH, G],
                  rwkv_a_lora=[DEPTH, 32, G], rwkv_a0=[DEPTH, G], rwkv_g_lora=[DEPTH, 96, G], rwkv_k_k=[DEPTH, G],
                  rwkv_k_a=[DEPTH, G], rwkv_r_k=[DEPTH, G], rwkv_gn_w=[DEPTH, G], rwkv_gn_b=[DEPTH, G],
                  sgu_ln_w=[DEPTH, G], sgu_ln_b=[DEPTH, G], sgu_w=[DEPTH, 4, 128, 128], sgu_b=[DEPTH, 4, 128],
                  sgu_norm=[DEPTH, G], hgrn_lb_logits=[DEPTH, G], hgrn_norm=[DEPTH, G], pool_w=[DEPTH, 4, 128, 128],
                  pool_scale=[DEPTH, G], w_out=[DEPTH, D, D], ffn_w_gu=[DEPTH, D, 2 * DFF], ffn_w_down=[DEPTH, DFF, D],
                  ple_gate=[DEPTH, D, D], ple_proj=[DEPTH, PLE, D])
    W = {k: din(k, v) for k, v in wnames.items()}
    d_out = {}

    def dout(name, shape):
        d_out[name] = P.dram(name, shape, F32, kind=EO)
        return d_out[name]

    y_p = dout("y_p", [SEQ, D])
    y_s = dout("y_s", [64, D])
    wkv_p = dout("wkv_p", [DEPTH, 8, 64, 64])
    shift_p = dout("shift_p", [DEPTH, RC])
    hgrn_p = dout("hgrn_p", [DEPTH, 4, 128, 128])
    pool_p = dout("pool_p", [DEPTH, 15, G])
    wkv_s = dout("wkv_s", [DEPTH, NSEQ_S, 8, 64, 64])
    shift_s = dout("shift_s", [DEPTH, NSEQ_S, RC])
    hgrn_s = dout("hgrn_s", [DEPTH, NSEQ_S, 4, 128, 128])
    pool_s = dout("pool_s", [DEPTH, NSEQ_S, 15, G])
    sguv_s = dout("sguv_s", [DEPTH, NSEQ_S, LS, G])
    DBG = EO if _DEBUG else "Internal"
    xmid_p = P.dram("xmid_p", [SEQ, D], kind=DBG)
    xmid_s = P.dram("xmid_s", [64, D], kind=DBG)
    xf1_d = P.dram("xf1_d", [512, D], kind=DBG)
    dbg_d = P.dram("dbg_d", [6, 128, D], kind=DBG)

    cst = P.sb("cst", [128, NCONST])
    cstb = P.sb("cstb", [128, 384], BF16)
    P.dma(cst[:, :], consts_d[:, :], r=[consts_d], w=[cst])
    P.dma(cstb[:, :], consts_d[:, 0:384], r=[consts_d], w=[cstb], eng=POOL)
    cst.const = True
    cstb.const = True

    def CF(name, rows=128, c0=0, c1=None):
        o, n = CONST_COLS[name]
        c1 = n if c1 is None else c1
        return cst[0:rows, o + c0:o + c1]

    def CB(name, rows=128, c0=0, c1=None):
        o, n = CONST_COLS[name]
        c1 = n if c1 is None else c1
        return cstb[0:rows, o + c0:o + c1]

    NSLOT = 3
    wslot = [P.sb("wslot%d" % i, [128, 16, 512], BF16) for i in range(NSLOT)]
    slot_rr = [0]
    hT = P.sb("hT", [128, 16, 512], BF16)
    yT = P.sb("yT", [128, 16, 512], BF16)
    xa = P.sb("xa", [128, D])
    xb = P.sb("xb", [128, D])
    xnb = P.sb("xnb", [128, D], BF16)
    gbc = P.sb("gbc", [128, D])
    stat = P.sb("stat", [128, 8])
    gcol = P.sb("gcol", [128, 16])
    epsn = P.sb("epsn", [128, 1])
    P.op(DVE, lambda e: e.memset(epsn[:, :], 1e-6), w=[epsn])
    epsn.const = True
    ARENA_F = 4 * D + 44 * 256
    arena = nc.alloc_sbuf_tensor("arena", [128, ARENA_F], F32)
    big = P.view("big", arena[:, 0:4 * D].rearrange("p (a b) -> p a b", a=4))
    actT = P.view("actT", arena[:, 4 * D:ARENA_F].bitcast(BF16).rearrange("p (a b) -> p a b", a=44))
    pT = P.view("pT", arena[:, 4 * D:4 * D + 512].bitcast(BF16).rearrange("p (a b) -> p a b", a=2))
    aoff = [0]
    mix_views = []
    last_fence = [None]

    def arena_reset():
        aoff[0] = 0
        del mix_views[:]

    def av(name, rows, cols, dt=F32):
        n32 = cols if dt == F32 else (cols + 1) // 2
        a0 = aoff[0]
        aoff[0] += n32
        assert aoff[0] <= ARENA_F, (name, aoff[0])
        ap = arena[0:rows, a0:a0 + n32]
        if dt != F32:
            ap = ap.bitcast(dt)
        r_ = P.view(name, ap)
        r_.last_w = last_fence[0]
        mix_views.append(r_)
        return r_

    fence_t = P.sb("fence_t", [128, 1])

    def arena_fence():
        last_fence[0] = P.fence(DVE, lambda e: e.memset(fence_t[:, :], 0.0), [fence_t, big, actT, pT, xb, silu_b] + mix_views)
    silu_t = P.sb("silu_t", [128, 512])
    silu_b = P.view("silu_b", xb[:, 0:512])
    pbig = [P.ps("pbig%d" % i, [128, 512]) for i in range(3)]
    ptr = [P.ps("ptr%d" % i, [128, 512], BF16) for i in range(2)]
    pg = P.ps("pg", [128, 512])
    pu = P.ps("pu", [128, 512])
    rr = {"pbig": 0, "ptr": 0}

    def next_pbig():
        rr["pbig"] = (rr["pbig"] + 1) % 3
        return pbig[rr["pbig"]]

    def next_ptr():
        rr["ptr"] = (rr["ptr"] + 1) % 2
        return ptr[rr["ptr"]]

    def load_w(wres, l, k0, nk, c0, ncols):
        s = wslot[slot_rr[0]]
        slot_rr[0] = (slot_rr[0] + 1) % NSLOT
        src = wres.t[l].rearrange("(kc p) c -> p kc c", p=128)[:, k0:k0 + nk, c0:c0 + ncols]
        P.dma(s[:, 0:nk, 0:ncols], src, r=[wres], w=[s], eng=POOL)
        return s

    def bcast_load(dst, vec_res, l, n=D, c0=0):
        v = vec_res.t[l]
        src = bass.AP(tensor=v.tensor, offset=v.offset + c0, ap=[[0, 128], [1, n]])
        P.dma(dst[:, 0:n], src, r=[vec_res], w=[dst])

    def rms_rstd(src_res, src_ap, m, ncol, col, scratch_res, scratch_ap, inv_n, eps_res):
        P.op(DVE, lambda e: e.memset(stat[0:m, col:col + 1], 0.0), w=[stat])
        P.op(ACT, lambda e: e.activation(scratch_ap, src_ap, AF.Square, accum_out=stat[0:m, col:col + 1]),
             r=[src_res, stat], w=[scratch_res, stat])
        P.op(ACT, lambda e: e.activation(stat[0:m, col + 1:col + 2], stat[0:m, col:col + 1], AF.Sqrt,
                                         bias=eps_res[0:m, 0:1], scale=inv_n), r=[stat, eps_res], w=[stat])
        P.op(DVE, lambda e: e.reciprocal(stat[0:m, col + 2:col + 3], stat[0:m, col + 1:col + 2]), r=[stat], w=[stat])
        return stat[0:m, col + 2:col + 3]

    def to_featmajor(src_res, src_bf_ap_fn, m, t0, dstT, nkc=16, kc0=0, gain=None):
        for g4 in range(0, nkc, 4):
            n4 = min(4, nkc - g4)
            pt = next_ptr()
            for q in range(n4):
                kc = g4 + q
                P.tr(pt[:, q * 128:q * 128 + m], src_bf_ap_fn(kc), CB("ident", m, 0, m), r=[src_res, cstb], w=[pt])
            o_ap = dstT[:, kc0 + g4:kc0 + g4 + n4, t0:t0 + m]
            i_ap = pt[:, 0:n4 * 128].rearrange("p (a b) -> p a b", a=n4)[:, :, 0:m]
            if gain is not None:
                for q in range(n4):
                    kc = g4 + q
                    oq = dstT[:, kc0 + kc, t0:t0 + m]
                    iq = pt[:, q * 128:q * 128 + m]
                    gq = gain[:, kc:kc + 1]
                    if q % 2 == 0:
                        P.op(DVE, lambda e, oq=oq, iq=iq, gq=gq: e.tensor_scalar(oq, iq, gq, None, ALU.mult), r=[pt, gcol], w=[dstT])
                    else:
                        P.op(ACT, lambda e, oq=oq, iq=iq, gq=gq: e.activation(oq, iq, AF.Copy, scale=gq), r=[pt, gcol], w=[dstT])
            elif (g4 // 4) % 2 == 0:
                P.op(DVE, lambda e, o_ap=o_ap, i_ap=i_ap: e.tensor_copy(o_ap, i_ap), r=[pt], w=[dstT])
            else:
                P.op(ACT, lambda e, o_ap=o_ap, i_ap=i_ap: e.copy(o_ap, i_ap), r=[pt], w=[dstT])


    NCOLP = 80
    pcol = P.sb("pcol", [128, NCOLP])
    PC = {}

    def load_cols(name, vec_res, l, base, n=G):
        nchk = n // 128
        src = vec_res.t[l, 0:n].rearrange("(j p) -> p j", p=128)
        P.dma(pcol[:, base:base + nchk], src, r=[vec_res], w=[pcol], allow_slow_non_contiguous=True)
        PC[name] = base

    pw_sb = P.sb("pw_sb", [128, 4, 128], BF16)
    pool_carry = P.sb("pool_carry", [128, 4, 15])
    sgu_bc = P.sb("sgu_bc", [128, 3, G])
    wmT = P.sb("wmT", [128, 4, 128], BF16)
    bd4 = P.sb("bd4", [64, 4, 64], BF16)
    sgub_p = P.sb("sgub_p", [128, 4])
    sgub_s = P.sb("sgub_s", [64, 4])
    px = P.ps("px", [128, 512])

    def layer_params(l):
        P.dma(gcol[:, :], W["ln_ffn_pre"].t[l].rearrange("(j p) -> p j", p=128), r=[W["ln_ffn_pre"]], w=[gcol],
              allow_slow_non_contiguous=True)
        load_cols("pool_scale", W["pool_scale"], l, 0)
        load_cols("hgrn_norm", W["hgrn_norm"], l, 4)
        load_cols("lg0", W["hgrn_lb_logits"], 0, 8)
        load_cols("lg1", W["hgrn_lb_logits"], 1, 12)
        PC["lb"], PC["oml"] = 16, 20
        load_cols("mu", W["rwkv_mu"], l, 24, n=1536)
        for q, (a_, b_) in enumerate(((1536, 1568), (1568, 1600), (1600, 1696))):
            P.dma(pcol[0:b_ - a_, 36 + q:37 + q], W["rwkv_mu"].t[l, a_:b_].rearrange("(c o) -> c o", o=1), r=[W["rwkv_mu"]], w=[pcol],
                  allow_slow_non_contiguous=True)
        for q, nm in enumerate(("w0", "a0", "k_k", "k_a", "r_k", "gn_w", "gn_b")):
            load_cols(nm, W["rwkv_" + nm], l, 40 + 4 * q)
        PC["omk"] = 68
        P.op(DVE, lambda e: e.tensor_scalar(pcol[:, 68:72], pcol[:, PC["k_a"]:PC["k_a"] + 4], -1.0, 1.0, ALU.mult, ALU.add), r=[pcol], w=[pcol])
        if l == 0:
            P.op(DVE, lambda e: e.memset(pcol[:, 16:20], 0.0), w=[pcol])
            P.op(DVE, lambda e: e.memset(pcol[:, 20:24], 1.0), w=[pcol])
        else:
            P.op(DVE, lambda e: e.tensor_tensor(pcol[:, 16:20], pcol[:, 12:16], pcol[:, 8:12], ALU.subtract), r=[pcol], w=[pcol])
            P.op(ACT, lambda e: e.activation(pcol[:, 16:20], pcol[:, 16:20], AF.Sigmoid), r=[pcol], w=[pcol])
            P.op(DVE, lambda e: e.tensor_scalar(pcol[:, 20:24], pcol[:, 16:20], -1.0, 1.0, ALU.mult, ALU.add), r=[pcol], w=[pcol])
        P.dma(pw_sb[:, :, :], W["pool_w"].t[l].rearrange("g c d -> c g d"), r=[W["pool_w"]], w=[pw_sb], eng=POOL)
        for i, nm in enumerate(("sgu_ln_w", "sgu_ln_b", "sgu_norm")):
            v = W[nm].t[l]
            P.dma(sgu_bc[:, i, :], bass.AP(tensor=v.tensor, offset=v.offset, ap=[[0, 128], [1, G]]), r=[W[nm]], w=[sgu_bc])
        wtmp = P.view("wtmp", arena[:, 0:512].rearrange("p (a b) -> p a b", a=4))
        wtmp.last_w = last_fence[0]
        wtb = P.view("wtb", arena[:, 512:768].bitcast(BF16).rearrange("p (a b) -> p a b", a=4))
        wtb.last_w = last_fence[0]
        mix_views.extend([wtmp, wtb])
        P.dma(wtmp[:, :, :], W["sgu_w"].t[l].rearrange("h t s -> t h s"), r=[W["sgu_w"]], w=[wtmp])
        tril = CF("TRIL")
        P.op(DVE, lambda e: e.tensor_tensor(wtb[:, :, :], wtmp[:, :, :], tril.unsqueeze(1).to_broadcast([128, 4, 128]), ALU.mult),
             r=[wtmp, cst], w=[wtb])
        pt = next_ptr()
        for h in range(4):
            P.tr(pt[:, h * 128:(h + 1) * 128], wtb[:, h, :], CB("ident"), r=[wtb, cstb], w=[pt])
        P.op(DVE, lambda e: e.tensor_copy(wmT[:, :, :], pt[:, :].rearrange("p (a b) -> p a b", a=4)), r=[pt], w=[wmT])
        P.dma(sgub_p[:, :], W["sgu_b"].t[l].rearrange("h t -> t h"), r=[W["sgu_b"]], w=[sgub_p], allow_slow_non_contiguous=True)
        w4 = P.view("w4", arena[0:64, 768:1024].rearrange("p (a b) -> p a b", a=4))
        w4.last_w = last_fence[0]
        w4b = P.view("w4b", arena[0:64, 1024:1152].bitcast(BF16).rearrange("p (a b) -> p a b", a=4))
        w4b.last_w = last_fence[0]
        mix_views.extend([w4, w4b])
        P.op(DVE, lambda e: e.memset(w4[:, :, :], 0.0), w=[w4])
        for sq in range(NSEQ_S):
            P.dma(w4[4 * sq:4 * sq + 4, :, 4 * sq:4 * sq + 4], W["sgu_w"].t[l, :, 0:4, 0:4].rearrange("h t s -> t h s"),
                  r=[W["sgu_w"]], w=[w4], allow_slow_non_contiguous=True)
            P.dma(sgub_s[4 * sq:4 * sq + 4, :], W["sgu_b"].t[l, :, 0:4].rearrange("h t -> t h"), r=[W["sgu_b"]], w=[sgub_s],
                  allow_slow_non_contiguous=True)
        bdt = CF("BD4T", 64)
        P.op(DVE, lambda e: e.tensor_tensor(w4b[:, :, :], w4[:, :, :], bdt.unsqueeze(1).to_broadcast([64, 4, 64]), ALU.mult),
             r=[w4, cst], w=[w4b])
        pt2 = next_ptr()
        for h in range(4):
            P.tr(pt2[0:64, h * 64:(h + 1) * 64], w4b[:, h, :], CB("ident", 64, 0, 64), r=[w4b, cstb], w=[pt2])
        P.op(DVE, lambda e: e.tensor_copy(bd4[:, :, :], pt2[0:64, 0:256].rearrange("p (a b) -> p a b", a=4)), r=[pt2], w=[bd4])

    def inproj_fm(l, col0, ncols, NT, consume):
        c = 0
        ci = 0
        while c < ncols:
            n = min(512, ncols - c)
            s = load_w(W["w_in"], l, 0, 16, col0 + c, n)
            for q in range(0, n, 128):
                rows = min(128, n - q)
                pb = next_pbig()
                for kc in range(16):
                    P.mm(pb[0:rows, 0:NT], s[:, kc, q:q + rows], hT[:, kc, 0:NT], kc == 0, kc == 15, r=[s, hT], w=[pb])
                consume(ci, pb, rows)
                ci += 1
            c += n

    def mixer_pool(l, kind, tok0, NT):
        nseq, L = (1, NT) if kind == "p" else (NSEQ_S, LS)
        E = 15 + L
        Zx = [av("poolZ%d" % g, 128, nseq * E) for g in range(4)]
        Ea = av("poolEa", 128, nseq * E)
        Eb = av("poolEb", 128, nseq * E)
        dbf = av("pooldbf", 128, NT, BF16)
        v3 = lambda r_: r_[:, :].rearrange("p (s e) -> p s e", s=nseq)
        for g in range(4):
            z3 = v3(Zx[g])
            if kind == "p":
                if tok0 == 0:
                    P.op(DVE, lambda e, z3=z3: e.memset(z3[:, :, 0:15], 0.0), w=[Zx[g]])
                else:
                    P.op(DVE, lambda e, z3=z3, g=g: e.tensor_copy(z3[:, 0, 0:15], pool_carry[:, g, :]), r=[pool_carry], w=[Zx[g]])
        if kind == "s":
            rows_ap = st_pool.t[l].rearrange("s p c -> (s p) c")
            for half in range(2):
                P.dma(xa[0:120, half * 512:(half + 1) * 512], rows_ap[half * 120:(half + 1) * 120, :], r=[st_pool], w=[xa])
            for g in range(4):
                z3 = v3(Zx[g])
                for half in range(2):
                    P.tr(px[:, 0:120], xa[0:120, half * 512 + g * 128:half * 512 + (g + 1) * 128], CF("ident", 120, 0, 120),
                         r=[xa, cst], w=[px])
                    P.op(ACT, lambda e, z3=z3, half=half: e.copy(z3[:, half * 8:(half + 1) * 8, 0:15],
                                                                px[:, 0:120].rearrange("p (s e) -> p s e", s=8)), r=[px], w=[Zx[g]])

        def consume(ci, pb, rows):
            z3 = v3(Zx[ci])
            P.op(ACT, lambda e: e.copy(z3[:, :, 15:E], pb[:, 0:NT].rearrange("p (s e) -> p s e", s=nseq)), r=[pb], w=[Zx[ci]])
        inproj_fm(l, C_P, G, NT, consume)
        for g, win in enumerate((2, 4, 8, 16)):
            src = Zx[g]
            bufs = [Ea, Eb]
            for k in range(1, g + 2):
                sft = 1 << (k - 1)
                lo = (1 << k) - 1
                dst = bufs[k % 2]
                s3, d3 = v3(src), v3(dst)
                P.op(DVE, lambda e, s3=s3, d3=d3, lo=lo, sft=sft: e.tensor_tensor(d3[:, :, lo:E], s3[:, :, lo:E], s3[:, :, lo - sft:E - sft], ALU.add),
                     r=[src], w=[dst])
                src = dst
            s3, z3 = v3(src), v3(Zx[g])
            d3 = dbf[:, :].rearrange("p (s e) -> p s e", s=nseq)
            P.op(DVE, lambda e, s3=s3, z3=z3, d3=d3, win=win: e.scalar_tensor_tensor(d3, s3[:, :, 15:E], 1.0 / win, z3[:, :, 15:E], ALU.mult, ALU.subtract),
                 r=[src, Zx[g]], w=[dbf])
            if kind == "p" and tok0 == 0:
                tmpc = bufs[(g + 2) % 2]
                pcv = CF("poolc", 128, g * 16, (g + 1) * 16)
                P.op(DVE, lambda e, s3=s3, tmpc=tmpc, pcv=pcv: e.tensor_tensor(tmpc[:, 0:16], s3[:, 0, 15:31], pcv, ALU.mult), r=[src, cst], w=[tmpc])
                P.op(DVE, lambda e, tmpc=tmpc, z3=z3, win=win: e.scalar_tensor_tensor(dbf[:, 0:16], tmpc[:, 0:16], 1.0 / win, z3[:, 0, 15:31], ALU.mult, ALU.subtract),
                     r=[tmpc, Zx[g]], w=[dbf])
            P.mm(px[:, 0:NT], pw_sb[:, g, :], dbf[:, 0:NT], True, True, r=[pw_sb, dbf], w=[px])
            sc = pcol[:, PC["pool_scale"] + g:PC["pool_scale"] + g + 1]
            P.op(ACT, lambda e, g=g, sc=sc: e.activation(yT[:, 12 + g, 0:NT], px[:, 0:NT], AF.Copy, scale=sc), r=[px, pcol], w=[yT])
            if kind == "p":
                if tok0 + NT < SEQ:
                    P.op(ACT, lambda e, z3=z3, g=g: e.copy(pool_carry[:, g, :], z3[:, 0, L:E]), r=[Zx[g]], w=[pool_carry])
                else:
                    P.dma(pool_p.t[l, :, g * 128:(g + 1) * 128].rearrange("p c -> c p"), z3[:, 0, L:E], r=[Zx[g]], w=[pool_p],
                          out=True, allow_slow_non_contiguous=True)
            else:
                for half in range(2):
                    P.op(ACT, lambda e, z3=z3, half=half: e.copy(Ea[:, 0:120].rearrange("p (s e) -> p s e", s=8),
                                                                z3[:, half * 8:(half + 1) * 8, L:E]), r=[Zx[g]], w=[Ea])
                    P.tr(px[0:120, 0:128], Ea[:, 0:120], CF("ident"), r=[Ea, cst], w=[px])
                    P.op(ACT, lambda e, g=g, half=half: e.copy(xb[0:120, half * 512 + g * 128:half * 512 + (g + 1) * 128], px[0:120, 0:128]),
                         r=[px], w=[xb])
        if kind == "s":
            orows = pool_s.t[l].rearrange("s p c -> (s p) c")
            for half in range(2):
                P.dma(orows[half * 120:(half + 1) * 120, :], xb[0:120, half * 512:(half + 1) * 512], r=[xb], w=[pool_s], out=True)

    def mixer_sgu(l, kind, tok0, NT, tiles):
        su_ = load_w(W["w_in"], l, 0, 16, C_S, 512)
        sv_ = load_w(W["w_in"], l, 0, 16, C_S + 512, 512)
        u_sb = av("sgu_u", 128, G)
        v_sb = av("sgu_v", 128, G)
        t_sb = av("sgu_t", 128, G)
        vnb = av("sgu_vnb", 128, G, BF16)
        ynb = av("sgu_ynb", 128, G, BF16)
        for (o, m) in tiles:
            pbu = next_pbig()
            for kc in range(16):
                P.mm(pbu[0:m, :], hT[:, kc, o:o + m], su_[:, kc, :], kc == 0, kc == 15, r=[hT, su_], w=[pbu])
            pbv = next_pbig()
            for kc in range(16):
                P.mm(pbv[0:m, :], hT[:, kc, o:o + m], sv_[:, kc, :], kc == 0, kc == 15, r=[hT, sv_], w=[pbv])
            P.op(DVE, lambda e, m=m: e.memset(stat[0:m, 0:8], 0.0), w=[stat])
            P.op(ACT, lambda e, m=m, pbu=pbu: e.activation(u_sb[0:m, :], pbu[0:m, :], AF.Gelu), r=[pbu], w=[u_sb])
            P.op(ACT, lambda e, m=m, pbv=pbv: e.activation(v_sb[0:m, :], pbv[0:m, :], AF.Gelu, accum_out=stat[0:m, 0:1]),
                 r=[pbv, stat], w=[v_sb, stat])
            P.op(ACT, lambda e, m=m: e.activation(t_sb[0:m, :], v_sb[0:m, :], AF.Square, accum_out=stat[0:m, 1:2]),
                 r=[v_sb, stat], w=[t_sb, stat])
            P.op(DVE, lambda e, m=m: e.tensor_scalar(stat[0:m, 2:4], stat[0:m, 0:2], 1.0 / G, None, ALU.mult), r=[stat], w=[stat])
            P.op(DVE, lambda e, m=m: e.tensor_tensor(stat[0:m, 4:5], stat[0:m, 2:3], stat[0:m, 2:3], ALU.mult), r=[stat], w=[stat])
            P.op(DVE, lambda e, m=m: e.tensor_tensor(stat[0:m, 5:6], stat[0:m, 3:4], stat[0:m, 4:5], ALU.subtract), r=[stat], w=[stat])
            P.op(DVE, lambda e, m=m: e.tensor_scalar(stat[0:m, 5:6], stat[0:m, 5:6], 1e-5, None, ALU.add), r=[stat], w=[stat])
            P.op(ACT, lambda e, m=m: e.activation(stat[0:m, 6:7], stat[0:m, 5:6], AF.Sqrt), r=[stat], w=[stat])
            P.op(DVE, lambda e, m=m: e.reciprocal(stat[0:m, 7:8], stat[0:m, 6:7]), r=[stat], w=[stat])
            P.op(DVE, lambda e, m=m: e.tensor_scalar(v_sb[0:m, :], v_sb[0:m, :], stat[0:m, 2:3], stat[0:m, 7:8], ALU.subtract, ALU.mult),
                 r=[v_sb, stat], w=[v_sb])
            P.op(DVE, lambda e, m=m: e.tensor_tensor(v_sb[0:m, :], v_sb[0:m, :], sgu_bc[0:m, 0, :], ALU.mult), r=[v_sb, sgu_bc], w=[v_sb])
            P.op(DVE, lambda e, m=m: e.tensor_tensor(v_sb[0:m, :], v_sb[0:m, :], sgu_bc[0:m, 1, :], ALU.add), r=[v_sb, sgu_bc], w=[v_sb])
            if kind == "s":
                P.dma(sguv_s.t[l].rearrange("s t c -> (s t) c"), v_sb[0:m, :], r=[v_sb], w=[sguv_s], out=True)
            P.op(ACT, lambda e, m=m: e.copy(vnb[0:m, :], v_sb[0:m, :]), r=[v_sb], w=[vnb])
            for h in range(4):
                lh = wmT[:, h, :] if kind == "p" else bd4[:, h, :]
                P.mm(px[0:m, h * 128:(h + 1) * 128], lh, vnb[0:m, h * 128:(h + 1) * 128], True, True,
                     r=[wmT if kind == "p" else bd4, vnb], w=[px])
            sbc = (sgub_p if kind == "p" else sgub_s)
            sb_ap = sbc[0:m, 0:4].unsqueeze(2).to_broadcast([m, 4, 128])
            P.op(DVE, lambda e, m=m, sb_ap=sb_ap: e.tensor_tensor(t_sb[0:m, :].rearrange("p (a b) -> p a b", a=4),
                                                                 px[0:m, :].rearrange("p (a b) -> p a b", a=4), sb_ap, ALU.add),
                 r=[px, sbc], w=[t_sb])
            P.op(DVE, lambda e, m=m: e.tensor_tensor(u_sb[0:m, :], u_sb[0:m, :], t_sb[0:m, :], ALU.mult), r=[u_sb, t_sb], w=[u_sb])
            rs = rms_rstd(u_sb, u_sb[0:m, :], m, G, 0, t_sb, t_sb[0:m, :], 1.0 / G, epsn)
            P.op(DVE, lambda e, m=m, rs=rs: e.scalar_tensor_tensor(ynb[0:m, :], u_sb[0:m, :], rs, sgu_bc[0:m, 2, :], ALU.mult, ALU.mult),
                 r=[u_sb, stat, sgu_bc], w=[ynb])
            to_featmajor(ynb, lambda kc, m=m: ynb[0:m, kc * 128:(kc + 1) * 128], m, o, yT, nkc=4, kc0=4)

    hS = P.sb("hS", [128, 4, 128])
    hSb = P.sb("hSb", [128, 4, 128], BF16)

    def mixer_hgrn(l, kind, tok0, NT):
        C = 64 if kind == "p" else LS
        nch = NT // C
        rst = CF("rst64") if kind == "p" else CF("rst4")
        qf = [av("h_q%d" % h, 128, NT) for h in range(4)]
        gate = [av("h_g%d" % h, 128, NT) for h in range(4)]
        qt = [av("h_qt%d" % h, 128, NT, BF16) for h in range(4)]
        kt = [av("h_kt%d" % h, 128, NT, BF16) for h in range(4)]
        kh = [av("h_kh%d" % h, 128, NT, BF16) for h in range(4)]
        vb = [av("h_vb%d" % h, 128, NT, BF16) for h in range(4)]
        egC = [av("h_egC%d" % h, 128, 16) for h in range(4)]
        T = [av("h_T%d" % i, 128, NT) for i in range(4)]
        Osb = av("h_O", 128, 4 * NT)
        TMK = [av("h_TMK%d" % i, 64, 512, BF16) for i in range(2)]
        TMV = [av("h_TMV%d" % i, 64, 512, BF16) for i in range(2)]
        ATs = [av("h_ATs%d" % i, 64, 4 * C, BF16) for i in range(2)]
        osq = av("h_osq", 128, NT, BF16)
        lbc, omc, hnc = PC["lb"], PC["oml"], PC["hgrn_norm"]
        c3 = lambda ap: ap.rearrange("p (n c) -> p n c", c=C)

        def consume(ci, pb, rows):
            grp, h = ci // 4, ci % 4
            if grp == 0:
                P.op(ACT, lambda e: e.activation(qf[h][:, :], pb[:, 0:NT], AF.Silu), r=[pb], w=[qf[h]])
            elif grp == 1:
                t0, t1, t2 = T[0], T[1], T[2]
                P.op(ACT, lambda e: e.activation(t0[:, :], pb[:, 0:NT], AF.Sigmoid), r=[pb], w=[t0])
                P.op(DVE, lambda e: e.tensor_scalar(t0[:, :], t0[:, :], pcol[:, omc + h:omc + h + 1], pcol[:, lbc + h:lbc + h + 1],
                                                    ALU.mult, ALU.add), r=[t0, pcol], w=[t0])
                P.op(ACT, lambda e: e.activation(t1[:, :], t0[:, :], AF.Ln), r=[t0], w=[t1])
                P.op(DVE, lambda e: e.tensor_scalar(t0[:, :], t0[:, :], -1.0, 1.0, ALU.mult, ALU.add), r=[t0], w=[t0])
                P.op(DVE, lambda e: e.tensor_tensor_scan(t2[:, :], rst[:, 0:NT], t1[:, :], 0.0, ALU.mult, ALU.add),
                     r=[cst, t1], w=[t2])
                P.op(ACT, lambda e: e.activation(t1[:, :], t2[:, :], AF.Exp), r=[t2], w=[t1])
                P.op(DVE, lambda e: e.tensor_tensor(qt[h][:, :], qf[h][:, :], t1[:, :], ALU.mult), r=[qf[h], t1], w=[qt[h]])
                P.op(ACT, lambda e: e.activation(t1[:, :], t2[:, :], AF.Exp, scale=-1.0), r=[t2], w=[t1])
                P.op(DVE, lambda e: e.tensor_tensor(kt[h][:, :], t0[:, :], t1[:, :], ALU.mult), r=[t0, t1], w=[kt[h]])
                cum3 = c3(t2[:, :])
                P.op(ACT, lambda e: e.activation(egC[h][:, 0:nch], cum3[:, :, C - 1], AF.Exp), r=[t2], w=[egC[h]])
                P.op(DVE, lambda e: e.tensor_tensor(c3(t1[:, :]), cum3[:, :, C - 1:C].to_broadcast([128, nch, C]), cum3, ALU.subtract),
                     r=[t2], w=[t1])
                P.op(ACT, lambda e: e.activation(t1[:, :], t1[:, :], AF.Exp), r=[t1], w=[t1])
                P.op(DVE, lambda e: e.tensor_tensor(kh[h][:, :], t0[:, :], t1[:, :], ALU.mult), r=[t0, t1], w=[kh[h]])
            elif grp == 2:
                P.op(ACT, lambda e: e.copy(vb[h][:, :], pb[:, 0:NT]), r=[pb], w=[vb[h]])
            else:
                P.op(ACT, lambda e: e.activation(gate[h][:, :], pb[:, 0:NT], AF.Silu), r=[pb], w=[gate[h]])
        inproj_fm(l, C_H, 4 * G, NT, consume)

        for c in range(nch):
            cs = slice(c * C, (c + 1) * C)
            tk, tv, at = TMK[c % 2], TMV[c % 2], ATs[c % 2]
            if kind == "s":
                P.dma(hS[:, :, :], st_hgrn.t[l, c].rearrange("h k v -> k h v"), r=[st_hgrn], w=[hS])
                P.op(ACT, lambda e: e.copy(hSb[:, :, :], hS[:, :, :]), r=[hS], w=[hSb])
            elif tok0 == 0 and c == 0:
                P.op(DVE, lambda e: e.memset(hS[:, :, :], 0.0), w=[hS])
                P.op(ACT, lambda e: e.copy(hSb[:, :, :], hS[:, :, :]), r=[hS], w=[hSb])
            ptk = next_ptr()
            for h in range(4):
                P.tr(ptk[0:C, h * 128:(h + 1) * 128], kh[h][:, cs], CB("ident"), r=[kh[h], cstb], w=[ptk])
            P.op(DVE, lambda e, ptk=ptk, tk=tk: e.tensor_copy(tk[0:C, :], ptk[0:C, :]), r=[ptk], w=[tk])
            ptv = next_ptr()
            for h in range(4):
                P.tr(ptv[0:C, h * 128:(h + 1) * 128], vb[h][:, cs], CB("ident"), r=[vb[h], cstb], w=[ptv])
            P.op(ACT, lambda e, ptv=ptv, tv=tv: e.copy(tv[0:C, :], ptv[0:C, :]), r=[ptv], w=[tv])
            for h in range(4):
                P.mm(px[0:C, h * C:(h + 1) * C], kt[h][:, cs], qt[h][:, cs], True, True, r=[kt[h], qt[h]], w=[px])
            iu = CF("IU", C, 0, C)
            P.op(DVE, lambda e, at=at, iu=iu: e.tensor_tensor(at[0:C, :].rearrange("p (h c) -> p h c", h=4),
                                                              px[0:C, 0:4 * C].rearrange("p (h c) -> p h c", h=4),
                                                              iu.unsqueeze(1).to_broadcast([C, 4, C]), ALU.mult), r=[px, cst], w=[at])
            for h in range(4):
                P.mm(pg[:, h * C:(h + 1) * C], tv[0:C, h * 128:(h + 1) * 128], at[0:C, h * C:(h + 1) * C], True, False,
                     r=[tv, at], w=[pg])
                P.mm(pg[:, h * C:(h + 1) * C], hSb[:, h, :], qt[h][:, cs], False, True, r=[hSb, qt[h]], w=[pg])
            P.op(ACT, lambda e, cs=cs: e.copy(Osb[:, :].rearrange("p (h t) -> p h t", h=4)[:, :, cs],
                                              pg[:, 0:4 * C].rearrange("p (h c) -> p h c", h=4)), r=[pg], w=[Osb])
            for h in range(4):
                P.mm(pu[:, h * 128:(h + 1) * 128], tk[0:C, h * 128:(h + 1) * 128], tv[0:C, h * 128:(h + 1) * 128], True, True,
                     r=[tk, tv], w=[pu])
            for h in range(4):
                P.op(DVE, lambda e, h=h, c=c: e.scalar_tensor_tensor(hS[:, h, :], hS[:, h, :], egC[h][:, c:c + 1],
                                                                    pu[:, h * 128:(h + 1) * 128], ALU.mult, ALU.add),
                     r=[hS, egC[h], pu], w=[hS])
            P.op(ACT, lambda e: e.copy(hSb[:, :, :], hS[:, :, :]), r=[hS], w=[hSb])
            if kind == "s":
                P.dma(hgrn_s.t[l, c].rearrange("h k v -> k h v"), hS[:, :, :], r=[hS], w=[hgrn_s], out=True)
            elif tok0 + NT == SEQ and c == nch - 1:
                P.dma(hgrn_p.t[l].rearrange("h k v -> k h v"), hS[:, :, :], r=[hS], w=[hgrn_p], out=True)
        for h in range(4):
            o_ap = Osb[:, h * NT:(h + 1) * NT]
            P.op(DVE, lambda e, o_ap=o_ap: e.tensor_tensor(osq[:, :], o_ap, o_ap, ALU.mult), r=[Osb], w=[osq])
            pb = next_pbig()
            P.mm(pb[:, 0:NT], CB("ones"), osq[:, :], True, True, r=[cstb, osq], w=[pb])
            t0 = T[0]
            P.op(ACT, lambda e, pb=pb, t0=t0: e.activation(t0[:, :], pb[:, 0:NT], AF.Sqrt, bias=epsn[:, 0:1], scale=1.0 / 128),
                 r=[pb, epsn], w=[t0])
            P.op(DVE, lambda e, t0=t0: e.reciprocal(t0[:, :], t0[:, :]), r=[t0], w=[t0])
            P.op(DVE, lambda e, t0=t0, o_ap=o_ap: e.tensor_tensor(t0[:, :], t0[:, :], o_ap, ALU.mult), r=[t0, Osb], w=[t0])
            P.op(DVE, lambda e, t0=t0, h=h: e.scalar_tensor_tensor(yT[:, 8 + h, 0:NT], t0[:, :], pcol[:, hnc + h:hnc + h + 1],
                                                                   gate[h][:, :], ALU.mult, ALU.mult), r=[t0, pcol, gate[h]], w=[yT])

    rS = P.sb("rS", [64, 8, 64])
    rSb = P.sb("rSb", [64, 8, 64], BF16)
    shcar = P.sb("shcar", [128, 16])

    def mixer_rwkv(l, kind, tok0, NT):
        C = 64 if kind == "p" else LS
        nch = NT // C
        nseq, L = (1, NT) if kind == "p" else (NSEQ_S, LS)
        E = L + 1
        C2 = 2 * C
        rst = CF("rst64") if kind == "p" else CF("rst4")
        last_blk = (kind == "p" and tok0 + NT == SEQ)
        c3 = lambda ap: ap.rearrange("p (n c) -> p n c", c=C)
        AR = [av("r_AR%d" % j, 128, 2 * NT, BF16) for j in range(4)]
        BK = [av("r_BK%d" % j, 128, 2 * NT, BF16) for j in range(4)]
        Bh = [av("r_Bh%d" % j, 128, NT, BF16) for j in range(4)]
        Kh = [av("r_Kh%d" % j, 128, NT, BF16) for j in range(4)]
        Vb = [av("r_Vb%d" % j, 128, NT, BF16) for j in range(4)]
        gsb = [av("r_g%d" % j, 128, NT) for j in range(4)]
        bon = [av("r_bon%d" % j, 128, NT) for j in range(4)]
        gC = [av("r_gC%d" % j, 128, 16) for j in range(4)]
        YN = av("r_YN", 128, 4 * NT, BF16)
        mark = aoff[0]
        lw = av("r_lw", 32, G, BF16)
        la = av("r_la", 32, G, BF16)
        lg = av("r_lg", 96, G, BF16)
        txw = av("r_txw", 32, NT, BF16)
        xab = av("r_xab", 32, NT, BF16)
        sgb = av("r_sgb", 96, NT, BF16)
        Zl = av("r_Zl", 128, nseq * E)
        Zr = av("r_Zr", 128, nseq * E)
        Zk = av("r_Zk", 128, nseq * E)
        Zv = av("r_Zv", 128, nseq * E)
        T = [av("r_T%d" % i, 128, NT) for i in range(5)]
        sqb = av("r_sqb", 128, NT, BF16)
        P.dma(lw[:, :], W["rwkv_w_lora"].t[l], r=[W["rwkv_w_lora"]], w=[lw], eng=POOL)
        P.dma(la[:, :], W["rwkv_a_lora"].t[l], r=[W["rwkv_a_lora"]], w=[la], eng=POOL)
        P.dma(lg[:, :], W["rwkv_g_lora"].t[l], r=[W["rwkv_g_lora"]], w=[lg], eng=POOL)
        mu0 = PC["mu"]

        def zl(Z, rows):
            return Z[0:rows, :].rearrange("p (s e) -> p s e", s=nseq)

        def zc(Z, rows=128):
            if kind == "p":
                return Z[0:rows, 1:E].rearrange("p (n c) -> p n c", c=C)
            return zl(Z, rows)[:, :, 1:E]

        def lerp(Z, rows, ci, c0, pb):
            z3 = zl(Z, rows)
            if kind == "p":
                if tok0 == 0:
                    P.op(DVE, lambda e: e.memset(z3[:, :, 0:1], 0.0), w=[Z])
                else:
                    P.op(DVE, lambda e: e.tensor_copy(z3[:, 0, 0:1], shcar[0:rows, ci:ci + 1]), r=[shcar], w=[Z])
            else:
                P.tr(px[0:rows, 0:16], xa[0:16, c0:c0 + rows], CF("ident", 16, 0, 16), r=[xa, cst], w=[px])
                P.op(ACT, lambda e: e.copy(z3[:, :, 0], px[0:rows, 0:16]), r=[px], w=[Z])
            P.op(ACT, lambda e: e.copy(z3[:, :, 1:E], pb[0:rows, 0:NT].rearrange("p (s t) -> p s t", s=nseq)), r=[pb], w=[Z])
            if kind == "p":
                if not last_blk:
                    P.op(ACT, lambda e: e.copy(shcar[0:rows, ci:ci + 1], z3[:, 0, L:E]), r=[Z], w=[shcar])
                else:
                    P.dma(shift_p.t[l, c0:c0 + rows].rearrange("(c o) -> c o", o=1), z3[:, 0, L:E], r=[Z], w=[shift_p], out=True,
                          allow_slow_non_contiguous=True)
            else:
                P.op(ACT, lambda e: e.copy(T[3][0:rows, 0:16], z3[:, :, L]), r=[Z], w=[T[3]])
                P.tr(px[0:16, 0:rows], T[3][0:rows, 0:16], CF("ident", rows, 0, rows), r=[T[3], cst], w=[px])
                P.op(ACT, lambda e: e.copy(xb[0:16, c0:c0 + rows], px[0:16, 0:rows]), r=[px], w=[xb])
            t4 = T[4][0:rows, :].rearrange("p (s t) -> p s t", s=nseq)
            mu = pcol[0:rows, mu0 + ci:mu0 + ci + 1]
            P.op(DVE, lambda e: e.tensor_tensor(t4, z3[:, :, 0:L], z3[:, :, 1:E], ALU.subtract), r=[Z], w=[T[4]])
            P.op(DVE, lambda e: e.scalar_tensor_tensor(z3[:, :, 1:E], t4, mu, z3[:, :, 1:E], ALU.mult, ALU.add),
                 r=[T[4], pcol, Z], w=[Z])

        if kind == "s":
            P.dma(xa[0:16, 0:RC], st_shift.t[l], r=[st_shift], w=[xa])
        sl_ = load_w(W["w_in"], l, 0, 16, 1536, 160)
        for (q0, rows, ci, func, dst) in ((0, 32, 12, AF.Tanh, txw), (32, 32, 13, AF.Copy, xab), (64, 96, 14, AF.Sigmoid, sgb)):
            pb = next_pbig()
            for kc in range(16):
                P.mm(pb[0:rows, 0:NT], sl_[:, kc, q0:q0 + rows], hT[:, kc, 0:NT], kc == 0, kc == 15, r=[sl_, hT], w=[pb])
            lerp(Zl, rows, ci, 1536 + q0, pb)
            zcv = zc(Zl, rows)
            P.op(ACT, lambda e, func=func, dst=dst, zcv=zcv, rows=rows: e.activation(c3(dst[0:rows, :]), zcv, func), r=[Zl], w=[dst])
        sr = load_w(W["w_in"], l, 0, 16, 0, 512)
        sk = load_w(W["w_in"], l, 0, 16, 512, 512)
        sv = load_w(W["w_in"], l, 0, 16, 1024, 512)
        bo_b = CB("blockones")
        for j in range(4):
            js = slice(j * 128, (j + 1) * 128)
            for (Z, s_, ci) in ((Zr, sr, j), (Zk, sk, 4 + j), (Zv, sv, 8 + j)):
                pb = next_pbig()
                for kc in range(16):
                    P.mm(pb[:, 0:NT], s_[:, kc, js], hT[:, kc, 0:NT], kc == 0, kc == 15, r=[s_, hT], w=[pb])
                lerp(Z, 128, ci, ci * 128, pb)
            rm, km, vm = zc(Zr), zc(Zk), zc(Zv)
            T0, T1, T2, T3, T4 = T
            col = lambda nm: pcol[:, PC[nm] + j:PC[nm] + j + 1]
            P.mm(px[:, 0:NT], lw[:, js], txw[:, :], True, True, r=[lw, txw], w=[px])
            P.op(ACT, lambda e, b=col("w0"): e.activation(T0[:, :], px[:, 0:NT], AF.Sigmoid, bias=b), r=[px, pcol], w=[T0])
            P.op(DVE, lambda e: e.tensor_scalar(T0[:, :], T0[:, :], NEG_E05, None, ALU.mult), r=[T0], w=[T0])
            P.op(DVE, lambda e: e.tensor_tensor_scan(T1[:, :], rst[:, 0:NT], T0[:, :], 0.0, ALU.mult, ALU.add), r=[cst, T0], w=[T1])
            P.op(ACT, lambda e, j=j: e.activation(gC[j][:, 0:nch], c3(T1[:, :])[:, :, C - 1], AF.Exp), r=[T1], w=[gC[j]])
            P.mm(px[:, 0:NT], la[:, js], xab[:, :], True, True, r=[la, xab], w=[px])
            P.op(ACT, lambda e, b=col("a0"): e.activation(T2[:, :], px[:, 0:NT], AF.Sigmoid, bias=b), r=[px, pcol], w=[T2])
            P.mm(px[:, 0:NT], lg[:, js], sgb[:, :], True, True, r=[lg, sgb], w=[px])
            P.op(ACT, lambda e, j=j: e.copy(gsb[j][:, :], px[:, 0:NT]), r=[px], w=[gsb[j]])
            P.op(DVE, lambda e, km=km, s_=col("k_k"): e.tensor_scalar(c3(T3[:, :]), km, s_, None, ALU.mult), r=[Zk, pcol], w=[T3])
            P.op(DVE, lambda e: e.tensor_tensor(sqb[:, :], T3[:, :], T3[:, :], ALU.mult), r=[T3], w=[sqb])
            P.mm(px[:, 0:NT], bo_b, sqb[:, :], True, True, r=[cstb, sqb], w=[px])
            P.op(ACT, lambda e: e.activation(T4[:, :], px[:, 0:NT], AF.Sqrt), r=[px], w=[T4])
            P.op(DVE, lambda e: e.tensor_scalar(T4[:, :], T4[:, :], 1e-12, None, ALU.max), r=[T4], w=[T4])
            P.op(DVE, lambda e: e.reciprocal(T4[:, :], T4[:, :]), r=[T4], w=[T4])
            P.op(DVE, lambda e: e.tensor_tensor(T3[:, :], T3[:, :], T4[:, :], ALU.mult), r=[T3, T4], w=[T3])
            P.op(DVE, lambda e, s1=col("k_a"), s2=col("omk"): e.tensor_scalar(T4[:, :], T2[:, :], s1, s2, ALU.mult, ALU.add),
                 r=[T2, pcol], w=[T4])
            P.op(DVE, lambda e, km=km: e.tensor_tensor(km, km, c3(T4[:, :]), ALU.mult), r=[Zk, T4], w=[Zk])
            P.op(DVE, lambda e: e.tensor_tensor(T4[:, :], T3[:, :], T2[:, :], ALU.mult), r=[T3, T2], w=[T4])
            P.op(DVE, lambda e, rm=rm, km=km: e.tensor_tensor(c3(T2[:, :]), rm, km, ALU.mult), r=[Zr, Zk], w=[T2])
            P.op(DVE, lambda e, s_=col("r_k"): e.tensor_scalar(sqb[:, :], T2[:, :], s_, None, ALU.mult), r=[T2, pcol], w=[sqb])
            P.mm(px[:, 0:NT], bo_b, sqb[:, :], True, True, r=[cstb, sqb], w=[px])
            P.op(DVE, lambda e, j=j, vm=vm: e.tensor_tensor(c3(bon[j][:, :]), c3(px[:, 0:NT]), vm, ALU.mult), r=[px, Zv], w=[bon[j]])
            P.op(ACT, lambda e, j=j, vm=vm: e.copy(c3(Vb[j][:, :]), vm), r=[Zv], w=[Vb[j]])
            AR4 = AR[j][:, :].rearrange("p (n two c) -> p n two c", two=2, c=C)
            BK4 = BK[j][:, :].rearrange("p (n two c) -> p n two c", two=2, c=C)
            P.op(ACT, lambda e: e.activation(T2[:, :], T1[:, :], AF.Exp), r=[T1], w=[T2])
            P.op(DVE, lambda e, rm=rm, o=AR4[:, :, 1, :]: e.tensor_tensor(o, rm, c3(T2[:, :]), ALU.mult), r=[Zr, T2], w=[AR[j]])
            P.op(ACT, lambda e: e.activation(T2[:, :], T1[:, :], AF.Exp, scale=-1.0), r=[T1], w=[T2])
            P.op(DVE, lambda e, o=BK4[:, :, 0, :]: e.tensor_tensor(o, c3(T4[:, :]), c3(T2[:, :]), ALU.mult), r=[T4, T2], w=[BK[j]])
            P.op(DVE, lambda e, km=km, o=BK4[:, :, 1, :]: e.tensor_tensor(o, km, c3(T2[:, :]), ALU.mult), r=[Zk, T2], w=[BK[j]])
            P.op(DVE, lambda e: e.tensor_tensor(T2[:, :], T1[:, :], T0[:, :], ALU.subtract), r=[T1, T0], w=[T2])
            P.op(ACT, lambda e: e.activation(T2[:, :], T2[:, :], AF.Exp), r=[T2], w=[T2])
            P.op(DVE, lambda e, o=AR4[:, :, 0, :]: e.scalar_tensor_tensor(o, c3(T3[:, :]), -1.0, c3(T2[:, :]), ALU.mult, ALU.mult),
                 r=[T3, T2], w=[AR[j]])
            cum3 = c3(T1[:, :])
            P.op(DVE, lambda e, cum3=cum3: e.tensor_tensor(c3(T2[:, :]), cum3[:, :, C - 1:C].to_broadcast([128, nch, C]), cum3, ALU.subtract),
                 r=[T1], w=[T2])
            P.op(ACT, lambda e: e.activation(T2[:, :], T2[:, :], AF.Exp), r=[T2], w=[T2])
            P.op(DVE, lambda e, j=j: e.tensor_tensor(Bh[j][:, :], T4[:, :], T2[:, :], ALU.mult), r=[T4, T2], w=[Bh[j]])
            P.op(DVE, lambda e, j=j, km=km: e.tensor_tensor(c3(Kh[j][:, :]), km, c3(T2[:, :]), ALU.mult), r=[Zk, T2], w=[Kh[j]])

        if kind == "s":
            P.dma(shift_s.t[l], xb[0:16, 0:RC], r=[xb], w=[shift_s], out=True)
        if _RW_LEVEL < 2:
            for j in range(4):
                P.op(DVE, lambda e, j=j: e.tensor_scalar(yT[:, j, 0:NT], hT[:, j, 0:NT], 0.0, None, ALU.mult), r=[hT], w=[yT])
            return
        arena_fence()
        aoff[0] = mark
        TMB = [av("r_TMB%d" % i, 64, 512, BF16) for i in range(2)]
        TMK = [av("r_TMK%d" % i, 64, 512, BF16) for i in range(2)]
        TMV = [av("r_TMV%d" % i, 64, 512, BF16) for i in range(2)]
        A1s = av("r_A1s", 64, 8 * C2, BF16)
        A2s = av("r_A2s", 64, 8 * C2, BF16)
        NTs = av("r_NTs", 64, 8 * C, BF16)
        Xb = [av("r_X%d" % i, 64, 8 * C, BF16) for i in range(2)]
        XTb = [av("r_XT%d" % i, 64, 8 * C, BF16) for i in range(2)]
        Tm = av("r_Tm", 64, 8 * C, BF16)
        TTm = av("r_TTm", 64, 8 * C, BF16)
        XtS = av("r_XtS", 64, 512, BF16)
        UtS = av("r_UtS", 64, 512, BF16)
        ysb = av("r_ysb", 64, 512)
        ynb = av("r_ynb", 64, 512, BF16)
        gst = av("r_gst", 64, 32)
        Sld = av("r_Sld", 64, 512)
        Sout = Sld
        fin = av("r_fin", 128, 512)
        ysq = fin
        su_m = CF("SU", C, 0, C).unsqueeze(1)
        iu_m = CF("IU", C, 0, C).unsqueeze(1)
        sl_m = CF("SL", C, 0, C).unsqueeze(1)
        id_m = CF("ident", C, 0, C).unsqueeze(1)
        idb = CB("ident")
        nlev = {64: 5, 4: 1}[C] if _RW_LEVEL >= 3 else 0
        hv = lambda ap, w_: ap.rearrange("p (h c) -> p h c", c=w_)
        gCo = av("r_gCo", 64, 64)
        ARo = [hT[0:64, 2 * j:2 * j + 2, :].rearrange("p a b -> p (a b)") for j in range(4)]
        BKo = [hT[0:64, 8 + 2 * j:10 + 2 * j, :].rearrange("p a b -> p (a b)") for j in range(4)]
        for j in range(4):
            P.dma(ARo[j][:, 0:2 * NT], AR[j][64:128, :], r=[AR[j]], w=[hT])
            P.dma(BKo[j][:, 0:2 * NT], BK[j][64:128, :], r=[BK[j]], w=[hT])
            P.dma(gCo[:, j * 16:j * 16 + nch], gC[j][64:128, 0:nch], r=[gC[j]], w=[gCo])

        def opA(j, e_, c0, c1):
            return (AR[j][0:64, c0:c1], AR[j]) if e_ == 0 else (ARo[j][:, c0:c1], hT)

        def opB(j, e_, c0, c1):
            return (BK[j][0:64, c0:c1], BK[j]) if e_ == 0 else (BKo[j][:, c0:c1], hT)

        def decay(j, e_, c):
            return (gC[j][0:64, c:c + 1], gC[j]) if e_ == 0 else (gCo[:, j * 16 + c:j * 16 + c + 1], gCo)

        for c in range(nch):
            cs = slice(c * C, (c + 1) * C)
            tb, tk, tv = TMB[c % 2], TMK[c % 2], TMV[c % 2]
            if kind == "s":
                P.dma(Sld[:, :].rearrange("p (h k) -> p h k", h=8), st_wkv.t[l, c].rearrange("h v k -> v h k"), r=[st_wkv], w=[Sld])
                for h in range(8):
                    P.tr(px[0:64, h * 64:(h + 1) * 64], Sld[:, h * 64:(h + 1) * 64], CF("ident", 64, 0, 64), r=[Sld, cst], w=[px])
                P.op(DVE, lambda e: e.tensor_copy(rS[:, :, :], hv(px[0:64, :], 64)), r=[px], w=[rS])
                P.op(ACT, lambda e: e.copy(rSb[:, :, :], rS[:, :, :]), r=[rS], w=[rSb])
            elif tok0 == 0 and c == 0:
                P.op(DVE, lambda e: e.memset(rS[:, :, :], 0.0), w=[rS])
                P.op(ACT, lambda e: e.copy(rSb[:, :, :], rS[:, :, :]), r=[rS], w=[rSb])
            for (srcs, dst, eng) in ((Bh, tb, DVE), (Kh, tk, ACT), (Vb, tv, DVE)):
                pt = next_ptr()
                for j in range(4):
                    P.tr(pt[0:C, j * 128:(j + 1) * 128], srcs[j][:, cs], idb, r=[srcs[j], cstb], w=[pt])
                if eng == DVE:
                    P.op(DVE, lambda e, pt=pt, dst=dst: e.tensor_copy(dst[0:C, :], pt[0:C, :]), r=[pt], w=[dst])
                else:
                    P.op(ACT, lambda e, pt=pt, dst=dst: e.copy(dst[0:C, :], pt[0:C, :]), r=[pt], w=[dst])
            if _RW_LEVEL < 2.1:
                continue
            pA1 = [pbig[0], pbig[1]]
            pA2 = [pbig[2], pg]
            for h in range(8):
                j, e_ = h // 2, h % 2
                ar, arR = opA(j, e_, c * C2, (c + 1) * C2)
                bt, bkR = opB(j, e_, c * C2, c * C2 + C)
                kt_, _ = opB(j, e_, c * C2 + C, (c + 1) * C2)
                at_, _ = opA(j, e_, c * C2, c * C2 + C)
                hh = h % 4
                P.mm(pA1[h // 4][0:C, hh * C2:(hh + 1) * C2], bt, ar, True, True, r=[bkR, arR], w=[pA1[h // 4]])
                P.mm(pA2[h // 4][0:C, hh * C2:(hh + 1) * C2], kt_, ar, True, True, r=[bkR, arR], w=[pA2[h // 4]])
                P.mm(pu[0:C, h * C:(h + 1) * C], at_, bt, True, True, r=[arR, bkR], w=[pu])
            if _RW_LEVEL < 2.12:
                continue
            for (ps2, dsts) in ((pA1, A1s), (pA2, A2s)):
                for half in range(2):
                    src3 = hv(ps2[half][0:C, 0:4 * C2], C2)
                    dst3 = hv(dsts[0:C, half * 4 * C2:(half + 1) * 4 * C2], C2)
                    P.op(DVE, lambda e, src3=src3, dst3=dst3: e.tensor_tensor(dst3[:, :, 0:C], src3[:, :, 0:C],
                                                                             su_m.to_broadcast([C, 4, C]), ALU.mult),
                         r=[ps2[half], cst], w=[dsts])
                    P.op(DVE, lambda e, src3=src3, dst3=dst3: e.tensor_tensor(dst3[:, :, C:C2], src3[:, :, C:C2],
                                                                             iu_m.to_broadcast([C, 4, C]), ALU.mult),
                         r=[ps2[half], cst], w=[dsts])
            P.op(DVE, lambda e: e.tensor_tensor(hv(NTs[0:C, :], C), hv(pu[0:C, 0:8 * C], C), sl_m.to_broadcast([C, 8, C]), ALU.mult),
                 r=[pu, cst], w=[NTs])
            if _RW_LEVEL < 2.2:
                continue
            A1v = hv(A1s[0:C, :], C2)
            P.op(DVE, lambda e: e.tensor_tensor(hv(Tm[0:C, :], C), A1v[:, :, 0:C], id_m.to_broadcast([C, 8, C]), ALU.add),
                 r=[A1s, cst], w=[Tm])
            P.op(DVE, lambda e: e.tensor_tensor(hv(TTm[0:C, :], C), hv(NTs[0:C, :], C), id_m.to_broadcast([C, 8, C]), ALU.add),
                 r=[NTs, cst], w=[TTm])
            Xc = (A1s, lambda h: A1s[0:C, h * C2:h * C2 + C])
            XTc = (NTs, lambda h: NTs[0:C, h * C:(h + 1) * C])
            for lev in range(nlev):
                Xn, XTn = Xb[lev % 2], XTb[lev % 2]
                lastl = (lev == nlev - 1)
                for h in range(8):
                    P.mm(px[0:C, h * C:(h + 1) * C], XTc[1](h), Xc[1](h), True, True, r=[XTc[0], Xc[0]], w=[px])
                P.op(ACT, lambda e, Xn=Xn: e.copy(Xn[0:C, :], px[0:C, 0:8 * C]), r=[px], w=[Xn])
                if not lastl:
                    for h in range(8):
                        P.mm(pu[0:C, h * C:(h + 1) * C], Xc[1](h), XTc[1](h), True, True, r=[XTc[0], Xc[0]], w=[pu])
                    P.op(DVE, lambda e, XTn=XTn: e.tensor_copy(XTn[0:C, :], pu[0:C, 0:8 * C]), r=[pu], w=[XTn])
                for h in range(8):
                    P.mm(pg[0:C, h * C:(h + 1) * C], TTm[0:C, h * C:(h + 1) * C], Xn[0:C, h * C:(h + 1) * C], True, True,
                         r=[TTm, Xn], w=[pg])
                if not lastl:
                    pq = pbig[lev % 3]
                    for h in range(8):
                        P.mm(pq[0:C, h * C:(h + 1) * C], Xn[0:C, h * C:(h + 1) * C], TTm[0:C, h * C:(h + 1) * C], True, True,
                             r=[TTm, Xn], w=[pq])
                P.op(DVE, lambda e: e.tensor_tensor(Tm[0:C, :], Tm[0:C, :], pg[0:C, 0:8 * C], ALU.add), r=[Tm, pg], w=[Tm])
                if not lastl:
                    P.op(DVE, lambda e, pq=pq: e.tensor_tensor(TTm[0:C, :], TTm[0:C, :], pq[0:C, 0:8 * C], ALU.add), r=[TTm, pq], w=[TTm])
                    Xc = (Xn, lambda h, Xn=Xn: Xn[0:C, h * C:(h + 1) * C])
                    XTc = (XTn, lambda h, XTn=XTn: XTn[0:C, h * C:(h + 1) * C])
            if _RW_LEVEL < 2.3:
                continue
            for h in range(8):
                j, e_ = h // 2, h % 2
                at_, arR = opA(j, e_, c * C2, c * C2 + C)
                P.mm(px[0:C, h * 64:(h + 1) * 64], at_, rSb[:, h, :], True, False, r=[arR, rSb], w=[px])
                P.mm(px[0:C, h * 64:(h + 1) * 64], A2s[0:C, h * C2:h * C2 + C], tv[0:C, h * 64:(h + 1) * 64], False, True,
                     r=[A2s, tv], w=[px])
            P.op(ACT, lambda e: e.copy(XtS[0:C, :], px[0:C, :]), r=[px], w=[XtS])
            for h in range(8):
                P.mm(pu[0:C, h * 64:(h + 1) * 64], Tm[0:C, h * C:(h + 1) * C], XtS[0:C, h * 64:(h + 1) * 64], True, True,
                     r=[Tm, XtS], w=[pu])
            P.op(DVE, lambda e: e.tensor_copy(UtS[0:C, :], pu[0:C, :]), r=[pu], w=[UtS])
            if _RW_LEVEL < 2.4:
                continue
            for h in range(8):
                j, e_ = h // 2, h % 2
                rt_, arR = opA(j, e_, c * C2 + C, (c + 1) * C2)
                o_ = pg[0:C, h * 64:(h + 1) * 64]
                P.mm(o_, rt_, rSb[:, h, :], True, False, r=[arR, rSb], w=[pg])
                P.mm(o_, A1s[0:C, h * C2 + C:(h + 1) * C2], UtS[0:C, h * 64:(h + 1) * 64], False, False, r=[A1s, UtS], w=[pg])
                P.mm(o_, A2s[0:C, h * C2 + C:(h + 1) * C2], tv[0:C, h * 64:(h + 1) * 64], False, True, r=[A2s, tv], w=[pg])
            P.op(ACT, lambda e: e.copy(ysb[0:C, :], pg[0:C, :]), r=[pg], w=[ysb])
            if _RW_LEVEL < 2.5:
                continue
            pS = pbig[c % 3]
            for h in range(8):
                hs = slice(h * 64, (h + 1) * 64)
                P.mm(pS[0:64, hs], tb[0:C, hs], UtS[0:C, hs], True, False, r=[tb, UtS], w=[pS])
                P.mm(pS[0:64, hs], tk[0:C, hs], tv[0:C, hs], False, True, r=[tk, tv], w=[pS])
            for h in range(8):
                dc, dcR = decay(h // 2, h % 2, c)
                P.op(DVE, lambda e, h=h, pS=pS, dc=dc: e.scalar_tensor_tensor(
                    rS[:, h, :], rS[:, h, :], dc, pS[0:64, h * 64:(h + 1) * 64], ALU.mult, ALU.add), r=[rS, dcR, pS], w=[rS])
            P.op(ACT, lambda e: e.copy(rSb[:, :, :], rS[:, :, :]), r=[rS], w=[rSb])
            if _RW_LEVEL < 2.6:
                continue
            y3 = hv(ysb[0:C, :], 64)
            P.op(DVE, lambda e, y3=y3: e.tensor_reduce(gst[0:C, 0:8], y3, AX.X, ALU.add), r=[ysb], w=[gst])
            P.op(DVE, lambda e: e.tensor_tensor(ysq[0:C, :], ysb[0:C, :], ysb[0:C, :], ALU.mult), r=[ysb], w=[ysq])
            P.op(DVE, lambda e: e.tensor_reduce(gst[0:C, 8:16], hv(ysq[0:C, :], 64), AX.X, ALU.add), r=[ysq], w=[gst])
            P.op(DVE, lambda e: e.tensor_scalar(gst[0:C, 16:32], gst[0:C, 0:16], 1.0 / 64, None, ALU.mult), r=[gst], w=[gst])
            P.op(DVE, lambda e: e.tensor_tensor(gst[0:C, 0:8], gst[0:C, 16:24], gst[0:C, 16:24], ALU.mult), r=[gst], w=[gst])
            P.op(DVE, lambda e: e.tensor_tensor(gst[0:C, 8:16], gst[0:C, 24:32], gst[0:C, 0:8], ALU.subtract), r=[gst], w=[gst])
            P.op(DVE, lambda e: e.tensor_scalar(gst[0:C, 8:16], gst[0:C, 8:16], 64e-5, None, ALU.add), r=[gst], w=[gst])
            P.op(ACT, lambda e: e.activation(gst[0:C, 0:8], gst[0:C, 8:16], AF.Sqrt), r=[gst], w=[gst])
            P.op(DVE, lambda e: e.reciprocal(gst[0:C, 8:16], gst[0:C, 0:8]), r=[gst], w=[gst])
            P.op(DVE, lambda e, y3=y3: e.tensor_tensor(y3, y3, gst[0:C, 16:24].unsqueeze(2).to_broadcast([C, 8, 64]), ALU.subtract),
                 r=[ysb, gst], w=[ysb])
            P.op(DVE, lambda e, y3=y3: e.tensor_tensor(hv(ynb[0:C, :], 64), y3, gst[0:C, 8:16].unsqueeze(2).to_broadcast([C, 8, 64]), ALU.mult),
                 r=[ysb, gst], w=[ynb])
            pt = next_ptr()
            for j in range(4):
                P.tr(pt[:, j * C:(j + 1) * C], ynb[0:C, j * 128:(j + 1) * 128], CB("ident", C, 0, C), r=[ynb, cstb], w=[pt])
            P.op(ACT, lambda e, pt=pt, cs=cs: e.copy(YN[:, :].rearrange("p (j t) -> p j t", j=4)[:, :, cs], hv(pt[:, 0:4 * C], C)),
                 r=[pt], w=[YN])
            if _RW_LEVEL < 2.7:
                continue
            if kind == "s" or (last_blk and c == nch - 1):
                for h in range(8):
                    P.tr(px[0:64, h * 64:(h + 1) * 64], rS[:, h, :], CF("ident", 64, 0, 64), r=[rS, cst], w=[px])
                P.op(ACT, lambda e: e.copy(Sout[:, :], px[0:64, :]), r=[px], w=[Sout])
                dst_ = (wkv_s.t[l, c] if kind == "s" else wkv_p.t[l]).rearrange("h v k -> v h k")
                P.dma(dst_, Sout[:, :].rearrange("p (h k) -> p h k", h=8), r=[Sout], w=[wkv_s if kind == "s" else wkv_p], out=True)
        for j in range(4):
            col = lambda nm: pcol[:, PC[nm] + j:PC[nm] + j + 1]
            P.op(DVE, lambda e, j=j, s1=col("gn_w"), s2=col("gn_b"): e.tensor_scalar(fin[:, 0:NT], YN[:, j * NT:(j + 1) * NT], s1, s2,
                                                                                  ALU.mult, ALU.add), r=[YN, pcol], w=[fin])
            P.op(DVE, lambda e, j=j: e.tensor_tensor(fin[:, 0:NT], fin[:, 0:NT], bon[j][:, :], ALU.add), r=[fin, bon[j]], w=[fin])
            P.op(DVE, lambda e, j=j: e.tensor_tensor(yT[:, j, 0:NT], fin[:, 0:NT], gsb[j][:, :], ALU.mult), r=[fin, gsb[j]], w=[yT])

    blocks = [("p", i * 512, 512) for i in range(4)] + [("s", 0, 64)]
    if mini:
        blocks = mini
    for l in range(1 if mini else DEPTH):
        arena_fence()
        arena_reset()
        layer_params(l)
        for (kind, tok0, NT) in blocks:
            tiles = [(i * 128, 128) for i in range(NT // 128)] if kind == "p" else [(0, 64)]
            if l == 0:
                xsrc = xp if kind == "p" else xs
            else:
                xsrc = xmid_p if kind == "p" else xmid_s
            if l == DEPTH - 1:
                xdst = y_p if kind == "p" else y_s
            else:
                xdst = xmid_p if kind == "p" else xmid_s
            pl = pp if kind == "p" else psm
            bcast_load(gbc, W["ln_mix_pre"], l)
            for (o, m) in tiles:
                P.dma(xa[0:m, :], xsrc[tok0 + o:tok0 + o + m, :], r=[xsrc], w=[xa])
                rs = rms_rstd(xa, xa[0:m, :], m, D, 0, xb, xb[0:m, :], 1.0 / D, epsn)
                P.op(DVE, lambda e, m=m, rs=rs: e.scalar_tensor_tensor(xnb[0:m, :], xa[0:m, :], rs, gbc[0:m, :],
                                                                    ALU.mult, ALU.mult), r=[xa, stat, gbc], w=[xnb])
                to_featmajor(xnb, lambda kc, m=m: xnb[0:m, kc * 128:(kc + 1) * 128], m, o, hT)
            if stage < 4:
                for kc in range(16):
                    P.op(DVE, lambda e, kc=kc: e.tensor_scalar(yT[:, kc, :], hT[:, kc, :], 0.0, None, ALU.mult), r=[hT], w=[yT])
            if mini:
                arena_fence()
                arena_reset()
                mixer_rwkv(l, kind, tok0, NT)
                arena_fence()
                for j in range(4):
                    P.op(ACT, lambda e, j=j, NT=NT: e.copy(xa[:, j * 512:j * 512 + NT], yT[:, j, 0:NT]), r=[yT], w=[xa])
                P.dma(dbg_d[0, :, :], xa[:, :], r=[xa], w=[dbg_d], out=True)
                continue
            if stage >= 1:
                arena_fence()
                arena_reset()
                mixer_pool(l, kind, tok0, NT)
            if stage >= 2:
                arena_fence()
                arena_reset()
                mixer_sgu(l, kind, tok0, NT, tiles)
            if stage >= 3:
                arena_fence()
                arena_reset()
                mixer_hgrn(l, kind, tok0, NT)
            if stage >= 4:
                arena_fence()
                arena_reset()
                mixer_rwkv(l, kind, tok0, NT)
            arena_fence()
            for db in range(4):
                s = load_w(W["w_out"], l, 0, 16, db * 512, 512)
                for ti, (o, m) in enumerate(tiles):
                    pb = next_pbig()
                    for kc in range(16):
                        P.mm(pb[0:m, :], yT[:, kc, o:o + m], s[:, kc, :], kc == 0, kc == 15, r=[yT, s], w=[pb])
                    P.op(ACT, lambda e, pb=pb, ti=ti, db=db, m=m: e.copy(big[0:m, ti, db * 512:(db + 1) * 512], pb[0:m, :]),
                         r=[pb], w=[big])
            bcast_load(gbc, W["ln_mix_post"], l)
            for ti, (o, m) in enumerate(tiles):
                dbg_on = False
                if dbg_on:
                    P.dma(dbg_d[0, :, :], big[:, 0, :], r=[big], w=[dbg_d])
                    for kc in range(4):
                        P.op(ACT, lambda e, kc=kc: e.copy(xa[:, kc * 512:(kc + 1) * 512], yT[:, kc * 4, :]), r=[yT], w=[xa])
                    P.dma(dbg_d[5, :, :], xa[:, :], r=[xa], w=[dbg_d])
                rs = rms_rstd(big, big[0:m, ti, :], m, D, 0, xb, xb[0:m, :], 1.0 / D, epsn)
                P.dma(xa[0:m, :], xsrc[tok0 + o:tok0 + o + m, :], r=[xsrc], w=[xa])
                P.op(DVE, lambda e, m=m, ti=ti, rs=rs: e.scalar_tensor_tensor(xb[0:m, :], big[0:m, ti, :], rs, gbc[0:m, :],
                                                                          ALU.mult, ALU.mult), r=[big, stat, gbc], w=[xb])
                if dbg_on:
                    P.dma(dbg_d[1, :, :], xb[:, :], r=[xb], w=[dbg_d])
                    P.dma(dbg_d[2, :, :], xa[:, :], r=[xa], w=[dbg_d])
                    P.dma(dbg_d[4, :, 0:8], stat[:, :], r=[stat], w=[dbg_d])
                P.op(DVE, lambda e, m=m, ti=ti: e.tensor_tensor(big[0:m, ti, :], xa[0:m, :], xb[0:m, :], ALU.add),
                     r=[xa, xb], w=[big])
                if dbg_on:
                    P.dma(dbg_d[3, :, :], big[:, 0, :], r=[big], w=[dbg_d])
                P.dma(xf1_d[o:o + m, :], big[0:m, ti, :], r=[big], w=[xf1_d])
                rs = rms_rstd(big, big[0:m, ti, :], m, D, 4, xb, xb[0:m, :], 1.0 / D, epsn)
                P.op(DVE, lambda e, m=m, ti=ti, rs=rs: e.tensor_scalar(xnb[0:m, :], big[0:m, ti, :], rs, None, ALU.mult),
                     r=[big, stat], w=[xnb])
                to_featmajor(xnb, lambda kc, m=m: xnb[0:m, kc * 128:(kc + 1) * 128], m, o, hT, gain=gcol)
            dbg4 = _DEBUG and l == 0 and kind == "p" and tok0 == 0
            if dbg4:
                P.op(ACT, lambda e: e.copy(xa[:, 0:512], hT[:, 0, :]), r=[hT], w=[xa])
                P.op(ACT, lambda e: e.copy(xa[:, 512:1024], hT[:, 5, :]), r=[hT], w=[xa])
                P.dma(dbg_d[0, :, 0:1024], xa[:, 0:1024], r=[xa], w=[dbg_d])
            arena_fence()
            ppairs = [(pg, pu), (pbig[0], pbig[1])]
            sbufs = [silu_t, silu_b]
            for fs in range(DFF // 256):
                sgu_ = wslot[slot_rr[0]]
                slot_rr[0] = (slot_rr[0] + 1) % NSLOT
                wsrc = W["ffn_w_gu"].t[l].rearrange("(kc p) c -> p kc c", p=128)
                P.dma(sgu_[:, :, 0:256], wsrc[:, :, fs * 256:(fs + 1) * 256], r=[W["ffn_w_gu"]], w=[sgu_], eng=POOL)
                P.dma(sgu_[:, :, 256:512], wsrc[:, :, DFF + fs * 256:DFF + (fs + 1) * 256], r=[W["ffn_w_gu"]], w=[sgu_], eng=POOL)
                for fc in range(2):
                    fidx = fs * 2 + fc
                    pgx, pux = ppairs[fidx % 2]
                    sbx = sbufs[fidx % 2]
                    for kc in range(16):
                        P.mm(pgx[:, 0:NT], sgu_[:, kc, fc * 128:(fc + 1) * 128], hT[:, kc, 0:NT], kc == 0, kc == 15,
                             r=[sgu_, hT], w=[pgx])
                    for kc in range(16):
                        P.mm(pux[:, 0:NT], sgu_[:, kc, 256 + fc * 128:256 + (fc + 1) * 128], hT[:, kc, 0:NT], kc == 0, kc == 15,
                             r=[sgu_, hT], w=[pux])
                    P.op(ACT, lambda e, NT=NT, sbx=sbx, pgx=pgx: e.activation(sbx[:, 0:NT], pgx[:, 0:NT], AF.Silu), r=[pgx], w=[sbx])
                    P.op(DVE, lambda e, fidx=fidx, NT=NT, sbx=sbx, pux=pux: e.tensor_tensor(actT[:, fidx, 0:NT], sbx[:, 0:NT], pux[:, 0:NT], ALU.mult),
                         r=[sbx, pux], w=[actT])
            if dbg4:
                P.op(ACT, lambda e: e.copy(xa[:, 0:512], actT[:, 0, :]), r=[actT], w=[xa])
                P.op(ACT, lambda e: e.copy(xa[:, 512:1024], actT[:, 43, :]), r=[actT], w=[xa])
                P.op(ACT, lambda e: e.copy(xa[:, 1024:1536], silu_t[:, :]), r=[silu_t], w=[xa])
                P.op(ACT, lambda e: e.copy(xa[:, 1536:2048], pu[:, :]), r=[pu], w=[xa])
                P.dma(dbg_d[1, :, :], xa[:, :], r=[xa], w=[dbg_d])
            accs = [pbig[0], pbig[1], pbig[2], pg]
            for db in range(4):
                for (k0, nk) in ((0, 16), (16, 16), (32, 12)):
                    s_ = load_w(W["ffn_w_down"], l, k0, nk, db * 512, 512)
                    for ti, (o, m) in enumerate(tiles):
                        pb = accs[ti]
                        for q in range(nk):
                            fc = k0 + q
                            P.mm(pb[0:m, :], actT[:, fc, o:o + m], s_[:, q, :], fc == 0, fc == 43, r=[actT, s_], w=[pb])
                for ti, (o, m) in enumerate(tiles):
                    pb = accs[ti]
                    if ti % 2 == 0:
                        P.op(ACT, lambda e, pb=pb, ti=ti, db=db, m=m: e.copy(big[0:m, ti, db * 512:(db + 1) * 512], pb[0:m, :]),
                             r=[pb], w=[big])
                    else:
                        P.op(DVE, lambda e, pb=pb, ti=ti, db=db, m=m: e.tensor_copy(big[0:m, ti, db * 512:(db + 1) * 512], pb[0:m, :]),
                             r=[pb], w=[big])
            arena_fence()
            bcast_load(gbc, W["ln_ffn_post"], l)
            for ti, (o, m) in enumerate(tiles):
                rs = rms_rstd(big, big[0:m, ti, :], m, D, 0, xb, xb[0:m, :], 1.0 / D, epsn)
                P.dma(xa[0:m, :], xf1_d[o:o + m, :], r=[xf1_d], w=[xa])
                dbg6 = False
                if dbg6:
                    P.dma(dbg_d[0, :, :], big[:, ti, :], r=[big], w=[dbg_d])
                    P.dma(dbg_d[2, :, :], xa[:, :], r=[xa], w=[dbg_d])
                P.op(DVE, lambda e, m=m, ti=ti, rs=rs: e.scalar_tensor_tensor(xb[0:m, :], big[0:m, ti, :], rs, gbc[0:m, :],
                                                                          ALU.mult, ALU.mult), r=[big, stat, gbc], w=[xb])
                if dbg6:
                    P.dma(dbg_d[1, :, :], xb[:, :], r=[xb], w=[dbg_d])
                    P.dma(dbg_d[4, :, 0:8], stat[:, :], r=[stat], w=[dbg_d])
                    P.dma(dbg_d[5, :, :], gbc[:, :], r=[gbc], w=[dbg_d])
                P.op(DVE, lambda e, m=m, ti=ti: e.tensor_tensor(big[0:m, ti, :], xa[0:m, :], xb[0:m, :], ALU.add),
                     r=[xa, xb], w=[big])
                if dbg6:
                    P.dma(dbg_d[3, :, :], big[:, ti, :], r=[big], w=[dbg_d])
                P.op(ACT, lambda e, m=m, ti=ti: e.copy(xnb[0:m, :], big[0:m, ti, :]), r=[big], w=[xnb])
                to_featmajor(xnb, lambda kc, m=m: xnb[0:m, kc * 128:(kc + 1) * 128], m, o, yT)
                P.dma(xa[0:m, 0:PLE], pl[l, tok0 + o:tok0 + o + m, :], r=[pl], w=[xa])
                P.op(ACT, lambda e, m=m: e.copy(xnb[0:m, 0:PLE], xa[0:m, 0:PLE]), r=[xa], w=[xnb])
                to_featmajor(xnb, lambda kc, m=m: xnb[0:m, kc * 128:(kc + 1) * 128], m, o, pT, nkc=2)
            for db in range(4):
                sgt = load_w(W["ple_gate"], l, 0, 16, db * 512, 512)
                spj = load_w(W["ple_proj"], l, 0, 2, db * 512, 512)
                for ti, (o, m) in enumerate(tiles):
                    pb = next_pbig()
                    for kc in range(16):
                        P.mm(pb[0:m, :], yT[:, kc, o:o + m], sgt[:, kc, :], kc == 0, kc == 15, r=[yT, sgt], w=[pb])
                    pb2 = next_pbig()
                    for kc in range(2):
                        P.mm(pb2[0:m, :], pT[:, kc, o:o + m], spj[:, kc, :], kc == 0, kc == 1, r=[pT, spj], w=[pb2])
                    P.op(ACT, lambda e, pb=pb, m=m: e.activation(silu_t[0:m, :], pb[0:m, :], AF.Sigmoid), r=[pb], w=[silu_t])
                    P.op(DVE, lambda e, pb2=pb2, m=m: e.tensor_tensor(silu_t[0:m, :], silu_t[0:m, :], pb2[0:m, :], ALU.mult),
                         r=[silu_t, pb2], w=[silu_t])
                    P.op(DVE, lambda e, m=m, ti=ti, db=db: e.tensor_tensor(big[0:m, ti, db * 512:(db + 1) * 512],
                                                                         big[0:m, ti, db * 512:(db + 1) * 512],
                                                                         silu_t[0:m, :], ALU.add), r=[big, silu_t], w=[big])
            for ti, (o, m) in enumerate(tiles):
                P.dma(xdst[tok0 + o:tok0 + o + m, :], big[0:m, ti, :], r=[big], w=[xdst], out=(l == DEPTH - 1))
    stats = P.emit()
    return nc, stats


def kernel(**inp):
    if "nc" not in _BUILT:
        _BUILT["nc"], _BUILT["stats"] = build()
    nc = _BUILT["nc"]
    f = lambda a: np.ascontiguousarray(np.asarray(a, dtype=np.float32))
    wkeys = ["ln_mix_pre", "ln_mix_post", "ln_ffn_pre", "ln_ffn_post", "w_in", "rwkv_mu", "rwkv_w_lora", "rwkv_w0",
             "rwkv_a_lora", "rwkv_a0", "rwkv_g_lora", "rwkv_k_k", "rwkv_k_a", "rwkv_r_k", "rwkv_gn_w", "rwkv_gn_b",
             "sgu_ln_w", "sgu_ln_b", "sgu_w", "sgu_b", "sgu_norm", "hgrn_lb_logits", "hgrn_norm", "pool_w",
             "pool_scale", "w_out", "ffn_w_gu", "ffn_w_down", "ple_gate", "ple_proj"]
    shared = {k: f(inp[k]) for k in wkeys}
    shared["rwkv_r_k"] = shared["rwkv_r_k"].reshape(DEPTH, G)
    shared["consts"] = CONSTS
    in_maps = []
    for c in range(8):
        b = c % 4
        sl = slice(c * NSEQ_S, (c + 1) * NSEQ_S)
        m = dict(shared)
        m["xp"] = f(inp["x_prompt"][b])
        m["xs"] = f(inp["x_sample"][sl]).reshape(64, D)
        m["pp"] = f(inp["p_prompt"][:, b])
        m["psm"] = f(inp["p_sample"][:, sl]).reshape(DEPTH, 64, PLE)
        m["st_wkv"] = f(inp["state_rwkv_wkv"][:, sl])
        m["st_shift"] = f(inp["state_rwkv_shift"][:, sl])
        m["st_hgrn"] = f(inp["state_hgrn"][:, sl])
        m["st_pool"] = f(inp["state_pool"][:, sl])
        in_maps.append(m)
    res = run_bass_kernel_spmd(nc, in_maps, core_ids=list(range(8)))
    R = res.results
    _BUILT["R"] = R
    y_prompt = np.stack([R[b]["y_p"] for b in range(4)], 0)
    y_sample = np.concatenate([R[c]["y_s"].reshape(NSEQ_S, LS, D) for c in range(8)], 0)
    pst = lambda k: np.stack([R[b][k] for b in range(4)], 1)
    sst = lambda k: np.concatenate([R[c][k] for c in range(8)], 1)
    out = (y_prompt, y_sample, pst("wkv_p"), pst("shift_p"), pst("hgrn_p"), pst("pool_p"),
           sst("wkv_s"), sst("shift_s"), sst("hgrn_s"), sst("pool_s"), sst("sguv_s"))
    return tuple(np.ascontiguousarray(o, dtype=np.float32) for o in out)
```

```python
import numpy as np
import concourse.bass as bass
import concourse.mybir as mybir
from concourse.bass_utils import run_bass_kernel_spmd

F32 = mybir.dt.float32
BF16 = mybir.dt.bfloat16
AF = mybir.ActivationFunctionType
ALU = mybir.AluOpType
AX = mybir.AxisListType
PE, DVE, ACT, POOL, SP = "tensor", "vector", "scalar", "gpsimd", "sync"

D = 2048
DEPTH = 2
SEQ = 2048
NSEQ_S = 16
LS = 4
G = 512
RC = 1696
IN_COLS = 5280
DFF = 5632
PLE = 256
C_R, C_S, C_H, C_P = 0, 1696, 2720, 4768
NEG_E05 = -0.6065306597126334


class Res:
    __slots__ = ("name", "t", "last_w", "readers", "const", "dma_w")

    def __init__(self, name, t, const=False):
        self.name = name
        self.t = t
        self.last_w = None
        self.readers = []
        self.const = const
        self.dma_w = []

    def __getitem__(self, idx):
        return self.t[idx]


class Op:
    __slots__ = ("eng", "fn", "deps", "signal", "sigidx", "is_dma", "sem", "target")

    def __init__(self, eng, fn, is_dma):
        self.eng = eng
        self.fn = fn
        self.deps = []
        self.signal = False
        self.sigidx = 0
        self.is_dma = is_dma
        self.sem = None
        self.target = 0


class Prog:
    NDMASEM = 48

    def __init__(self, nc):
        self.nc = nc
        self.ops = []
        self.dma_last = [None] * self.NDMASEM
        self.dma_uses = [0] * self.NDMASEM
        self.dma_rr = 0
        self.out_dmas = []

    def sb(self, name, shape, dt=F32):
        return Res(name, self.nc.alloc_sbuf_tensor(name, list(shape), dt))

    def ps(self, name, shape, dt=F32):
        return Res(name, self.nc.alloc_psum_tensor(name, list(shape), dt))

    def dram(self, name, shape, dt=F32, kind="Internal"):
        return Res(name, self.nc.dram_tensor(name, list(shape), dt, kind=kind).ap())

    def view(self, name, ap):
        return Res(name, ap)

    def op(self, eng, fn, r=(), w=(), dma=False, out=False):
        o = Op(eng, fn, dma)
        deps = {}
        for res in r:
            if res.last_w is not None:
                deps[id(res.last_w)] = (res.last_w, "RAW")
            for dw in res.dma_w:
                deps[id(dw)] = (dw, "RAW")
        for res in w:
            if res.last_w is not None and id(res.last_w) not in deps:
                deps[id(res.last_w)] = (res.last_w, "WAW")
            for dw in res.dma_w:
                if id(dw) not in deps:
                    deps[id(dw)] = (dw, "WAW")
            for rd in res.readers:
                if id(rd) not in deps:
                    deps[id(rd)] = (rd, "WAR")
        for p, kind in deps.values():
            if (not p.is_dma) and (not dma) and p.eng == eng:
                if eng == PE or kind != "RAW":
                    continue
            o.deps.append(p)
        if dma:
            j = self.dma_rr
            self.dma_rr = (j + 1) % self.NDMASEM
            prev = self.dma_last[j]
            if prev is not None and all(prev is not d for d in o.deps):
                o.deps.append(prev)
            self.dma_uses[j] += 1
            o.sem = j
            o.target = 16 * self.dma_uses[j]
            self.dma_last[j] = o
            if out:
                self.out_dmas.append(o)
        for res in r:
            if not res.const:
                if not dma:
                    res.readers = [x for x in res.readers if x.is_dma or x.eng != eng]
                res.readers.append(o)
        for res in w:
            res.last_w = o
            res.readers = []
            if dma:
                res.dma_w.append(o)
                if len(res.dma_w) > 40:
                    res.dma_w = res.dma_w[-40:]
            else:
                res.dma_w = []
        self.ops.append(o)
        return o

    def fence(self, eng, fn, ress):
        return self.op(eng, fn, r=(), w=list(ress))

    def mm(self, out_ap, lhsT, rhs, start, stop, r, w):
        return self.op(PE, lambda e: e.matmul(out_ap, lhsT, rhs, start=start, stop=stop), r=r, w=w)

    def tr(self, out_ap, in_ap, ident_ap, r, w):
        return self.op(PE, lambda e: e.transpose(out_ap, in_ap, ident_ap), r=r, w=w)

    def dma(self, out_ap, in_ap, r, w, eng=SP, out=False, **kw):
        return self.op(eng, lambda e: e.dma_start(out=out_ap, in_=in_ap, **kw), r=r, w=w, dma=True, out=out)

    def emit(self):
        nc = self.nc
        ops = self.ops
        fin = Op(SP, None, False)
        fin.deps = list(self.out_dmas)
        ops.append(fin)
        for o in ops:
            for d in o.deps:
                d.signal = True
        engs = [PE, DVE, ACT, POOL, SP]
        cnt = {e: 0 for e in engs}
        for o in ops:
            if o.signal and not o.is_dma:
                cnt[o.eng] += 1
                o.sigidx = cnt[o.eng]
        esem = {e: nc.alloc_semaphore("es_" + e) for e in engs}
        dsem = [nc.alloc_semaphore("ds_%d" % j) for j in range(self.NDMASEM)]
        per = {e: [o for o in ops if o.eng == e] for e in engs}
        NS = self.NDMASEM

        nw = {x: 0 for x in engs}

        def run(e, engobj):
            seen = {x: 0 for x in engs}
            seend = [0] * NS
            for o in per[e]:
                nw[e] += sum(1 for d in o.deps if ((seend[d.sem] < d.target) if d.is_dma else (seen[d.eng] < d.sigidx)))
                for d in o.deps:
                    if d.is_dma:
                        if seend[d.sem] >= d.target:
                            continue
                        engobj.wait_ge(dsem[d.sem], d.target)
                        seend[d.sem] = d.target
                    else:
                        if seen[d.eng] >= d.sigidx:
                            continue
                        engobj.wait_ge(esem[d.eng], d.sigidx)
                        seen[d.eng] = d.sigidx
                if o.fn is None:
                    continue
                ins = o.fn(engobj)
                if o.is_dma:
                    ins.then_inc(dsem[o.sem], 16)
                elif o.signal:
                    ins.then_inc(esem[e], 1)

        with nc.Block() as block:
            @block.tensor
            def _(e):
                run(PE, e)

            @block.vector
            def _(e):
                run(DVE, e)

            @block.scalar
            def _(e):
                run(ACT, e)

            @block.gpsimd
            def _(e):
                run(POOL, e)

            @block.sync
            def _(e):
                run(SP, e)
        return dict(n_ops=len(ops), per={e: len(per[e]) for e in engs}, sig=cnt, waits=nw)


CONST_COLS = {}


def _make_consts():
    cols = []
    off = [0]

    def add(name, arr):
        a = np.zeros((128, arr.shape[1]), np.float32)
        a[:arr.shape[0]] = arr
        CONST_COLS[name] = (off[0], arr.shape[1])
        off[0] += arr.shape[1]
        cols.append(a)

    add("ident", np.eye(128, dtype=np.float32))
    bo = np.zeros((128, 128), np.float32)
    bo[:64, :64] = 1
    bo[64:, 64:] = 1
    add("blockones", bo)
    add("ones", np.ones((128, 128), np.float32))
    i = np.arange(64)
    add("SU", (i[:, None] < i[None, :]).astype(np.float32))
    add("IU", (i[:, None] <= i[None, :]).astype(np.float32))
    add("SL", (i[:, None] > i[None, :]).astype(np.float32))
    r64 = np.ones((128, 512), np.float32)
    r64[:, ::64] = 0
    add("rst64", r64)
    r4 = np.ones((128, 64), np.float32)
    r4[:, ::4] = 0
    add("rst4", r4)
    t128 = np.arange(128)
    add("TRIL", (t128[:, None] >= t128[None, :]).astype(np.float32))
    pc = np.ones((128, 64), np.float32)
    for gi, win in enumerate((2, 4, 8, 16)):
        pos = np.arange(16)
        pc[:, gi * 16:(gi + 1) * 16] = (win / np.minimum(pos + 1, win))[None, :]
    add("poolc", pc)
    bd = np.zeros((64, 64), np.float32)
    for s in range(16):
        bd[4 * s:4 * s + 4, 4 * s:4 * s + 4] = np.tril(np.ones((4, 4))).T
    add("BD4T", bd.T.copy())
    return np.concatenate(cols, axis=1)


CONSTS = _make_consts()
NCONST = CONSTS.shape[1]

_BUILT = {}
_DEBUG = False
_RW_LEVEL = 3


def build(stage=99, mini=None):
    nc = bass.Bass("TRN2", target_bir_lowering=False)
    P = Prog(nc)
    EI, EO = "ExternalInput", "ExternalOutput"
    d_in = {}

    def din(name, shape):
        d_in[name] = P.dram(name, shape, F32, kind=EI)
        d_in[name].const = True
        return d_in[name]

    xp = din("xp", [SEQ, D])
    xs = din("xs", [64, D])
    pp = din("pp", [DEPTH, SEQ, PLE])
    psm = din("psm", [DEPTH, 64, PLE])
    st_wkv = din("st_wkv", [DEPTH, NSEQ_S, 8, 64, 64])
    st_shift = din("st_shift", [DEPTH, NSEQ_S, RC])
    st_hgrn = din("st_hgrn", [DEPTH, NSEQ_S, 4, 128, 128])
    st_pool = din("st_pool", [DEPTH, NSEQ_S, 15, G])
    consts_d = din("consts", [128, NCONST])
    wnames = dict(ln_mix_pre=[DEPTH, D], ln_mix_post=[DEPTH, D], ln_ffn_pre=[DEPTH, D], ln_ffn_post=[DEPTH, D],
                  w_in=[DEPTH, D, IN_COLS], rwkv_mu=[DEPTH, RC], rwkv_w_lora=[DEPTH, 32, G], rwkv_w0=[DEPTH, G],
                  rwkv_a_lora=[DEPTH, 32, G], rwkv_a0=[DEPTH, G], rwkv_g_lora=[DEPTH, 96, G], rwkv_k_k=[DEPTH, G],
                  rwkv_k_a=[DEPTH, G], rwkv_r_k=[DEPTH, G], rwkv_gn_w=[DEPTH, G], rwkv_gn_b=[DEPTH, G],
                  sgu_ln_w=[DEPTH, G], sgu_ln_b=[DEPTH, G], sgu_w=[DEPTH, 4, 128, 128], sgu_b=[DEPTH, 4, 128],
                  sgu_norm=[DEPTH, G], hgrn_lb_logits=[DEPTH, G], hgrn_norm=[DEPTH, G], pool_w=[DEPTH, 4, 128, 128],
                  pool_scale=[DEPTH, G], w_out=[DEPTH, D, D], ffn_w_gu=[DEPTH, D, 2 * DFF], ffn_w_down=[DEPTH, DFF, D],
                  ple_gate=[DEPTH, D, D], ple_proj=[DEPTH, PLE, D])
    W = {k: din(k, v) for k, v in wnames.items()}
    d_out = {}

    def dout(name, shape):
        d_out[name] = P.dram(name, shape, F32, kind=EO)
        return d_out[name]

    y_p = dout("y_p", [SEQ, D])
    y_s = dout("y_s", [64, D])
    wkv_p = dout("wkv_p", [DEPTH, 8, 64, 64])
    shift_p = dout("shift_p", [DEPTH, RC])
    hgrn_p = dout("hgrn_p", [DEPTH, 4, 128, 128])
    pool_p = dout("pool_p", [DEPTH, 15, G])
    wkv_s = dout("wkv_s", [DEPTH, NSEQ_S, 8, 64, 64])
    shift_s = dout("shift_s", [DEPTH, NSEQ_S, RC])
    hgrn_s = dout("hgrn_s", [DEPTH, NSEQ_S, 4, 128, 128])
    pool_s = dout("pool_s", [DEPTH, NSEQ_S, 15, G])
    sguv_s = dout("sguv_s", [DEPTH, NSEQ_S, LS, G])
    DBG = EO if _DEBUG else "Internal"
    xmid_p = P.dram("xmid_p", [SEQ, D], kind=DBG)
    xmid_s = P.dram("xmid_s", [64, D], kind=DBG)
    xf1_d = P.dram("xf1_d", [512, D], kind=DBG)
    dbg_d = P.dram("dbg_d", [6, 128, D], kind=DBG)

    cst = P.sb("cst", [128, NCONST])
    cstb = P.sb("cstb", [128, 384], BF16)
    P.dma(cst[:, :], consts_d[:, :], r=[consts_d], w=[cst])
    P.dma(cstb[:, :], consts_d[:, 0:384], r=[consts_d], w=[cstb], eng=POOL)
    cst.const = True
    cstb.const = True

    def CF(name, rows=128, c0=0, c1=None):
        o, n = CONST_COLS[name]
        c1 = n if c1 is None else c1
        return cst[0:rows, o + c0:o + c1]

    def CB(name, rows=128, c0=0, c1=None):
        o, n = CONST_COLS[name]
        c1 = n if c1 is None else c1
        return cstb[0:rows, o + c0:o + c1]

    NSLOT = 3
    wslot = [P.sb("wslot%d" % i, [128, 16, 512], BF16) for i in range(NSLOT)]
    slot_rr = [0]
    hT = P.sb("hT", [128, 16, 512], BF16)
    yT = P.sb("yT", [128, 16, 512], BF16)
    xa = P.sb("xa", [128, D])
    xb = P.sb("xb", [128, D])
    xnb = P.sb("xnb", [128, D], BF16)
    gbc = P.sb("gbc", [128, D])
    stat = P.sb("stat", [128, 8])
    gcol = P.sb("gcol", [128, 16])
    epsn = P.sb("epsn", [128, 1])
    P.op(DVE, lambda e: e.memset(epsn[:, :], 1e-6), w=[epsn])
    epsn.const = True
    ARENA_F = 4 * D + 44 * 256
    arena = nc.alloc_sbuf_tensor("arena", [128, ARENA_F], F32)
    big = P.view("big", arena[:, 0:4 * D].rearrange("p (a b) -> p a b", a=4))
    actT = P.view("actT", arena[:, 4 * D:ARENA_F].bitcast(BF16).rearrange("p (a b) -> p a b", a=44))
    pT = P.view("pT", arena[:, 4 * D:4 * D + 512].bitcast(BF16).rearrange("p (a b) -> p a b", a=2))
    aoff = [0]
    mix_views = []
    last_fence = [None]

    def arena_reset():
        aoff[0] = 0
        del mix_views[:]

    def av(name, rows, cols, dt=F32):
        n32 = cols if dt == F32 else (cols + 1) // 2
        a0 = aoff[0]
        aoff[0] += n32
        assert aoff[0] <= ARENA_F, (name, aoff[0])
        ap = arena[0:rows, a0:a0 + n32]
        if dt != F32:
            ap = ap.bitcast(dt)
        r_ = P.view(name, ap)
        r_.last_w = last_fence[0]
        mix_views.append(r_)
        return r_

    fence_t = P.sb("fence_t", [128, 1])

    def arena_fence():
        last_fence[0] = P.fence(DVE, lambda e: e.memset(fence_t[:, :], 0.0), [fence_t, big, actT, pT, xb, silu_b] + mix_views)
    silu_t = P.sb("silu_t", [128, 512])
    silu_b = P.view("silu_b", xb[:, 0:512])
    pbig = [P.ps("pbig%d" % i, [128, 512]) for i in range(3)]
    ptr = [P.ps("ptr%d" % i, [128, 512], BF16) for i in range(2)]
    pg = P.ps("pg", [128, 512])
    pu = P.ps("pu", [128, 512])
    rr = {"pbig": 0, "ptr": 0}

    def next_pbig():
        rr["pbig"] = (rr["pbig"] + 1) % 3
        return pbig[rr["pbig"]]

    def next_ptr():
        rr["ptr"] = (rr["ptr"] + 1) % 2
        return ptr[rr["ptr"]]

    def load_w(wres, l, k0, nk, c0, ncols):
        s = wslot[slot_rr[0]]
        slot_rr[0] = (slot_rr[0] + 1) % NSLOT
        src = wres.t[l].rearrange("(kc p) c -> p kc c", p=128)[:, k0:k0 + nk, c0:c0 + ncols]
        P.dma(s[:, 0:nk, 0:ncols], src, r=[wres], w=[s], eng=POOL)
        return s

    def bcast_load(dst, vec_res, l, n=D, c0=0):
        v = vec_res.t[l]
        src = bass.AP(tensor=v.tensor, offset=v.offset + c0, ap=[[0, 128], [1, n]])
        P.dma(dst[:, 0:n], src, r=[vec_res], w=[dst])

    def rms_rstd(src_res, src_ap, m, ncol, col, scratch_res, scratch_ap, inv_n, eps_res):
        P.op(DVE, lambda e: e.memset(stat[0:m, col:col + 1], 0.0), w=[stat])
        P.op(ACT, lambda e: e.activation(scratch_ap, src_ap, AF.Square, accum_out=stat[0:m, col:col + 1]),
             r=[src_res, stat], w=[scratch_res, stat])
        P.op(ACT, lambda e: e.activation(stat[0:m, col + 1:col + 2], stat[0:m, col:col + 1], AF.Sqrt,
                                         bias=eps_res[0:m, 0:1], scale=inv_n), r=[stat, eps_res], w=[stat])
        P.op(DVE, lambda e: e.reciprocal(stat[0:m, col + 2:col + 3], stat[0:m, col + 1:col + 2]), r=[stat], w=[stat])
        return stat[0:m, col + 2:col + 3]

    def to_featmajor(src_res, src_bf_ap_fn, m, t0, dstT, nkc=16, kc0=0, gain=None):
        for g4 in range(0, nkc, 4):
            n4 = min(4, nkc - g4)
            pt = next_ptr()
            for q in range(n4):
                kc = g4 + q
                P.tr(pt[:, q * 128:q * 128 + m], src_bf_ap_fn(kc), CB("ident", m, 0, m), r=[src_res, cstb], w=[pt])
            o_ap = dstT[:, kc0 + g4:kc0 + g4 + n4, t0:t0 + m]
            i_ap = pt[:, 0:n4 * 128].rearrange("p (a b) -> p a b", a=n4)[:, :, 0:m]
            if gain is not None:
                for q in range(n4):
                    kc = g4 + q
                    oq = dstT[:, kc0 + kc, t0:t0 + m]
                    iq = pt[:, q * 128:q * 128 + m]
                    gq = gain[:, kc:kc + 1]
                    if q % 2 == 0:
                        P.op(DVE, lambda e, oq=oq, iq=iq, gq=gq: e.tensor_scalar(oq, iq, gq, None, ALU.mult), r=[pt, gcol], w=[dstT])
                    else:
                        P.op(ACT, lambda e, oq=oq, iq=iq, gq=gq: e.activation(oq, iq, AF.Copy, scale=gq), r=[pt, gcol], w=[dstT])
            elif (g4 // 4) % 2 == 0:
                P.op(DVE, lambda e, o_ap=o_ap, i_ap=i_ap: e.tensor_copy(o_ap, i_ap), r=[pt], w=[dstT])
            else:
                P.op(ACT, lambda e, o_ap=o_ap, i_ap=i_ap: e.copy(o_ap, i_ap), r=[pt], w=[dstT])


    NCOLP = 80
    pcol = P.sb("pcol", [128, NCOLP])
    PC = {}

    def load_cols(name, vec_res, l, base, n=G):
        nchk = n // 128
        src = vec_res.t[l, 0:n].rearrange("(j p) -> p j", p=128)
        P.dma(pcol[:, base:base + nchk], src, r=[vec_res], w=[pcol], allow_slow_non_contiguous=True)
        PC[name] = base

    pw_sb = P.sb("pw_sb", [128, 4, 128], BF16)
    pool_carry = P.sb("pool_carry", [128, 4, 15])
    sgu_bc = P.sb("sgu_bc", [128, 3, G])
    wmT = P.sb("wmT", [128, 4, 128], BF16)
    bd4 = P.sb("bd4", [64, 4, 64], BF16)
    sgub_p = P.sb("sgub_p", [128, 4])
    sgub_s = P.sb("sgub_s", [64, 4])
    px = P.ps("px", [128, 512])

    def layer_params(l):
        P.dma(gcol[:, :], W["ln_ffn_pre"].t[l].rearrange("(j p) -> p j", p=128), r=[W["ln_ffn_pre"]], w=[gcol],
              allow_slow_non_contiguous=True)
        load_cols("pool_scale", W["pool_scale"], l, 0)
        load_cols("hgrn_norm", W["hgrn_norm"], l, 4)
        load_cols("lg0", W["hgrn_lb_logits"], 0, 8)
        load_cols("lg1", W["hgrn_lb_logits"], 1, 12)
        PC["lb"], PC["oml"] = 16, 20
        load_cols("mu", W["rwkv_mu"], l, 24, n=1536)
        for q, (a_, b_) in enumerate(((1536, 1568), (1568, 1600), (1600, 1696))):
            P.dma(pcol[0:b_ - a_, 36 + q:37 + q], W["rwkv_mu"].t[l, a_:b_].rearrange("(c o) -> c o", o=1), r=[W["rwkv_mu"]], w=[pcol],
                  allow_slow_non_contiguous=True)
        for q, nm in enumerate(("w0", "a0", "k_k", "k_a", "r_k", "gn_w", "gn_b")):
            load_cols(nm, W["rwkv_" + nm], l, 40 + 4 * q)
        PC["omk"] = 68
        P.op(DVE, lambda e: e.tensor_scalar(pcol[:, 68:72], pcol[:, PC["k_a"]:PC["k_a"] + 4], -1.0, 1.0, ALU.mult, ALU.add), r=[pcol], w=[pcol])
        if l == 0:
            P.op(DVE, lambda e: e.memset(pcol[:, 16:20], 0.0), w=[pcol])
            P.op(DVE, lambda e: e.memset(pcol[:, 20:24], 1.0), w=[pcol])
        else:
            P.op(DVE, lambda e: e.tensor_tensor(pcol[:, 16:20], pcol[:, 12:16], pcol[:, 8:12], ALU.subtract), r=[pcol], w=[pcol])
            P.op(ACT, lambda e: e.activation(pcol[:, 16:20], pcol[:, 16:20], AF.Sigmoid), r=[pcol], w=[pcol])
            P.op(DVE, lambda e: e.tensor_scalar(pcol[:, 20:24], pcol[:, 16:20], -1.0, 1.0, ALU.mult, ALU.add), r=[pcol], w=[pcol])
        P.dma(pw_sb[:, :, :], W["pool_w"].t[l].rearrange("g c d -> c g d"), r=[W["pool_w"]], w=[pw_sb], eng=POOL)
        for i, nm in enumerate(("sgu_ln_w", "sgu_ln_b", "sgu_norm")):
            v = W[nm].t[l]
            P.dma(sgu_bc[:, i, :], bass.AP(tensor=v.tensor, offset=v.offset, ap=[[0, 128], [1, G]]), r=[W[nm]], w=[sgu_bc])
        wtmp = P.view("wtmp", arena[:, 0:512].rearrange("p (a b) -> p a b", a=4))
        wtmp.last_w = last_fence[0]
        wtb = P.view("wtb", arena[:, 512:768].bitcast(BF16).rearrange("p (a b) -> p a b", a=4))
        wtb.last_w = last_fence[0]
        mix_views.extend([wtmp, wtb])
        P.dma(wtmp[:, :, :], W["sgu_w"].t[l].rearrange("h t s -> t h s"), r=[W["sgu_w"]], w=[wtmp])
        tril = CF("TRIL")
        P.op(DVE, lambda e: e.tensor_tensor(wtb[:, :, :], wtmp[:, :, :], tril.unsqueeze(1).to_broadcast([128, 4, 128]), ALU.mult),
             r=[wtmp, cst], w=[wtb])
        pt = next_ptr()
        for h in range(4):
            P.tr(pt[:, h * 128:(h + 1) * 128], wtb[:, h, :], CB("ident"), r=[wtb, cstb], w=[pt])
        P.op(DVE, lambda e: e.tensor_copy(wmT[:, :, :], pt[:, :].rearrange("p (a b) -> p a b", a=4)), r=[pt], w=[wmT])
        P.dma(sgub_p[:, :], W["sgu_b"].t[l].rearrange("h t -> t h"), r=[W["sgu_b"]], w=[sgub_p], allow_slow_non_contiguous=True)
        w4 = P.view("w4", arena[0:64, 768:1024].rearrange("p (a b) -> p a b", a=4))
        w4.last_w = last_fence[0]
        w4b = P.view("w4b", arena[0:64, 1024:1152].bitcast(BF16).rearrange("p (a b) -> p a b", a=4))
        w4b.last_w = last_fence[0]
        mix_views.extend([w4, w4b])
        P.op(DVE, lambda e: e.memset(w4[:, :, :], 0.0), w=[w4])
        for sq in range(NSEQ_S):
            P.dma(w4[4 * sq:4 * sq + 4, :, 4 * sq:4 * sq + 4], W["sgu_w"].t[l, :, 0:4, 0:4].rearrange("h t s -> t h s"),
                  r=[W["sgu_w"]], w=[w4], allow_slow_non_contiguous=True)
            P.dma(sgub_s[4 * sq:4 * sq + 4, :], W["sgu_b"].t[l, :, 0:4].rearrange("h t -> t h"), r=[W["sgu_b"]], w=[sgub_s],
                  allow_slow_non_contiguous=True)
        bdt = CF("BD4T", 64)
        P.op(DVE, lambda e: e.tensor_tensor(w4b[:, :, :], w4[:, :, :], bdt.unsqueeze(1).to_broadcast([64, 4, 64]), ALU.mult),
             r=[w4, cst], w=[w4b])
        pt2 = next_ptr()
        for h in range(4):
            P.tr(pt2[0:64, h * 64:(h + 1) * 64], w4b[:, h, :], CB("ident", 64, 0, 64), r=[w4b, cstb], w=[pt2])
        P.op(DVE, lambda e: e.tensor_copy(bd4[:, :, :], pt2[0:64, 0:256].rearrange("p (a b) -> p a b", a=4)), r=[pt2], w=[bd4])

    def inproj_fm(l, col0, ncols, NT, consume):
        c = 0
        ci = 0
        while c < ncols:
            n = min(512, ncols - c)
            s = load_w(W["w_in"], l, 0, 16, col0 + c, n)
            for q in range(0, n, 128):
                rows = min(128, n - q)
                pb = next_pbig()
                for kc in range(16):
                    P.mm(pb[0:rows, 0:NT], s[:, kc, q:q + rows], hT[:, kc, 0:NT], kc == 0, kc == 15, r=[s, hT], w=[pb])
                consume(ci, pb, rows)
                ci += 1
            c += n

    def mixer_pool(l, kind, tok0, NT):
        nseq, L = (1, NT) if kind == "p" else (NSEQ_S, LS)
        E = 15 + L
        Zx = [av("poolZ%d" % g, 128, nseq * E) for g in range(4)]
        Ea = av("poolEa", 128, nseq * E)
        Eb = av("poolEb", 128, nseq * E)
        dbf = av("pooldbf", 128, NT, BF16)
        v3 = lambda r_: r_[:, :].rearrange("p (s e) -> p s e", s=nseq)
        for g in range(4):
            z3 = v3(Zx[g])
            if kind == "p":
                if tok0 == 0:
                    P.op(DVE, lambda e, z3=z3: e.memset(z3[:, :, 0:15], 0.0), w=[Zx[g]])
                else:
                    P.op(DVE, lambda e, z3=z3, g=g: e.tensor_copy(z3[:, 0, 0:15], pool_carry[:, g, :]), r=[pool_carry], w=[Zx[g]])
        if kind == "s":
            rows_ap = st_pool.t[l].rearrange("s p c -> (s p) c")
            for half in range(2):
                P.dma(xa[0:120, half * 512:(half + 1) * 512], rows_ap[half * 120:(half + 1) * 120, :], r=[st_pool], w=[xa])
            for g in range(4):
                z3 = v3(Zx[g])
                for half in range(2):
                    P.tr(px[:, 0:120], xa[0:120, half * 512 + g * 128:half * 512 + (g + 1) * 128], CF("ident", 120, 0, 120),
                         r=[xa, cst], w=[px])
                    P.op(ACT, lambda e, z3=z3, half=half: e.copy(z3[:, half * 8:(half + 1) * 8, 0:15],
                                                                px[:, 0:120].rearrange("p (s e) -> p s e", s=8)), r=[px], w=[Zx[g]])

        def consume(ci, pb, rows):
            z3 = v3(Zx[ci])
            P.op(ACT, lambda e: e.copy(z3[:, :, 15:E], pb[:, 0:NT].rearrange("p (s e) -> p s e", s=nseq)), r=[pb], w=[Zx[ci]])
        inproj_fm(l, C_P, G, NT, consume)
        for g, win in enumerate((2, 4, 8, 16)):
            src = Zx[g]
            bufs = [Ea, Eb]
            for k in range(1, g + 2):
                sft = 1 << (k - 1)
                lo = (1 << k) - 1
                dst = bufs[k % 2]
                s3, d3 = v3(src), v3(dst)
                P.op(DVE, lambda e, s3=s3, d3=d3, lo=lo, sft=sft: e.tensor_tensor(d3[:, :, lo:E], s3[:, :, lo:E], s3[:, :, lo - sft:E - sft], ALU.add),
                     r=[src], w=[dst])
                src = dst
            s3, z3 = v3(src), v3(Zx[g])
            d3 = dbf[:, :].rearrange("p (s e) -> p s e", s=nseq)
            P.op(DVE, lambda e, s3=s3, z3=z3, d3=d3, win=win: e.scalar_tensor_tensor(d3, s3[:, :, 15:E], 1.0 / win, z3[:, :, 15:E], ALU.mult, ALU.subtract),
                 r=[src, Zx[g]], w=[dbf])
            if kind == "p" and tok0 == 0:
                tmpc = bufs[(g + 2) % 2]
                pcv = CF("poolc", 128, g * 16, (g + 1) * 16)
                P.op(DVE, lambda e, s3=s3, tmpc=tmpc, pcv=pcv: e.tensor_tensor(tmpc[:, 0:16], s3[:, 0, 15:31], pcv, ALU.mult), r=[src, cst], w=[tmpc])
                P.op(DVE, lambda e, tmpc=tmpc, z3=z3, win=win: e.scalar_tensor_tensor(dbf[:, 0:16], tmpc[:, 0:16], 1.0 / win, z3[:, 0, 15:31], ALU.mult, ALU.subtract),
                     r=[tmpc, Zx[g]], w=[dbf])
            P.mm(px[:, 0:NT], pw_sb[:, g, :], dbf[:, 0:NT], True, True, r=[pw_sb, dbf], w=[px])
            sc = pcol[:, PC["pool_scale"] + g:PC["pool_scale"] + g + 1]
            P.op(ACT, lambda e, g=g, sc=sc: e.activation(yT[:, 12 + g, 0:NT], px[:, 0:NT], AF.Copy, scale=sc), r=[px, pcol], w=[yT])
            if kind == "p":
                if tok0 + NT < SEQ:
                    P.op(ACT, lambda e, z3=z3, g=g: e.copy(pool_carry[:, g, :], z3[:, 0, L:E]), r=[Zx[g]], w=[pool_carry])
                else:
                    P.dma(pool_p.t[l, :, g * 128:(g + 1) * 128].rearrange("p c -> c p"), z3[:, 0, L:E], r=[Zx[g]], w=[pool_p],
                          out=True, allow_slow_non_contiguous=True)
            else:
                for half in range(2):
                    P.op(ACT, lambda e, z3=z3, half=half: e.copy(Ea[:, 0:120].rearrange("p (s e) -> p s e", s=8),
                                                                z3[:, half * 8:(half + 1) * 8, L:E]), r=[Zx[g]], w=[Ea])
                    P.tr(px[0:120, 0:128], Ea[:, 0:120], CF("ident"), r=[Ea, cst], w=[px])
                    P.op(ACT, lambda e, g=g, half=half: e.copy(xb[0:120, half * 512 + g * 128:half * 512 + (g + 1) * 128], px[0:120, 0:128]),
                         r=[px], w=[xb])
        if kind == "s":
            orows = pool_s.t[l].rearrange("s p c -> (s p) c")
            for half in range(2):
                P.dma(orows[half * 120:(half + 1) * 120, :], xb[0:120, half * 512:(half + 1) * 512], r=[xb], w=[pool_s], out=True)

    def mixer_sgu(l, kind, tok0, NT, tiles):
        su_ = load_w(W["w_in"], l, 0, 16, C_S, 512)
        sv_ = load_w(W["w_in"], l, 0, 16, C_S + 512, 512)
        u_sb = av("sgu_u", 128, G)
        v_sb = av("sgu_v", 128, G)
        t_sb = av("sgu_t", 128, G)
        vnb = av("sgu_vnb", 128, G, BF16)
        ynb = av("sgu_ynb", 128, G, BF16)
        for (o, m) in tiles:
            pbu = next_pbig()
            for kc in range(16):
                P.mm(pbu[0:m, :], hT[:, kc, o:o + m], su_[:, kc, :], kc == 0, kc == 15, r=[hT, su_], w=[pbu])
            pbv = next_pbig()
            for kc in range(16):
                P.mm(pbv[0:m, :], hT[:, kc, o:o + m], sv_[:, kc, :], kc == 0, kc == 15, r=[hT, sv_], w=[pbv])
            P.op(DVE, lambda e, m=m: e.memset(stat[0:m, 0:8], 0.0), w=[stat])
            P.op(ACT, lambda e, m=m, pbu=pbu: e.activation(u_sb[0:m, :], pbu[0:m, :], AF.Gelu), r=[pbu], w=[u_sb])
            P.op(ACT, lambda e, m=m, pbv=pbv: e.activation(v_sb[0:m, :], pbv[0:m, :], AF.Gelu, accum_out=stat[0:m, 0:1]),
                 r=[pbv, stat], w=[v_sb, stat])
            P.op(ACT, lambda e, m=m: e.activation(t_sb[0:m, :], v_sb[0:m, :], AF.Square, accum_out=stat[0:m, 1:2]),
                 r=[v_sb, stat], w=[t_sb, stat])
            P.op(DVE, lambda e, m=m: e.tensor_scalar(stat[0:m, 2:4], stat[0:m, 0:2], 1.0 / G, None, ALU.mult), r=[stat], w=[stat])
            P.op(DVE, lambda e, m=m: e.tensor_tensor(stat[0:m, 4:5], stat[0:m, 2:3], stat[0:m, 2:3], ALU.mult), r=[stat], w=[stat])
            P.op(DVE, lambda e, m=m: e.tensor_tensor(stat[0:m, 5:6], stat[0:m, 3:4], stat[0:m, 4:5], ALU.subtract), r=[stat], w=[stat])
            P.op(DVE, lambda e, m=m: e.tensor_scalar(stat[0:m, 5:6], stat[0:m, 5:6], 1e-5, None, ALU.add), r=[stat], w=[stat])
            P.op(ACT, lambda e, m=m: e.activation(stat[0:m, 6:7], stat[0:m, 5:6], AF.Sqrt), r=[stat], w=[stat])
            P.op(DVE, lambda e, m=m: e.reciprocal(stat[0:m, 7:8], stat[0:m, 6:7]), r=[stat], w=[stat])
            P.op(DVE, lambda e, m=m: e.tensor_scalar(v_sb[0:m, :], v_sb[0:m, :], stat[0:m, 2:3], stat[0:m, 7:8], ALU.subtract, ALU.mult),
                 r=[v_sb, stat], w=[v_sb])
            P.op(DVE, lambda e, m=m: e.tensor_tensor(v_sb[0:m, :], v_sb[0:m, :], sgu_bc[0:m, 0, :], ALU.mult), r=[v_sb, sgu_bc], w=[v_sb])
            P.op(DVE, lambda e, m=m: e.tensor_tensor(v_sb[0:m, :], v_sb[0:m, :], sgu_bc[0:m, 1, :], ALU.add), r=[v_sb, sgu_bc], w=[v_sb])
            if kind == "s":
                P.dma(sguv_s.t[l].rearrange("s t c -> (s t) c"), v_sb[0:m, :], r=[v_sb], w=[sguv_s], out=True)
            P.op(ACT, lambda e, m=m: e.copy(vnb[0:m, :], v_sb[0:m, :]), r=[v_sb], w=[vnb])
            for h in range(4):
                lh = wmT[:, h, :] if kind == "p" else bd4[:, h, :]
                P.mm(px[0:m, h * 128:(h + 1) * 128], lh, vnb[0:m, h * 128:(h + 1) * 128], True, True,
                     r=[wmT if kind == "p" else bd4, vnb], w=[px])
            sbc = (sgub_p if kind == "p" else sgub_s)
            sb_ap = sbc[0:m, 0:4].unsqueeze(2).to_broadcast([m, 4, 128])
            P.op(DVE, lambda e, m=m, sb_ap=sb_ap: e.tensor_tensor(t_sb[0:m, :].rearrange("p (a b) -> p a b", a=4),
                                                                 px[0:m, :].rearrange("p (a b) -> p a b", a=4), sb_ap, ALU.add),
                 r=[px, sbc], w=[t_sb])
            P.op(DVE, lambda e, m=m: e.tensor_tensor(u_sb[0:m, :], u_sb[0:m, :], t_sb[0:m, :], ALU.mult), r=[u_sb, t_sb], w=[u_sb])
            rs = rms_rstd(u_sb, u_sb[0:m, :], m, G, 0, t_sb, t_sb[0:m, :], 1.0 / G, epsn)
            P.op(DVE, lambda e, m=m, rs=rs: e.scalar_tensor_tensor(ynb[0:m, :], u_sb[0:m, :], rs, sgu_bc[0:m, 2, :], ALU.mult, ALU.mult),
                 r=[u_sb, stat, sgu_bc], w=[ynb])
            to_featmajor(ynb, lambda kc, m=m: ynb[0:m, kc * 128:(kc + 1) * 128], m, o, yT, nkc=4, kc0=4)

    hS = P.sb("hS", [128, 4, 128])
    hSb = P.sb("hSb", [128, 4, 128], BF16)

    def mixer_hgrn(l, kind, tok0, NT):
        C = 64 if kind == "p" else LS
        nch = NT // C
        rst = CF("rst64") if kind == "p" else CF("rst4")
        qf = [av("h_q%d" % h, 128, NT) for h in range(4)]
        gate = [av("h_g%d" % h, 128, NT) for h in range(4)]
        qt = [av("h_qt%d" % h, 128, NT, BF16) for h in range(4)]
        kt = [av("h_kt%d" % h, 128, NT, BF16) for h in range(4)]
        kh = [av("h_kh%d" % h, 128, NT, BF16) for h in range(4)]
        vb = [av("h_vb%d" % h, 128, NT, BF16) for h in range(4)]
        egC = [av("h_egC%d" % h, 128, 16) for h in range(4)]
        T = [av("h_T%d" % i, 128, NT) for i in range(4)]
        Osb = av("h_O", 128, 4 * NT)
        TMK = [av("h_TMK%d" % i, 64, 512, BF16) for i in range(2)]
        TMV = [av("h_TMV%d" % i, 64, 512, BF16) for i in range(2)]
        ATs = [av("h_ATs%d" % i, 64, 4 * C, BF16) for i in range(2)]
        osq = av("h_osq", 128, NT, BF16)
        lbc, omc, hnc = PC["lb"], PC["oml"], PC["hgrn_norm"]
        c3 = lambda ap: ap.rearrange("p (n c) -> p n c", c=C)

        def consume(ci, pb, rows):
            grp, h = ci // 4, ci % 4
            if grp == 0:
                P.op(ACT, lambda e: e.activation(qf[h][:, :], pb[:, 0:NT], AF.Silu), r=[pb], w=[qf[h]])
            elif grp == 1:
                t0, t1, t2 = T[0], T[1], T[2]
                P.op(ACT, lambda e: e.activation(t0[:, :], pb[:, 0:NT], AF.Sigmoid), r=[pb], w=[t0])
                P.op(DVE, lambda e: e.tensor_scalar(t0[:, :], t0[:, :], pcol[:, omc + h:omc + h + 1], pcol[:, lbc + h:lbc + h + 1],
                                                    ALU.mult, ALU.add), r=[t0, pcol], w=[t0])
                P.op(ACT, lambda e: e.activation(t1[:, :], t0[:, :], AF.Ln), r=[t0], w=[t1])
                P.op(DVE, lambda e: e.tensor_scalar(t0[:, :], t0[:, :], -1.0, 1.0, ALU.mult, ALU.add), r=[t0], w=[t0])
                P.op(DVE, lambda e: e.tensor_tensor_scan(t2[:, :], rst[:, 0:NT], t1[:, :], 0.0, ALU.mult, ALU.add),
                     r=[cst, t1], w=[t2])
                P.op(ACT, lambda e: e.activation(t1[:, :], t2[:, :], AF.Exp), r=[t2], w=[t1])
                P.op(DVE, lambda e: e.tensor_tensor(qt[h][:, :], qf[h][:, :], t1[:, :], ALU.mult), r=[qf[h], t1], w=[qt[h]])
                P.op(ACT, lambda e: e.activation(t1[:, :], t2[:, :], AF.Exp, scale=-1.0), r=[t2], w=[t1])
                P.op(DVE, lambda e: e.tensor_tensor(kt[h][:, :], t0[:, :], t1[:, :], ALU.mult), r=[t0, t1], w=[kt[h]])
                cum3 = c3(t2[:, :])
                P.op(ACT, lambda e: e.activation(egC[h][:, 0:nch], cum3[:, :, C - 1], AF.Exp), r=[t2], w=[egC[h]])
                P.op(DVE, lambda e: e.tensor_tensor(c3(t1[:, :]), cum3[:, :, C - 1:C].to_broadcast([128, nch, C]), cum3, ALU.subtract),
                     r=[t2], w=[t1])
                P.op(ACT, lambda e: e.activation(t1[:, :], t1[:, :], AF.Exp), r=[t1], w=[t1])
                P.op(DVE, lambda e: e.tensor_tensor(kh[h][:, :], t0[:, :], t1[:, :], ALU.mult), r=[t0, t1], w=[kh[h]])
            elif grp == 2:
                P.op(ACT, lambda e: e.copy(vb[h][:, :], pb[:, 0:NT]), r=[pb], w=[vb[h]])
            else:
                P.op(ACT, lambda e: e.activation(gate[h][:, :], pb[:, 0:NT], AF.Silu), r=[pb], w=[gate[h]])
        inproj_fm(l, C_H, 4 * G, NT, consume)

        for c in range(nch):
            cs = slice(c * C, (c + 1) * C)
            tk, tv, at = TMK[c % 2], TMV[c % 2], ATs[c % 2]
            if kind == "s":
                P.dma(hS[:, :, :], st_hgrn.t[l, c].rearrange("h k v -> k h v"), r=[st_hgrn], w=[hS])
                P.op(ACT, lambda e: e.copy(hSb[:, :, :], hS[:, :, :]), r=[hS], w=[hSb])
            elif tok0 == 0 and c == 0:
                P.op(DVE, lambda e: e.memset(hS[:, :, :], 0.0), w=[hS])
                P.op(ACT, lambda e: e.copy(hSb[:, :, :], hS[:, :, :]), r=[hS], w=[hSb])
            ptk = next_ptr()
            for h in range(4):
                P.tr(ptk[0:C, h * 128:(h + 1) * 128], kh[h][:, cs], CB("ident"), r=[kh[h], cstb], w=[ptk])
            P.op(DVE, lambda e, ptk=ptk, tk=tk: e.tensor_copy(tk[0:C, :], ptk[0:C, :]), r=[ptk], w=[tk])
            ptv = next_ptr()
            for h in range(4):
                P.tr(ptv[0:C, h * 128:(h + 1) * 128], vb[h][:, cs], CB("ident"), r=[vb[h], cstb], w=[ptv])
            P.op(ACT, lambda e, ptv=ptv, tv=tv: e.copy(tv[0:C, :], ptv[0:C, :]), r=[ptv], w=[tv])
            for h in range(4):
                P.mm(px[0:C, h * C:(h + 1) * C], kt[h][:, cs], qt[h][:, cs], True, True, r=[kt[h], qt[h]], w=[px])
            iu = CF("IU", C, 0, C)
            P.op(DVE, lambda e, at=at, iu=iu: e.tensor_tensor(at[0:C, :].rearrange("p (h c) -> p h c", h=4),
                                                              px[0:C, 0:4 * C].rearrange("p (h c) -> p h c", h=4),
                                                              iu.unsqueeze(1).to_broadcast([C, 4, C]), ALU.mult), r=[px, cst], w=[at])
            for h in range(4):
                P.mm(pg[:, h * C:(h + 1) * C], tv[0:C, h * 128:(h + 1) * 128], at[0:C, h * C:(h + 1) * C], True, False,
                     r=[tv, at], w=[pg])
                P.mm(pg[:, h * C:(h + 1) * C], hSb[:, h, :], qt[h][:, cs], False, True, r=[hSb, qt[h]], w=[pg])
            P.op(ACT, lambda e, cs=cs: e.copy(Osb[:, :].rearrange("p (h t) -> p h t", h=4)[:, :, cs],
                                              pg[:, 0:4 * C].rearrange("p (h c) -> p h c", h=4)), r=[pg], w=[Osb])
            for h in range(4):
                P.mm(pu[:, h * 128:(h + 1) * 128], tk[0:C, h * 128:(h + 1) * 128], tv[0:C, h * 128:(h + 1) * 128], True, True,
                     r=[tk, tv], w=[pu])
            for h in range(4):
                P.op(DVE, lambda e, h=h, c=c: e.scalar_tensor_tensor(hS[:, h, :], hS[:, h, :], egC[h][:, c:c + 1],
                                                                    pu[:, h * 128:(h + 1) * 128], ALU.mult, ALU.add),
                     r=[hS, egC[h], pu], w=[hS])
            P.op(ACT, lambda e: e.copy(hSb[:, :, :], hS[:, :, :]), r=[hS], w=[hSb])
            if kind == "s":
                P.dma(hgrn_s.t[l, c].rearrange("h k v -> k h v"), hS[:, :, :], r=[hS], w=[hgrn_s], out=True)
            elif tok0 + NT == SEQ and c == nch - 1:
                P.dma(hgrn_p.t[l].rearrange("h k v -> k h v"), hS[:, :, :], r=[hS], w=[hgrn_p], out=True)
        for h in range(4):
            o_ap = Osb[:, h * NT:(h + 1) * NT]
            P.op(DVE, lambda e, o_ap=o_ap: e.tensor_tensor(osq[:, :], o_ap, o_ap, ALU.mult), r=[Osb], w=[osq])
            pb = next_pbig()
            P.mm(pb[:, 0:NT], CB("ones"), osq[:, :], True, True, r=[cstb, osq], w=[pb])
            t0 = T[0]
            P.op(ACT, lambda e, pb=pb, t0=t0: e.activation(t0[:, :], pb[:, 0:NT], AF.Sqrt, bias=epsn[:, 0:1], scale=1.0 / 128),
                 r=[pb, epsn], w=[t0])
            P.op(DVE, lambda e, t0=t0: e.reciprocal(t0[:, :], t0[:, :]), r=[t0], w=[t0])
            P.op(DVE, lambda e, t0=t0, o_ap=o_ap: e.tensor_tensor(t0[:, :], t0[:, :], o_ap, ALU.mult), r=[t0, Osb], w=[t0])
            P.op(DVE, lambda e, t0=t0, h=h: e.scalar_tensor_tensor(yT[:, 8 + h, 0:NT], t0[:, :], pcol[:, hnc + h:hnc + h + 1],
                                                                   gate[h][:, :], ALU.mult, ALU.mult), r=[t0, pcol, gate[h]], w=[yT])

    rS = P.sb("rS", [64, 8, 64])
    rSb = P.sb("rSb", [64, 8, 64], BF16)
    shcar = P.sb("shcar", [128, 16])

    def mixer_rwkv(l, kind, tok0, NT):
        C = 64 if kind == "p" else LS
        nch = NT // C
        nseq, L = (1, NT) if kind == "p" else (NSEQ_S, LS)
        E = L + 1
        C2 = 2 * C
        rst = CF("rst64") if kind == "p" else CF("rst4")
        last_blk = (kind == "p" and tok0 + NT == SEQ)
        c3 = lambda ap: ap.rearrange("p (n c) -> p n c", c=C)
        AR = [av("r_AR%d" % j, 128, 2 * NT, BF16) for j in range(4)]
        BK = [av("r_BK%d" % j, 128, 2 * NT, BF16) for j in range(4)]
        Bh = [av("r_Bh%d" % j, 128, NT, BF16) for j in range(4)]
        Kh = [av("r_Kh%d" % j, 128, NT, BF16) for j in range(4)]
        Vb = [av("r_Vb%d" % j, 128, NT, BF16) for j in range(4)]
        gsb = [av("r_g%d" % j, 128, NT) for j in range(4)]
        bon = [av("r_bon%d" % j, 128, NT) for j in range(4)]
        gC = [av("r_gC%d" % j, 128, 16) for j in range(4)]
        YN = av("r_YN", 128, 4 * NT, BF16)
        mark = aoff[0]
        lw = av("r_lw", 32, G, BF16)
        la = av("r_la", 32, G, BF16)
        lg = av("r_lg", 96, G, BF16)
        txw = av("r_txw", 32, NT, BF16)
        xab = av("r_xab", 32, NT, BF16)
        sgb = av("r_sgb", 96, NT, BF16)
        Zl = av("r_Zl", 128, nseq * E)
        Zr = av("r_Zr", 128, nseq * E)
        Zk = av("r_Zk", 128, nseq * E)
        Zv = av("r_Zv", 128, nseq * E)
        T = [av("r_T%d" % i, 128, NT) for i in range(5)]
        sqb = av("r_sqb", 128, NT, BF16)
        P.dma(lw[:, :], W["rwkv_w_lora"].t[l], r=[W["rwkv_w_lora"]], w=[lw], eng=POOL)
        P.dma(la[:, :], W["rwkv_a_lora"].t[l], r=[W["rwkv_a_lora"]], w=[la], eng=POOL)
        P.dma(lg[:, :], W["rwkv_g_lora"].t[l], r=[W["rwkv_g_lora"]], w=[lg], eng=POOL)
        mu0 = PC["mu"]

        def zl(Z, rows):
            return Z[0:rows, :].rearrange("p (s e) -> p s e", s=nseq)

        def zc(Z, rows=128):
            if kind == "p":
                return Z[0:rows, 1:E].rearrange("p (n c) -> p n c", c=C)
            return zl(Z, rows)[:, :, 1:E]

        def lerp(Z, rows, ci, c0, pb):
            z3 = zl(Z, rows)
            if kind == "p":
                if tok0 == 0:
                    P.op(DVE, lambda e: e.memset(z3[:, :, 0:1], 0.0), w=[Z])
                else:
                    P.op(DVE, lambda e: e.tensor_copy(z3[:, 0, 0:1], shcar[0:rows, ci:ci + 1]), r=[shcar], w=[Z])
            else:
                P.tr(px[0:rows, 0:16], xa[0:16, c0:c0 + rows], CF("ident", 16, 0, 16), r=[xa, cst], w=[px])
                P.op(ACT, lambda e: e.copy(z3[:, :, 0], px[0:rows, 0:16]), r=[px], w=[Z])
            P.op(ACT, lambda e: e.copy(z3[:, :, 1:E], pb[0:rows, 0:NT].rearrange("p (s t) -> p s t", s=nseq)), r=[pb], w=[Z])
            if kind == "p":
                if not last_blk:
                    P.op(ACT, lambda e: e.copy(shcar[0:rows, ci:ci + 1], z3[:, 0, L:E]), r=[Z], w=[shcar])
                else:
                    P.dma(shift_p.t[l, c0:c0 + rows].rearrange("(c o) -> c o", o=1), z3[:, 0, L:E], r=[Z], w=[shift_p], out=True,
                          allow_slow_non_contiguous=True)
            else:
                P.op(ACT, lambda e: e.copy(T[3][0:rows, 0:16], z3[:, :, L]), r=[Z], w=[T[3]])
                P.tr(px[0:16, 0:rows], T[3][0:rows, 0:16], CF("ident", rows, 0, rows), r=[T[3], cst], w=[px])
                P.op(ACT, lambda e: e.copy(xb[0:16, c0:c0 + rows], px[0:16, 0:rows]), r=[px], w=[xb])
            t4 = T[4][0:rows, :].rearrange("p (s t) -> p s t", s=nseq)
            mu = pcol[0:rows, mu0 + ci:mu0 + ci + 1]
            P.op(DVE, lambda e: e.tensor_tensor(t4, z3[:, :, 0:L], z3[:, :, 1:E], ALU.subtract), r=[Z], w=[T[4]])
            P.op(DVE, lambda e: e.scalar_tensor_tensor(z3[:, :, 1:E], t4, mu, z3[:, :, 1:E], ALU.mult, ALU.add),
                 r=[T[4], pcol, Z], w=[Z])

        if kind == "s":
            P.dma(xa[0:16, 0:RC], st_shift.t[l], r=[st_shift], w=[xa])
        sl_ = load_w(W["w_in"], l, 0, 16, 1536, 160)
        for (q0, rows, ci, func, dst) in ((0, 32, 12, AF.Tanh, txw), (32, 32, 13, AF.Copy, xab), (64, 96, 14, AF.Sigmoid, sgb)):
            pb = next_pbig()
            for kc in range(16):
                P.mm(pb[0:rows, 0:NT], sl_[:, kc, q0:q0 + rows], hT[:, kc, 0:NT], kc == 0, kc == 15, r=[sl_, hT], w=[pb])
            lerp(Zl, rows, ci, 1536 + q0, pb)
            zcv = zc(Zl, rows)
            P.op(ACT, lambda e, func=func, dst=dst, zcv=zcv, rows=rows: e.activation(c3(dst[0:rows, :]), zcv, func), r=[Zl], w=[dst])
        sr = load_w(W["w_in"], l, 0, 16, 0, 512)
        sk = load_w(W["w_in"], l, 0, 16, 512, 512)
        sv = load_w(W["w_in"], l, 0, 16, 1024, 512)
        bo_b = CB("blockones")
        for j in range(4):
            js = slice(j * 128, (j + 1) * 128)
            for (Z, s_, ci) in ((Zr, sr, j), (Zk, sk, 4 + j), (Zv, sv, 8 + j)):
                pb = next_pbig()
                for kc in range(16):
                    P.mm(pb[:, 0:NT], s_[:, kc, js], hT[:, kc, 0:NT], kc == 0, kc == 15, r=[s_, hT], w=[pb])
                lerp(Z, 128, ci, ci * 128, pb)
            rm, km, vm = zc(Zr), zc(Zk), zc(Zv)
            T0, T1, T2, T3, T4 = T
            col = lambda nm: pcol[:, PC[nm] + j:PC[nm] + j + 1]
            P.mm(px[:, 0:NT], lw[:, js], txw[:, :], True, True, r=[lw, txw], w=[px])
            P.op(ACT, lambda e, b=col("w0"): e.activation(T0[:, :], px[:, 0:NT], AF.Sigmoid, bias=b), r=[px, pcol], w=[T0])
            P.op(DVE, lambda e: e.tensor_scalar(T0[:, :], T0[:, :], NEG_E05, None, ALU.mult), r=[T0], w=[T0])
            P.op(DVE, lambda e: e.tensor_tensor_scan(T1[:, :], rst[:, 0:NT], T0[:, :], 0.0, ALU.mult, ALU.add), r=[cst, T0], w=[T1])
            P.op(ACT, lambda e, j=j: e.activation(gC[j][:, 0:nch], c3(T1[:, :])[:, :, C - 1], AF.Exp), r=[T1], w=[gC[j]])
            P.mm(px[:, 0:NT], la[:, js], xab[:, :], True, True, r=[la, xab], w=[px])
            P.op(ACT, lambda e, b=col("a0"): e.activation(T2[:, :], px[:, 0:NT], AF.Sigmoid, bias=b), r=[px, pcol], w=[T2])
            P.mm(px[:, 0:NT], lg[:, js], sgb[:, :], True, True, r=[lg, sgb], w=[px])
            P.op(ACT, lambda e, j=j: e.copy(gsb[j][:, :], px[:, 0:NT]), r=[px], w=[gsb[j]])
            P.op(DVE, lambda e, km=km, s_=col("k_k"): e.tensor_scalar(c3(T3[:, :]), km, s_, None, ALU.mult), r=[Zk, pcol], w=[T3])
            P.op(DVE, lambda e: e.tensor_tensor(sqb[:, :], T3[:, :], T3[:, :], ALU.mult), r=[T3], w=[sqb])
            P.mm(px[:, 0:NT], bo_b, sqb[:, :], True, True, r=[cstb, sqb], w=[px])
            P.op(ACT, lambda e: e.activation(T4[:, :], px[:, 0:NT], AF.Sqrt), r=[px], w=[T4])
            P.op(DVE, lambda e: e.tensor_scalar(T4[:, :], T4[:, :], 1e-12, None, ALU.max), r=[T4], w=[T4])
            P.op(DVE, lambda e: e.reciprocal(T4[:, :], T4[:, :]), r=[T4], w=[T4])
            P.op(DVE, lambda e: e.tensor_tensor(T3[:, :], T3[:, :], T4[:, :], ALU.mult), r=[T3, T4], w=[T3])
            P.op(DVE, lambda e, s1=col("k_a"), s2=col("omk"): e.tensor_scalar(T4[:, :], T2[:, :], s1, s2, ALU.mult, ALU.add),
                 r=[T2, pcol], w=[T4])
            P.op(DVE, lambda e, km=km: e.tensor_tensor(km, km, c3(T4[:, :]), ALU.mult), r=[Zk, T4], w=[Zk])
            P.op(DVE, lambda e: e.tensor_tensor(T4[:, :], T3[:, :], T2[:, :], ALU.mult), r=[T3, T2], w=[T4])
            P.op(DVE, lambda e, rm=rm, km=km: e.tensor_tensor(c3(T2[:, :]), rm, km, ALU.mult), r=[Zr, Zk], w=[T2])
            P.op(DVE, lambda e, s_=col("r_k"): e.tensor_scalar(sqb[:, :], T2[:, :], s_, None, ALU.mult), r=[T2, pcol], w=[sqb])
            P.mm(px[:, 0:NT], bo_b, sqb[:, :], True, True, r=[cstb, sqb], w=[px])
            P.op(DVE, lambda e, j=j, vm=vm: e.tensor_tensor(c3(bon[j][:, :]), c3(px[:, 0:NT]), vm, ALU.mult), r=[px, Zv], w=[bon[j]])
            P.op(ACT, lambda e, j=j, vm=vm: e.copy(c3(Vb[j][:, :]), vm), r=[Zv], w=[Vb[j]])
            AR4 = AR[j][:, :].rearrange("p (n two c) -> p n two c", two=2, c=C)
            BK4 = BK[j][:, :].rearrange("p (n two c) -> p n two c", two=2, c=C)
            P.op(ACT, lambda e: e.activation(T2[:, :], T1[:, :], AF.Exp), r=[T1], w=[T2])
            P.op(DVE, lambda e, rm=rm, o=AR4[:, :, 1, :]: e.tensor_tensor(o, rm, c3(T2[:, :]), ALU.mult), r=[Zr, T2], w=[AR[j]])
            P.op(ACT, lambda e: e.activation(T2[:, :], T1[:, :], AF.Exp, scale=-1.0), r=[T1], w=[T2])
            P.op(DVE, lambda e, o=BK4[:, :, 0, :]: e.tensor_tensor(o, c3(T4[:, :]), c3(T2[:, :]), ALU.mult), r=[T4, T2], w=[BK[j]])
            P.op(DVE, lambda e, km=km, o=BK4[:, :, 1, :]: e.tensor_tensor(o, km, c3(T2[:, :]), ALU.mult), r=[Zk, T2], w=[BK[j]])
            P.op(DVE, lambda e: e.tensor_tensor(T2[:, :], T1[:, :], T0[:, :], ALU.subtract), r=[T1, T0], w=[T2])
            P.op(ACT, lambda e: e.activation(T2[:, :], T2[:, :], AF.Exp), r=[T2], w=[T2])
            P.op(DVE, lambda e, o=AR4[:, :, 0, :]: e.scalar_tensor_tensor(o, c3(T3[:, :]), -1.0, c3(T2[:, :]), ALU.mult, ALU.mult),
                 r=[T3, T2], w=[AR[j]])
            cum3 = c3(T1[:, :])
            P.op(DVE, lambda e, cum3=cum3: e.tensor_tensor(c3(T2[:, :]), cum3[:, :, C - 1:C].to_broadcast([128, nch, C]), cum3, ALU.subtract),
                 r=[T1], w=[T2])
            P.op(ACT, lambda e: e.activation(T2[:, :], T2[:, :], AF.Exp), r=[T2], w=[T2])
            P.op(DVE, lambda e, j=j: e.tensor_tensor(Bh[j][:, :], T4[:, :], T2[:, :], ALU.mult), r=[T4, T2], w=[Bh[j]])
            P.op(DVE, lambda e, j=j, km=km: e.tensor_tensor(c3(Kh[j][:, :]), km, c3(T2[:, :]), ALU.mult), r=[Zk, T2], w=[Kh[j]])

        if kind == "s":
            P.dma(shift_s.t[l], xb[0:16, 0:RC], r=[xb], w=[shift_s], out=True)
        if _RW_LEVEL < 2:
            for j in range(4):
                P.op(DVE, lambda e, j=j: e.tensor_scalar(yT[:, j, 0:NT], hT[:, j, 0:NT], 0.0, None, ALU.mult), r=[hT], w=[yT])
            return
        arena_fence()
        aoff[0] = mark
        TMB = [av("r_TMB%d" % i, 64, 512, BF16) for i in range(2)]
        TMK = [av("r_TMK%d" % i, 64, 512, BF16) for i in range(2)]
        TMV = [av("r_TMV%d" % i, 64, 512, BF16) for i in range(2)]
        A1d = [av("r_A1s%d" % i, 64, 8 * C2, BF16) for i in range(2)]
        A2d = [av("r_A2s%d" % i, 64, 8 * C2, BF16) for i in range(2)]
        Tmd = [av("r_Tm%d" % i, 64, 8 * C, BF16) for i in range(2)]
        NTs = av("r_NTs", 64, 8 * C, BF16)
        Xb = [av("r_X%d" % i, 64, 8 * C, BF16) for i in range(2)]
        XTb = [av("r_XT%d" % i, 64, 8 * C, BF16) for i in range(2)]
        TTm = av("r_TTm", 64, 8 * C, BF16)
        XtS = av("r_XtS", 64, 512, BF16)
        UtS = av("r_UtS", 64, 512, BF16)
        ysb = av("r_ysb", 128, 512)
        ynb = av("r_ynb", 64, 512, BF16)
        gst = av("r_gst", 64, 32)
        Sld = ysb
        Sout = ysb
        fin = ysb
        su_m = CF("SU", C, 0, C).unsqueeze(1)
        iu_m = CF("IU", C, 0, C).unsqueeze(1)
        sl_m = CF("SL", C, 0, C).unsqueeze(1)
        id_m = CF("ident", C, 0, C).unsqueeze(1)
        idb = CB("ident")
        nlev = {64: 5, 4: 1}[C] if _RW_LEVEL >= 3 else 0
        hv = lambda ap, w_: ap.rearrange("p (h c) -> p h c", c=w_)
        gCo = av("r_gCo", 64, 64)
        ARo = [hT[0:64, 2 * j:2 * j + 2, :].rearrange("p a b -> p (a b)") for j in range(4)]
        BKo = [hT[0:64, 8 + 2 * j:10 + 2 * j, :].rearrange("p a b -> p (a b)") for j in range(4)]
        for j in range(4):
            P.dma(ARo[j][:, 0:2 * NT], AR[j][64:128, :], r=[AR[j]], w=[hT])
            P.dma(BKo[j][:, 0:2 * NT], BK[j][64:128, :], r=[BK[j]], w=[hT])
            P.dma(gCo[:, j * 16:j * 16 + nch], gC[j][64:128, 0:nch], r=[gC[j]], w=[gCo])

        def opA(j, e_, c0, c1):
            return (AR[j][0:64, c0:c1], AR[j]) if e_ == 0 else (ARo[j][:, c0:c1], hT)

        def opB(j, e_, c0, c1):
            return (BK[j][0:64, c0:c1], BK[j]) if e_ == 0 else (BKo[j][:, c0:c1], hT)

        def decay(j, e_, c):
            return (gC[j][0:64, c:c + 1], gC[j]) if e_ == 0 else (gCo[:, j * 16 + c:j * 16 + c + 1], gCo)

        def stageA(c):
            cs = slice(c * C, (c + 1) * C)
            tb, tk, tv = TMB[c % 2], TMK[c % 2], TMV[c % 2]
            A1s, A2s, Tm = A1d[c % 2], A2d[c % 2], Tmd[c % 2]
            for (srcs, dst, eng) in ((Bh, tb, DVE), (Kh, tk, ACT), (Vb, tv, DVE)):
                pt = next_ptr()
                for j in range(4):
                    P.tr(pt[0:C, j * 128:(j + 1) * 128], srcs[j][:, cs], idb, r=[srcs[j], cstb], w=[pt])
                if eng == DVE:
                    P.op(DVE, lambda e, pt=pt, dst=dst: e.tensor_copy(dst[0:C, :], pt[0:C, :]), r=[pt], w=[dst])
                else:
                    P.op(ACT, lambda e, pt=pt, dst=dst: e.copy(dst[0:C, :], pt[0:C, :]), r=[pt], w=[dst])
            yield
            pA1 = [pbig[0], pbig[1]]
            pA2 = [pbig[2], px]
            for h in range(8):
                j, e_ = h // 2, h % 2
                ar, arR = opA(j, e_, c * C2, (c + 1) * C2)
                bt, bkR = opB(j, e_, c * C2, c * C2 + C)
                hh = h % 4
                P.mm(pA1[h // 4][0:C, hh * C2:(hh + 1) * C2], bt, ar, True, True, r=[bkR, arR], w=[pA1[h // 4]])
            for half in range(2):
                src3 = hv(pA1[half][0:C, 0:4 * C2], C2)
                dst3 = hv(A1s[0:C, half * 4 * C2:(half + 1) * 4 * C2], C2)
                P.op(DVE, lambda e, src3=src3, dst3=dst3: e.tensor_tensor(dst3[:, :, 0:C], src3[:, :, 0:C],
                                                                         su_m.to_broadcast([C, 4, C]), ALU.mult),
                     r=[pA1[half], cst], w=[A1s])
                P.op(DVE, lambda e, src3=src3, dst3=dst3: e.tensor_tensor(dst3[:, :, C:C2], src3[:, :, C:C2],
                                                                         iu_m.to_broadcast([C, 4, C]), ALU.mult),
                     r=[pA1[half], cst], w=[A1s])
            yield
            for h in range(8):
                j, e_ = h // 2, h % 2
                ar, arR = opA(j, e_, c * C2, (c + 1) * C2)
                kt_, bkR = opB(j, e_, c * C2 + C, (c + 1) * C2)
                hh = h % 4
                P.mm(pA2[h // 4][0:C, hh * C2:(hh + 1) * C2], kt_, ar, True, True, r=[bkR, arR], w=[pA2[h // 4]])
            for half in range(2):
                src3 = hv(pA2[half][0:C, 0:4 * C2], C2)
                dst3 = hv(A2s[0:C, half * 4 * C2:(half + 1) * 4 * C2], C2)
                P.op(DVE, lambda e, src3=src3, dst3=dst3: e.tensor_tensor(dst3[:, :, 0:C], src3[:, :, 0:C],
                                                                         su_m.to_broadcast([C, 4, C]), ALU.mult),
                     r=[pA2[half], cst], w=[A2s])
                P.op(DVE, lambda e, src3=src3, dst3=dst3: e.tensor_tensor(dst3[:, :, C:C2], src3[:, :, C:C2],
                                                                         iu_m.to_broadcast([C, 4, C]), ALU.mult),
                     r=[pA2[half], cst], w=[A2s])
            yield
            pn = pbig[0]
            for h in range(8):
                j, e_ = h // 2, h % 2
                at_, arR = opA(j, e_, c * C2, c * C2 + C)
                bt, bkR = opB(j, e_, c * C2, c * C2 + C)
                P.mm(pn[0:C, h * C:(h + 1) * C], at_, bt, True, True, r=[arR, bkR], w=[pn])
            P.op(DVE, lambda e: e.tensor_tensor(hv(NTs[0:C, :], C), hv(pn[0:C, 0:8 * C], C), sl_m.to_broadcast([C, 8, C]), ALU.mult),
                 r=[pn, cst], w=[NTs])
            A1v = hv(A1s[0:C, :], C2)
            P.op(DVE, lambda e: e.tensor_tensor(hv(Tm[0:C, :], C), A1v[:, :, 0:C], id_m.to_broadcast([C, 8, C]), ALU.add),
                 r=[A1s, cst], w=[Tm])
            P.op(DVE, lambda e: e.tensor_tensor(hv(TTm[0:C, :], C), hv(NTs[0:C, :], C), id_m.to_broadcast([C, 8, C]), ALU.add),
                 r=[NTs, cst], w=[TTm])
            yield
            Xc = (A1s, lambda h: A1s[0:C, h * C2:h * C2 + C])
            XTc = (NTs, lambda h: NTs[0:C, h * C:(h + 1) * C])
            for lev in range(nlev):
                Xn, XTn = Xb[lev % 2], XTb[lev % 2]
                lastl = (lev == nlev - 1)
                p1, p2, p3, p4 = pbig[1], pbig[2], px, pbig[0]
                for h in range(8):
                    P.mm(p1[0:C, h * C:(h + 1) * C], XTc[1](h), Xc[1](h), True, True, r=[XTc[0], Xc[0]], w=[p1])
                P.op(ACT, lambda e, Xn=Xn, p1=p1: e.copy(Xn[0:C, :], p1[0:C, 0:8 * C]), r=[p1], w=[Xn])
                if not lastl:
                    for h in range(8):
                        P.mm(p2[0:C, h * C:(h + 1) * C], Xc[1](h), XTc[1](h), True, True, r=[XTc[0], Xc[0]], w=[p2])
                    P.op(DVE, lambda e, XTn=XTn, p2=p2: e.tensor_copy(XTn[0:C, :], p2[0:C, 0:8 * C]), r=[p2], w=[XTn])
                yield
                for h in range(8):
                    P.mm(p3[0:C, h * C:(h + 1) * C], TTm[0:C, h * C:(h + 1) * C], Xn[0:C, h * C:(h + 1) * C], True, True,
                         r=[TTm, Xn], w=[p3])
                if not lastl:
                    for h in range(8):
                        P.mm(p4[0:C, h * C:(h + 1) * C], Xn[0:C, h * C:(h + 1) * C], TTm[0:C, h * C:(h + 1) * C], True, True,
                             r=[TTm, Xn], w=[p4])
                P.op(DVE, lambda e, p3=p3, Tm=Tm: e.tensor_tensor(Tm[0:C, :], Tm[0:C, :], p3[0:C, 0:8 * C], ALU.add), r=[Tm, p3], w=[Tm])
                if not lastl:
                    P.op(DVE, lambda e, p4=p4: e.tensor_tensor(TTm[0:C, :], TTm[0:C, :], p4[0:C, 0:8 * C], ALU.add), r=[TTm, p4], w=[TTm])
                    Xc = (Xn, lambda h, Xn=Xn: Xn[0:C, h * C:(h + 1) * C])
                    XTc = (XTn, lambda h, XTn=XTn: XTn[0:C, h * C:(h + 1) * C])
                yield

        def stageB(c):
            cs = slice(c * C, (c + 1) * C)
            tb, tk, tv = TMB[c % 2], TMK[c % 2], TMV[c % 2]
            A1s, A2s, Tm = A1d[c % 2], A2d[c % 2], Tmd[c % 2]
            if kind == "s":
                P.dma(Sld[0:64, :].rearrange("p (h k) -> p h k", h=8), st_wkv.t[l, c].rearrange("h v k -> v h k"), r=[st_wkv], w=[Sld])
                for h in range(8):
                    P.tr(pg[0:64, h * 64:(h + 1) * 64], Sld[0:64, h * 64:(h + 1) * 64], CF("ident", 64, 0, 64), r=[Sld, cst], w=[pg])
                P.op(DVE, lambda e: e.tensor_copy(rS[:, :, :], hv(pg[0:64, :], 64)), r=[pg], w=[rS])
                P.op(ACT, lambda e: e.copy(rSb[:, :, :], rS[:, :, :]), r=[rS], w=[rSb])
                yield
            elif tok0 == 0 and c == 0:
                P.op(DVE, lambda e: e.memset(rS[:, :, :], 0.0), w=[rS])
                P.op(ACT, lambda e: e.copy(rSb[:, :, :], rS[:, :, :]), r=[rS], w=[rSb])
            for h in range(8):
                j, e_ = h // 2, h % 2
                at_, arR = opA(j, e_, c * C2, c * C2 + C)
                P.mm(pg[0:C, h * 64:(h + 1) * 64], at_, rSb[:, h, :], True, False, r=[arR, rSb], w=[pg])
                P.mm(pg[0:C, h * 64:(h + 1) * 64], A2s[0:C, h * C2:h * C2 + C], tv[0:C, h * 64:(h + 1) * 64], False, True,
                     r=[A2s, tv], w=[pg])
            P.op(ACT, lambda e: e.copy(XtS[0:C, :], pg[0:C, :]), r=[pg], w=[XtS])
            yield
            for h in range(8):
                P.mm(pu[0:C, h * 64:(h + 1) * 64], Tm[0:C, h * C:(h + 1) * C], XtS[0:C, h * 64:(h + 1) * 64], True, True,
                     r=[Tm, XtS], w=[pu])
            P.op(DVE, lambda e: e.tensor_copy(UtS[0:C, :], pu[0:C, :]), r=[pu], w=[UtS])
            yield
            for h in range(8):
                j, e_ = h // 2, h % 2
                rt_, arR = opA(j, e_, c * C2 + C, (c + 1) * C2)
                o_ = pg[0:C, h * 64:(h + 1) * 64]
                P.mm(o_, rt_, rSb[:, h, :], True, False, r=[arR, rSb], w=[pg])
                P.mm(o_, A1s[0:C, h * C2 + C:(h + 1) * C2], UtS[0:C, h * 64:(h + 1) * 64], False, False, r=[A1s, UtS], w=[pg])
                P.mm(o_, A2s[0:C, h * C2 + C:(h + 1) * C2], tv[0:C, h * 64:(h + 1) * 64], False, True, r=[A2s, tv], w=[pg])
            P.op(ACT, lambda e: e.copy(ysb[0:C, :], pg[0:C, :]), r=[pg], w=[ysb])
            yield
            pS = pu
            for h in range(8):
                hs = slice(h * 64, (h + 1) * 64)
                P.mm(pS[0:64, hs], tb[0:C, hs], UtS[0:C, hs], True, False, r=[tb, UtS], w=[pS])
                P.mm(pS[0:64, hs], tk[0:C, hs], tv[0:C, hs], False, True, r=[tk, tv], w=[pS])
            for h in range(8):
                dc, dcR = decay(h // 2, h % 2, c)
                P.op(DVE, lambda e, h=h, pS=pS, dc=dc: e.scalar_tensor_tensor(
                    rS[:, h, :], rS[:, h, :], dc, pS[0:64, h * 64:(h + 1) * 64], ALU.mult, ALU.add), r=[rS, dcR, pS], w=[rS])
            P.op(ACT, lambda e: e.copy(rSb[:, :, :], rS[:, :, :]), r=[rS], w=[rSb])
            yield
            y3 = hv(ysb[0:C, :], 64)
            P.op(DVE, lambda e: e.memset(gst[0:C, 8:16], 0.0), w=[gst])
            P.op(DVE, lambda e, y3=y3: e.tensor_reduce(gst[0:C, 0:8], y3, AX.X, ALU.add), r=[ysb], w=[gst])
            for h in range(8):
                P.op(ACT, lambda e, h=h: e.activation(ynb[0:C, h * 64:(h + 1) * 64], ysb[0:C, h * 64:(h + 1) * 64], AF.Square,
                                                     accum_out=gst[0:C, 8 + h:9 + h]), r=[ysb, gst], w=[ynb, gst])
            P.op(DVE, lambda e: e.tensor_scalar(gst[0:C, 16:32], gst[0:C, 0:16], 1.0 / 64, None, ALU.mult), r=[gst], w=[gst])
            P.op(DVE, lambda e: e.tensor_tensor(gst[0:C, 0:8], gst[0:C, 16:24], gst[0:C, 16:24], ALU.mult), r=[gst], w=[gst])
            P.op(DVE, lambda e: e.tensor_tensor(gst[0:C, 8:16], gst[0:C, 24:32], gst[0:C, 0:8], ALU.subtract), r=[gst], w=[gst])
            P.op(DVE, lambda e: e.tensor_scalar(gst[0:C, 8:16], gst[0:C, 8:16], 64e-5, None, ALU.add), r=[gst], w=[gst])
            P.op(ACT, lambda e: e.activation(gst[0:C, 0:8], gst[0:C, 8:16], AF.Sqrt), r=[gst], w=[gst])
            P.op(DVE, lambda e: e.reciprocal(gst[0:C, 8:16], gst[0:C, 0:8]), r=[gst], w=[gst])
            P.op(DVE, lambda e, y3=y3: e.tensor_tensor(y3, y3, gst[0:C, 16:24].unsqueeze(2).to_broadcast([C, 8, 64]), ALU.subtract),
                 r=[ysb, gst], w=[ysb])
            P.op(DVE, lambda e, y3=y3: e.tensor_tensor(hv(ynb[0:C, :], 64), y3, gst[0:C, 8:16].unsqueeze(2).to_broadcast([C, 8, 64]), ALU.mult),
                 r=[ysb, gst], w=[ynb])
            pt = next_ptr()
            for j in range(4):
                P.tr(pt[:, j * C:(j + 1) * C], ynb[0:C, j * 128:(j + 1) * 128], CB("ident", C, 0, C), r=[ynb, cstb], w=[pt])
            P.op(ACT, lambda e, pt=pt, cs=cs: e.copy(YN[:, :].rearrange("p (j t) -> p j t", j=4)[:, :, cs], hv(pt[:, 0:4 * C], C)),
                 r=[pt], w=[YN])
            yield
            if kind == "s" or (last_blk and c == nch - 1):
                for h in range(8):
                    P.tr(pg[0:64, h * 64:(h + 1) * 64], rS[:, h, :], CF("ident", 64, 0, 64), r=[rS, cst], w=[pg])
                P.op(ACT, lambda e: e.copy(Sout[0:64, :], pg[0:64, :]), r=[pg], w=[Sout])
                dst_ = (wkv_s.t[l, c] if kind == "s" else wkv_p.t[l]).rearrange("h v k -> v h k")
                P.dma(dst_, Sout[0:64, :].rearrange("p (h k) -> p h k", h=8), r=[Sout], w=[wkv_s if kind == "s" else wkv_p], out=True)

        def drive(gens):
            gens = [g for g in gens if g is not None]
            while gens:
                for g in list(gens):
                    try:
                        next(g)
                    except StopIteration:
                        gens.remove(g)

        drive([stageA(0)])
        for c in range(nch):
            drive([stageA(c + 1) if c + 1 < nch else None, stageB(c)])
        for j in range(4):
            col = lambda nm: pcol[:, PC[nm] + j:PC[nm] + j + 1]
            P.op(DVE, lambda e, j=j, s1=col("gn_w"), s2=col("gn_b"): e.tensor_scalar(fin[:, 0:NT], YN[:, j * NT:(j + 1) * NT], s1, s2,
                                                                                  ALU.mult, ALU.add), r=[YN, pcol], w=[fin])
            P.op(DVE, lambda e, j=j: e.tensor_tensor(fin[:, 0:NT], fin[:, 0:NT], bon[j][:, :], ALU.add), r=[fin, bon[j]], w=[fin])
            P.op(DVE, lambda e, j=j: e.tensor_tensor(yT[:, j, 0:NT], fin[:, 0:NT], gsb[j][:, :], ALU.mult), r=[fin, gsb[j]], w=[yT])

    blocks = [("p", i * 512, 512) for i in range(4)] + [("s", 0, 64)]
    if mini:
        blocks = mini
    for l in range(1 if mini else DEPTH):
        arena_fence()
        arena_reset()
        layer_params(l)
        for (kind, tok0, NT) in blocks:
            tiles = [(i * 128, 128) for i in range(NT // 128)] if kind == "p" else [(0, 64)]
            if l == 0:
                xsrc = xp if kind == "p" else xs
            else:
                xsrc = xmid_p if kind == "p" else xmid_s
            if l == DEPTH - 1:
                xdst = y_p if kind == "p" else y_s
            else:
                xdst = xmid_p if kind == "p" else xmid_s
            pl = pp if kind == "p" else psm
            bcast_load(gbc, W["ln_mix_pre"], l)
            for (o, m) in tiles:
                P.dma(xa[0:m, :], xsrc[tok0 + o:tok0 + o + m, :], r=[xsrc], w=[xa])
                rs = rms_rstd(xa, xa[0:m, :], m, D, 0, xb, xb[0:m, :], 1.0 / D, epsn)
                P.op(DVE, lambda e, m=m, rs=rs: e.scalar_tensor_tensor(xnb[0:m, :], xa[0:m, :], rs, gbc[0:m, :],
                                                                    ALU.mult, ALU.mult), r=[xa, stat, gbc], w=[xnb])
                to_featmajor(xnb, lambda kc, m=m: xnb[0:m, kc * 128:(kc + 1) * 128], m, o, hT)
            if stage < 4:
                for kc in range(16):
                    P.op(DVE, lambda e, kc=kc: e.tensor_scalar(yT[:, kc, :], hT[:, kc, :], 0.0, None, ALU.mult), r=[hT], w=[yT])
            if mini:
                arena_fence()
                arena_reset()
                mixer_rwkv(l, kind, tok0, NT)
                arena_fence()
                for j in range(4):
                    P.op(ACT, lambda e, j=j, NT=NT: e.copy(xa[:, j * 512:j * 512 + NT], yT[:, j, 0:NT]), r=[yT], w=[xa])
                P.dma(dbg_d[0, :, :], xa[:, :], r=[xa], w=[dbg_d], out=True)
                continue
            if stage >= 1:
                arena_fence()
                arena_reset()
                mixer_pool(l, kind, tok0, NT)
            if stage >= 2:
                arena_fence()
                arena_reset()
                mixer_sgu(l, kind, tok0, NT, tiles)
            if stage >= 3:
                arena_fence()
                arena_reset()
                mixer_hgrn(l, kind, tok0, NT)
            if stage >= 4:
                arena_fence()
                arena_reset()
                mixer_rwkv(l, kind, tok0, NT)
            arena_fence()
            for db in range(4):
                s = load_w(W["w_out"], l, 0, 16, db * 512, 512)
                for ti, (o, m) in enumerate(tiles):
                    pb = next_pbig()
                    for kc in range(16):
                        P.mm(pb[0:m, :], yT[:, kc, o:o + m], s[:, kc, :], kc == 0, kc == 15, r=[yT, s], w=[pb])
                    P.op(ACT, lambda e, pb=pb, ti=ti, db=db, m=m: e.copy(big[0:m, ti, db * 512:(db + 1) * 512], pb[0:m, :]),
                         r=[pb], w=[big])
            bcast_load(gbc, W["ln_mix_post"], l)
            for ti, (o, m) in enumerate(tiles):
                dbg_on = False
                if dbg_on:
                    P.dma(dbg_d[0, :, :], big[:, 0, :], r=[big], w=[dbg_d])
                    for kc in range(4):
                        P.op(ACT, lambda e, kc=kc: e.copy(xa[:, kc * 512:(kc + 1) * 512], yT[:, kc * 4, :]), r=[yT], w=[xa])
                    P.dma(dbg_d[5, :, :], xa[:, :], r=[xa], w=[dbg_d])
                rs = rms_rstd(big, big[0:m, ti, :], m, D, 0, xb, xb[0:m, :], 1.0 / D, epsn)
                P.dma(xa[0:m, :], xsrc[tok0 + o:tok0 + o + m, :], r=[xsrc], w=[xa])
                P.op(DVE, lambda e, m=m, ti=ti, rs=rs: e.scalar_tensor_tensor(xb[0:m, :], big[0:m, ti, :], rs, gbc[0:m, :],
                                                                          ALU.mult, ALU.mult), r=[big, stat, gbc], w=[xb])
                if dbg_on:
                    P.dma(dbg_d[1, :, :], xb[:, :], r=[xb], w=[dbg_d])
                    P.dma(dbg_d[2, :, :], xa[:, :], r=[xa], w=[dbg_d])
                    P.dma(dbg_d[4, :, 0:8], stat[:, :], r=[stat], w=[dbg_d])
                P.op(DVE, lambda e, m=m, ti=ti: e.tensor_tensor(big[0:m, ti, :], xa[0:m, :], xb[0:m, :], ALU.add),
                     r=[xa, xb], w=[big])
                if dbg_on:
                    P.dma(dbg_d[3, :, :], big[:, 0, :], r=[big], w=[dbg_d])
                P.dma(xf1_d[o:o + m, :], big[0:m, ti, :], r=[big], w=[xf1_d])
                rs = rms_rstd(big, big[0:m, ti, :], m, D, 4, xb, xb[0:m, :], 1.0 / D, epsn)
                P.op(DVE, lambda e, m=m, ti=ti, rs=rs: e.tensor_scalar(xnb[0:m, :], big[0:m, ti, :], rs, None, ALU.mult),
                     r=[big, stat], w=[xnb])
                to_featmajor(xnb, lambda kc, m=m: xnb[0:m, kc * 128:(kc + 1) * 128], m, o, hT, gain=gcol)
            dbg4 = _DEBUG and l == 0 and kind == "p" and tok0 == 0
            if dbg4:
                P.op(ACT, lambda e: e.copy(xa[:, 0:512], hT[:, 0, :]), r=[hT], w=[xa])
                P.op(ACT, lambda e: e.copy(xa[:, 512:1024], hT[:, 5, :]), r=[hT], w=[xa])
                P.dma(dbg_d[0, :, 0:1024], xa[:, 0:1024], r=[xa], w=[dbg_d])
            arena_fence()
            ppairs = [(pg, pu), (pbig[0], pbig[1])]
            sbufs = [silu_t, silu_b]
            for fs in range(DFF // 256):
                sgu_ = wslot[slot_rr[0]]
                slot_rr[0] = (slot_rr[0] + 1) % NSLOT
                wsrc = W["ffn_w_gu"].t[l].rearrange("(kc p) c -> p kc c", p=128)
                P.dma(sgu_[:, :, 0:256], wsrc[:, :, fs * 256:(fs + 1) * 256], r=[W["ffn_w_gu"]], w=[sgu_], eng=POOL)
                P.dma(sgu_[:, :, 256:512], wsrc[:, :, DFF + fs * 256:DFF + (fs + 1) * 256], r=[W["ffn_w_gu"]], w=[sgu_], eng=POOL)
                for fc in range(2):
                    fidx = fs * 2 + fc
                    pgx, pux = ppairs[fidx % 2]
                    sbx = sbufs[fidx % 2]
                    for kc in range(16):
                        P.mm(pgx[:, 0:NT], sgu_[:, kc, fc * 128:(fc + 1) * 128], hT[:, kc, 0:NT], kc == 0, kc == 15,
                             r=[sgu_, hT], w=[pgx])
                    for kc in range(16):
                        P.mm(pux[:, 0:NT], sgu_[:, kc, 256 + fc * 128:256 + (fc + 1) * 128], hT[:, kc, 0:NT], kc == 0, kc == 15,
                             r=[sgu_, hT], w=[pux])
                    P.op(ACT, lambda e, NT=NT, sbx=sbx, pgx=pgx: e.activation(sbx[:, 0:NT], pgx[:, 0:NT], AF.Silu), r=[pgx], w=[sbx])
                    P.op(DVE, lambda e, fidx=fidx, NT=NT, sbx=sbx, pux=pux: e.tensor_tensor(actT[:, fidx, 0:NT], sbx[:, 0:NT], pux[:, 0:NT], ALU.mult),
                         r=[sbx, pux], w=[actT])
            if dbg4:
                P.op(ACT, lambda e: e.copy(xa[:, 0:512], actT[:, 0, :]), r=[actT], w=[xa])
                P.op(ACT, lambda e: e.copy(xa[:, 512:1024], actT[:, 43, :]), r=[actT], w=[xa])
                P.op(ACT, lambda e: e.copy(xa[:, 1024:1536], silu_t[:, :]), r=[silu_t], w=[xa])
                P.op(ACT, lambda e: e.copy(xa[:, 1536:2048], pu[:, :]), r=[pu], w=[xa])
                P.dma(dbg_d[1, :, :], xa[:, :], r=[xa], w=[dbg_d])
            accs = [pbig[0], pbig[1], pbig[2], pg]
            for db in range(4):
                for (k0, nk) in ((0, 16), (16, 16), (32, 12)):
                    s_ = load_w(W["ffn_w_down"], l, k0, nk, db * 512, 512)
                    for ti, (o, m) in enumerate(tiles):
                        pb = accs[ti]
                        for q in range(nk):
                            fc = k0 + q
                            P.mm(pb[0:m, :], actT[:, fc, o:o + m], s_[:, q, :], fc == 0, fc == 43, r=[actT, s_], w=[pb])
                for ti, (o, m) in enumerate(tiles):
                    pb = accs[ti]
                    if ti % 2 == 0:
                        P.op(ACT, lambda e, pb=pb, ti=ti, db=db, m=m: e.copy(big[0:m, ti, db * 512:(db + 1) * 512], pb[0:m, :]),
                             r=[pb], w=[big])
                    else:
                        P.op(DVE, lambda e, pb=pb, ti=ti, db=db, m=m: e.tensor_copy(big[0:m, ti, db * 512:(db + 1) * 512], pb[0:m, :]),
                             r=[pb], w=[big])
            arena_fence()
            bcast_load(gbc, W["ln_ffn_post"], l)
            for ti, (o, m) in enumerate(tiles):
                rs = rms_rstd(big, big[0:m, ti, :], m, D, 0, xb, xb[0:m, :], 1.0 / D, epsn)
                P.dma(xa[0:m, :], xf1_d[o:o + m, :], r=[xf1_d], w=[xa])
                dbg6 = False
                if dbg6:
                    P.dma(dbg_d[0, :, :], big[:, ti, :], r=[big], w=[dbg_d])
                    P.dma(dbg_d[2, :, :], xa[:, :], r=[xa], w=[dbg_d])
                P.op(DVE, lambda e, m=m, ti=ti, rs=rs: e.scalar_tensor_tensor(xb[0:m, :], big[0:m, ti, :], rs, gbc[0:m, :],
                                                                          ALU.mult, ALU.mult), r=[big, stat, gbc], w=[xb])
                if dbg6:
                    P.dma(dbg_d[1, :, :], xb[:, :], r=[xb], w=[dbg_d])
                    P.dma(dbg_d[4, :, 0:8], stat[:, :], r=[stat], w=[dbg_d])
                    P.dma(dbg_d[5, :, :], gbc[:, :], r=[gbc], w=[dbg_d])
                P.op(DVE, lambda e, m=m, ti=ti: e.tensor_tensor(big[0:m, ti, :], xa[0:m, :], xb[0:m, :], ALU.add),
                     r=[xa, xb], w=[big])
                if dbg6:
                    P.dma(dbg_d[3, :, :], big[:, ti, :], r=[big], w=[dbg_d])
                P.op(ACT, lambda e, m=m, ti=ti: e.copy(xnb[0:m, :], big[0:m, ti, :]), r=[big], w=[xnb])
                to_featmajor(xnb, lambda kc, m=m: xnb[0:m, kc * 128:(kc + 1) * 128], m, o, yT)
                P.dma(xa[0:m, 0:PLE], pl[l, tok0 + o:tok0 + o + m, :], r=[pl], w=[xa])
                P.op(ACT, lambda e, m=m: e.copy(xnb[0:m, 0:PLE], xa[0:m, 0:PLE]), r=[xa], w=[xnb])
                to_featmajor(xnb, lambda kc, m=m: xnb[0:m, kc * 128:(kc + 1) * 128], m, o, pT, nkc=2)
            for db in range(4):
                sgt = load_w(W["ple_gate"], l, 0, 16, db * 512, 512)
                spj = load_w(W["ple_proj"], l, 0, 2, db * 512, 512)
                for ti, (o, m) in enumerate(tiles):
                    pb = next_pbig()
                    for kc in range(16):
                        P.mm(pb[0:m, :], yT[:, kc, o:o + m], sgt[:, kc, :], kc == 0, kc == 15, r=[yT, sgt], w=[pb])
                    pb2 = next_pbig()
                    for kc in range(2):
                        P.mm(pb2[0:m, :], pT[:, kc, o:o + m], spj[:, kc, :], kc == 0, kc == 1, r=[pT, spj], w=[pb2])
                    P.op(ACT, lambda e, pb=pb, m=m: e.activation(silu_t[0:m, :], pb[0:m, :], AF.Sigmoid), r=[pb], w=[silu_t])
                    P.op(DVE, lambda e, pb2=pb2, m=m: e.tensor_tensor(silu_t[0:m, :], silu_t[0:m, :], pb2[0:m, :], ALU.mult),
                         r=[silu_t, pb2], w=[silu_t])
                    P.op(DVE, lambda e, m=m, ti=ti, db=db: e.tensor_tensor(big[0:m, ti, db * 512:(db + 1) * 512],
                                                                         big[0:m, ti, db * 512:(db + 1) * 512],
                                                                         silu_t[0:m, :], ALU.add), r=[big, silu_t], w=[big])
            for ti, (o, m) in enumerate(tiles):
                P.dma(xdst[tok0 + o:tok0 + o + m, :], big[0:m, ti, :], r=[big], w=[xdst], out=(l == DEPTH - 1))
    stats = P.emit()
    return nc, stats


def kernel(**inp):
    if "nc" not in _BUILT:
        _BUILT["nc"], _BUILT["stats"] = build()
    nc = _BUILT["nc"]
    f = lambda a: np.ascontiguousarray(np.asarray(a, dtype=np.float32))
    wkeys = ["ln_mix_pre", "ln_mix_post", "ln_ffn_pre", "ln_ffn_post", "w_in", "rwkv_mu", "rwkv_w_lora", "rwkv_w0",
             "rwkv_a_lora", "rwkv_a0", "rwkv_g_lora", "rwkv_k_k", "rwkv_k_a", "rwkv_r_k", "rwkv_gn_w", "rwkv_gn_b",
             "sgu_ln_w", "sgu_ln_b", "sgu_w", "sgu_b", "sgu_norm", "hgrn_lb_logits", "hgrn_norm", "pool_w",
             "pool_scale", "w_out", "ffn_w_gu", "ffn_w_down", "ple_gate", "ple_proj"]
    shared = {k: f(inp[k]) for k in wkeys}
    shared["rwkv_r_k"] = shared["rwkv_r_k"].reshape(DEPTH, G)
    shared["consts"] = CONSTS
    in_maps = []
    for c in range(8):
        b = c % 4
        sl = slice(c * NSEQ_S, (c + 1) * NSEQ_S)
        m = dict(shared)
        m["xp"] = f(inp["x_prompt"][b])
        m["xs"] = f(inp["x_sample"][sl]).reshape(64, D)
        m["pp"] = f(inp["p_prompt"][:, b])
        m["psm"] = f(inp["p_sample"][:, sl]).reshape(DEPTH, 64, PLE)
        m["st_wkv"] = f(inp["state_rwkv_wkv"][:, sl])
        m["st_shift"] = f(inp["state_rwkv_shift"][:, sl])
        m["st_hgrn"] = f(inp["state_hgrn"][:, sl])
        m["st_pool"] = f(inp["state_pool"][:, sl])
        in_maps.append(m)
    res = run_bass_kernel_spmd(nc, in_maps, core_ids=list(range(8)))
    R = res.results
    _BUILT["R"] = R
    y_prompt = np.stack([R[b]["y_p"] for b in range(4)], 0)
    y_sample = np.concatenate([R[c]["y_s"].reshape(NSEQ_S, LS, D) for c in range(8)], 0)
    pst = lambda k: np.stack([R[b][k] for b in range(4)], 1)
    sst = lambda k: np.concatenate([R[c][k] for c in range(8)], 1)
    out = (y_prompt, y_sample, pst("wkv_p"), pst("shift_p"), pst("hgrn_p"), pst("pool_p"),
           sst("wkv_s"), sst("shift_s"), sst("hgrn_s"), sst("pool_s"), sst("sguv_s"))
    return tuple(np.ascontiguousarray(o, dtype=np.float32) for o in out)
```
